# Optimizing a Trainium2 kernel written in Bass

```python
import math
import jax
import jax.numpy as jnp
from jax import lax
import numpy as np

D_MODEL = 1024
BATCH = 2
SEQ = 8192
DEPTH = 2

GRID_W = 64
CTX_LEN = 256
EPS = 1e-6
LB_FLOOR = 1e-30
F32 = jnp.float32
CHUNK = 64
Q_BLOCK = 128

GDN_HEADS = 4
GDN_DK = 128
GDN_DV = 128
GDN_KEY = GDN_HEADS * GDN_DK
GDN_VAL = GDN_HEADS * GDN_DV
CONV_K = 3

MLA_HEADS = 4
MLA_Q_RANK = 384
MLA_KV_RANK = 256
MLA_NOPE = 128
MLA_ROPE = 64
MLA_V = 128
MLA_SCALE = (MLA_NOPE + MLA_ROPE) ** -0.5
ROPE_BASE = 10000.0
ROPE_NF = MLA_ROPE // 4

HG_HEADS = 4
HG_DK = 128
HG_DV = 128
HG_KEY = HG_HEADS * HG_DK
HG_VAL = HG_HEADS * HG_DV

N_BRANCH = 3
BRANCH_W = GDN_VAL
D_FF = 2816
N_MOD = 9

IN_SIZES = (GDN_KEY, GDN_KEY, GDN_VAL, GDN_VAL, 2 * GDN_HEADS, 2 * GDN_HEADS,
            MLA_Q_RANK, MLA_KV_RANK, MLA_ROPE,
            HG_KEY, 2 * HG_KEY, HG_VAL, HG_VAL,
            N_BRANCH * D_MODEL)
IN_COLS = sum(IN_SIZES)

kernel_name = 'hybrid_gdn_mla_hgrn2_diffusion_block'


def rms_norm(x, w):
    xf = x.astype(F32)
    y = xf * lax.rsqrt(jnp.mean(xf * xf, axis=-1, keepdims=True) + EPS)
    return (y * w.astype(F32)).astype(x.dtype)


def l2_normalize(x):
    return x * lax.rsqrt(jnp.sum(x * x, axis=-1, keepdims=True) + EPS)


def modulate(h, shift, scale):
    return h * (1 + scale) + shift


def split_heads(t, n_heads):
    b, s, _ = t.shape
    return t.reshape(b, s, n_heads, -1).transpose(0, 2, 1, 3)


def merge_heads(t):
    b, h, s, d = t.shape
    return t.transpose(0, 2, 1, 3).reshape(b, s, h * d)


def swiglu(h, w_i, w_o):
    gate, up = jnp.split(h @ w_i, 2, axis=-1)
    return (jax.nn.silu(gate) * up) @ w_o


def ffn_sublayer(x, mod, pre_w, post_w, w_i, w_o):
    shift, scale, gate = mod
    h = modulate(rms_norm(x, pre_w), shift, scale)
    return x + 0.5 * gate * rms_norm(swiglu(h, w_i, w_o), post_w)


def short_conv(x, w):
    pad = CONV_K // 2
    t = x.shape[1]
    xp = jnp.pad(x, ((0, 0), (pad, pad), (0, 0)))
    return jax.nn.silu(sum(xp[:, j:j + t] * w[j] for j in range(CONV_K)))


def to_chunks(t):
    b, h, n = t.shape[:3]
    t = t.reshape(b, h, n // CHUNK, CHUNK, *t.shape[3:])
    return jnp.moveaxis(t, 2, 0)


def from_chunks(t):
    t = jnp.moveaxis(t, 0, 2)
    return t.reshape(t.shape[0], t.shape[1], -1, *t.shape[4:])


def masked_decay(diff, mask):
    return jnp.where(mask, jnp.exp(jnp.where(mask, diff, 0.0)), 0.0)


def gdn_chunk_scan(s0, q, k, v, g, beta):
    incl = jnp.tril(jnp.ones((CHUNK, CHUNK), bool))
    strict = jnp.tril(jnp.ones((CHUNK, CHUNK), bool), -1)
    eye = jnp.eye(CHUNK, dtype=F32)

    def step(s, inp):
        qc, kc, vc, gc, bc = inp
        gcum = jnp.cumsum(gc, axis=-1)
        dmask = masked_decay(gcum[..., :, None] - gcum[..., None, :], incl)
        kb = kc * bc[..., None]
        a = jnp.where(strict, jnp.einsum('bhid,bhjd->bhij', kb, kc) * dmask, 0.0)
        m = eye + a
        u = lax.linalg.triangular_solve(m, vc * bc[..., None], left_side=True, lower=True, unit_diagonal=True)
        w = lax.linalg.triangular_solve(m, kb * jnp.exp(gcum)[..., None], left_side=True, lower=True, unit_diagonal=True)
        v_new = u - jnp.einsum('bhck,bhkv->bhcv', w, s)
        attn = jnp.einsum('bhid,bhjd->bhij', qc, kc) * dmask
        o = (jnp.einsum('bhck,bhkv->bhcv', qc * jnp.exp(gcum)[..., None], s)
             + jnp.einsum('bhij,bhjv->bhiv', attn, v_new))
        g_last = gcum[..., -1:]
        s = (s * jnp.exp(g_last)[..., None]
             + jnp.einsum('bhck,bhcv->bhkv', kc * jnp.exp(g_last - gcum)[..., None], v_new))
        return s, o

    s, o = lax.scan(step, s0, tuple(to_chunks(t) for t in (q, k, v, g, beta)))
    return from_chunks(o), s


def gla_chunk_scan(s0, q, k, v, g):
    incl = jnp.tril(jnp.ones((CHUNK, CHUNK), bool))[:, :, None]

    def step(s, inp):
        qc, kc, vc, gc = inp
        gcum = jnp.cumsum(gc, axis=-2)
        rel = masked_decay(gcum[..., :, None, :] - gcum[..., None, :, :], incl)
        attn = jnp.einsum('bhik,bhjk,bhijk->bhij', qc, kc, rel)
        o = (jnp.einsum('bhck,bhkv->bhcv', qc * jnp.exp(gcum), s)
             + jnp.einsum('bhij,bhjv->bhiv', attn, vc))
        g_last = gcum[..., -1:, :]
        s = (s * jnp.exp(g_last)[..., 0, :, None]
             + jnp.einsum('bhck,bhcv->bhkv', kc * jnp.exp(g_last - gcum), vc))
        return s, o

    s, o = lax.scan(step, s0, tuple(to_chunks(t) for t in (q, k, v, g)))
    return from_chunks(o), s


def bidirectional_scan(scan_fn, s0, ctx_fwd, lat_fwd, ctx_bwd, lat_bwd):
    flip = lambda t: jnp.flip(t, axis=2)
    oc_f, sc_f = scan_fn(s0, *ctx_fwd)
    ol_f, _ = scan_fn(sc_f, *lat_fwd)
    oc_b, sc_b = scan_fn(s0, *[flip(t) for t in ctx_bwd])
    ol_b, _ = scan_fn(sc_b, *[flip(t) for t in lat_bwd])
    return oc_f + flip(oc_b), ol_f + flip(ol_b)


def gdn_branch(p_ctx, p_lat, conv_w, a_log, dt_bias, norm_w):
    def prep(q, k, v, a, b):
        bsz, t = q.shape[:2]
        qkv = short_conv(jnp.concatenate([q, k, v], axis=-1), conv_w).astype(F32)
        q, k, v = jnp.split(qkv, [GDN_KEY, 2 * GDN_KEY], axis=-1)
        q = l2_normalize(split_heads(q, GDN_HEADS)) * GDN_DK ** -0.5
        k = l2_normalize(split_heads(k, GDN_HEADS))
        v = split_heads(v, GDN_HEADS)
        a = a.astype(F32).reshape(bsz, t, 2, GDN_HEADS)
        b = b.astype(F32).reshape(bsz, t, 2, GDN_HEADS)
        g = -jnp.exp(a_log.astype(F32)) * jax.nn.softplus(a + dt_bias.astype(F32))
        beta = jax.nn.sigmoid(b)
        g = jnp.transpose(g, (2, 0, 3, 1))
        beta = jnp.transpose(beta, (2, 0, 3, 1))
        return (q, k, v, g[0], beta[0]), (q, k, v, g[1], beta[1])

    c_fwd, c_bwd = prep(p_ctx[0], p_ctx[1], p_ctx[2], p_ctx[4], p_ctx[5])
    l_fwd, l_bwd = prep(p_lat[0], p_lat[1], p_lat[2], p_lat[4], p_lat[5])
    s0 = jnp.zeros((p_lat[0].shape[0], GDN_HEADS, GDN_DK, GDN_DV), F32)
    oc, ol = bidirectional_scan(gdn_chunk_scan, s0, c_fwd, l_fwd, c_bwd, l_bwd)

    def readout(o, gate):
        y = rms_norm(o, norm_w) * jax.nn.silu(split_heads(gate, GDN_HEADS).astype(F32))
        return merge_heads(y).astype(gate.dtype)

    return readout(oc, p_ctx[3]), readout(ol, p_lat[3])


def axial_rope_tables(n):
    rows = n // GRID_W
    row = jnp.repeat(jnp.arange(rows, dtype=F32), GRID_W)
    col = jnp.tile(jnp.arange(GRID_W, dtype=F32), rows)
    inv = ROPE_BASE ** (-jnp.arange(ROPE_NF, dtype=F32) / ROPE_NF)
    ang = jnp.stack([row[:, None] * inv, col[:, None] * inv], axis=1)
    return jnp.cos(ang), jnp.sin(ang)


def apply_axial_rope(x, cos, sin):
    xs = x.astype(F32).reshape(*x.shape[:-1], 2, 2, ROPE_NF)
    x1, x2 = xs[..., 0, :], xs[..., 1, :]
    c, s = cos[:, None], sin[:, None]
    out = jnp.stack([x1 * c - x2 * s, x2 * c + x1 * s], axis=-2)
    return out.reshape(x.shape).astype(x.dtype)


def softmax_attention(q, k, v):
    s = jnp.einsum('bqhd,bkhd->bhqk', q, k).astype(F32) * MLA_SCALE
    p = jax.nn.softmax(s, axis=-1).astype(v.dtype)
    o = jnp.einsum('bhqk,bkhd->bqhd', p, v)
    return o.reshape(o.shape[0], o.shape[1], -1)


def mla_branch(p_ctx, p_lat, q_norm_w, kv_norm_w, w_q_b, w_kv_b, cos, sin):
    def qkv(qa, kva, kr, rotate):
        bsz, t = qa.shape[:2]
        q = (rms_norm(qa, q_norm_w) @ w_q_b).reshape(bsz, t, MLA_HEADS, MLA_NOPE + MLA_ROPE)
        kv = (rms_norm(kva, kv_norm_w) @ w_kv_b).reshape(bsz, t, MLA_HEADS, MLA_NOPE + MLA_V)
        q_nope, q_rope = q[..., :MLA_NOPE], q[..., MLA_NOPE:]
        k_nope, v = kv[..., :MLA_NOPE], kv[..., MLA_NOPE:]
        k_rope = kr[:, :, None, :]
        if rotate:
            q_rope = apply_axial_rope(q_rope, cos, sin)
            k_rope = apply_axial_rope(k_rope, cos, sin)
        q = jnp.concatenate([q_nope, q_rope], axis=-1)
        k = jnp.concatenate([k_nope, jnp.broadcast_to(k_rope, (bsz, t, MLA_HEADS, MLA_ROPE))], axis=-1)
        return q, k, v

    qc, kc, vc = qkv(*p_ctx, False)
    ql, kl, vl = qkv(*p_lat, True)
    y_ctx = softmax_attention(qc, kc, vc)
    k_all = jnp.concatenate([kc, kl], axis=1)
    v_all = jnp.concatenate([vc, vl], axis=1)
    bsz, n = ql.shape[:2]
    qb = jnp.moveaxis(ql.reshape(bsz, n // Q_BLOCK, Q_BLOCK, MLA_HEADS, -1), 1, 0)
    ob = lax.map(lambda qblk: softmax_attention(qblk, k_all, v_all), qb)
    y_lat = jnp.moveaxis(ob, 0, 1).reshape(bsz, n, MLA_HEADS * MLA_V)
    return y_ctx, y_lat


def hgrn2_branch(p_ctx, p_lat, lb, norm_w):
    lb = lb.astype(F32)
    log_lb = jnp.log(jnp.maximum(lb, LB_FLOOR))
    log_1m_lb = jnp.log1p(-lb)

    def prep(q, f, i):
        bsz, t = q.shape[:2]
        f = f.astype(F32).reshape(bsz, t, 2, HG_KEY)
        log_f = jnp.logaddexp(log_lb, log_1m_lb + jax.nn.log_sigmoid(f))
        k = (1 - lb) * jax.nn.sigmoid(-f)
        q = split_heads(q.astype(F32), HG_HEADS) * HG_DK ** -0.5
        v = split_heads(i.astype(F32), HG_HEADS)
        return tuple((q, split_heads(k[:, :, d], HG_HEADS), v, split_heads(log_f[:, :, d], HG_HEADS))
                     for d in range(2))

    c_fwd, c_bwd = prep(p_ctx[0], p_ctx[1], p_ctx[2])
    l_fwd, l_bwd = prep(p_lat[0], p_lat[1], p_lat[2])
    s0 = jnp.zeros((p_lat[0].shape[0], HG_HEADS, HG_DK, HG_DV), F32)
    oc, ol = bidirectional_scan(gla_chunk_scan, s0, c_fwd, l_fwd, c_bwd, l_bwd)

    def readout(o, gate):
        y = rms_norm(o, norm_w) * jax.nn.silu(split_heads(gate, HG_HEADS).astype(F32))
        return merge_heads(y).astype(gate.dtype)

    return readout(oc, p_ctx[3]), readout(ol, p_lat[3])


def merge_branches(ys, gate_logits, w_branch, w_out):
    bsz, t = gate_logits.shape[:2]
    g = jax.nn.sigmoid(gate_logits.reshape(bsz, t, N_BRANCH, D_MODEL))
    m = sum(g[:, :, j] * (ys[j] @ w_branch[j]) for j in range(N_BRANCH))
    return m @ w_out


def setup_inputs(seed: int = 0) -> dict:
    key = jax.random.key(seed)
    ks = jax.random.split(key, 24)

    def dense(k, shape, fan_in):
        return jax.random.normal(k, shape, F32) * fan_in ** -0.5

    def gain(k, shape):
        return 1.0 + 0.02 * jax.random.normal(k, shape, F32)

    dt = jnp.exp(jax.random.uniform(ks[9], (DEPTH, 2, GDN_HEADS), F32, math.log(1e-3), math.log(1e-1)))
    return {
        'x': jax.random.normal(ks[0], (BATCH, SEQ, D_MODEL), F32),
        'c': jax.random.normal(ks[1], (BATCH, D_MODEL), F32),
        'ctx': jax.random.normal(ks[2], (BATCH, CTX_LEN, D_MODEL), F32),
        'c_ctx': jax.random.normal(ks[3], (D_MODEL,), F32),
        'w_ada': dense(ks[4], (DEPTH, D_MODEL, N_MOD * D_MODEL), D_MODEL),
        'b_ada': 0.01 * jax.random.normal(ks[5], (DEPTH, N_MOD * D_MODEL), F32),
        'norm_w': gain(ks[6], (DEPTH, 6, D_MODEL)),
        'ffn_w_in': dense(ks[7], (DEPTH, 2, D_MODEL, 2 * D_FF), D_MODEL),
        'ffn_w_out': dense(ks[8], (DEPTH, 2, D_FF, D_MODEL), D_FF),
        'w_in': dense(ks[10], (DEPTH, D_MODEL, IN_COLS), D_MODEL),
        'gdn_conv': dense(ks[11], (DEPTH, CONV_K, 2 * GDN_KEY + GDN_VAL), CONV_K),
        'gdn_a_log': jnp.log(jax.random.uniform(ks[12], (DEPTH, 2, GDN_HEADS), F32, 1.0, 16.0)),
        'gdn_dt_bias': dt + jnp.log(-jnp.expm1(-dt)),
        'gdn_norm': gain(ks[13], (DEPTH, GDN_DV)),
        'mla_q_norm': gain(ks[14], (DEPTH, MLA_Q_RANK)),
        'mla_kv_norm': gain(ks[15], (DEPTH, MLA_KV_RANK)),
        'mla_w_q_b': dense(ks[16], (DEPTH, MLA_Q_RANK, MLA_HEADS * (MLA_NOPE + MLA_ROPE)), MLA_Q_RANK),
        'mla_w_kv_b': dense(ks[17], (DEPTH, MLA_KV_RANK, MLA_HEADS * (MLA_NOPE + MLA_V)), MLA_KV_RANK),
        'hg_lb_logits': 0.1 * jax.random.normal(ks[18], (DEPTH, 2, HG_KEY), F32),
        'hg_norm': gain(ks[19], (DEPTH, HG_DV)),
        'w_branch': dense(ks[20], (DEPTH, N_BRANCH, BRANCH_W, D_MODEL), BRANCH_W),
        'w_out': dense(ks[21], (DEPTH, D_MODEL, D_MODEL), D_MODEL),
    }


def reference(x, c, ctx, c_ctx, w_ada, b_ada, norm_w, ffn_w_in, ffn_w_out, w_in,
              gdn_conv, gdn_a_log, gdn_dt_bias, gdn_norm,
              mla_q_norm, mla_kv_norm, mla_w_q_b, mla_w_kv_b,
              hg_lb_logits, hg_norm, w_branch, w_out):
    n = x.shape[1]
    cos, sin = axial_rope_tables(n)
    sm = jax.nn.softmax(hg_lb_logits.astype(F32), axis=0)
    lower_bounds = jnp.cumsum(sm, axis=0) - sm[0:1]
    split_pts = np.cumsum(IN_SIZES)[:-1].tolist()
    s_lat = jax.nn.silu(c)
    s_ctx = jax.nn.silu(c_ctx)
    h_lat, h_ctx = x, ctx
    for l in range(DEPTH):
        last = l == DEPTH - 1
        m_lat = jnp.split((s_lat @ w_ada[l] + b_ada[l])[:, None, :], N_MOD, axis=-1)
        m_ctx = jnp.split((s_ctx @ w_ada[l] + b_ada[l])[None, None, :], N_MOD, axis=-1)
        nw = norm_w[l]
        h_lat = ffn_sublayer(h_lat, m_lat[0:3], nw[0], nw[1], ffn_w_in[l, 0], ffn_w_out[l, 0])
        h_ctx = ffn_sublayer(h_ctx, m_ctx[0:3], nw[0], nw[1], ffn_w_in[l, 0], ffn_w_out[l, 0])
        u_lat = modulate(rms_norm(h_lat, nw[2]), m_lat[3], m_lat[4])
        u_ctx = modulate(rms_norm(h_ctx, nw[2]), m_ctx[3], m_ctx[4])
        p_lat = jnp.split(u_lat @ w_in[l], split_pts, axis=-1)
        p_ctx = jnp.split(u_ctx @ w_in[l], split_pts, axis=-1)
        ya_c, ya_l = gdn_branch(p_ctx[0:6], p_lat[0:6], gdn_conv[l], gdn_a_log[l], gdn_dt_bias[l], gdn_norm[l])
        yb_c, yb_l = mla_branch(p_ctx[6:9], p_lat[6:9], mla_q_norm[l], mla_kv_norm[l],
                                mla_w_q_b[l], mla_w_kv_b[l], cos, sin)
        yc_c, yc_l = hgrn2_branch(p_ctx[9:13], p_lat[9:13], lower_bounds[l], hg_norm[l])
        y_lat = merge_branches((ya_l, yb_l, yc_l), p_lat[13], w_branch[l], w_out[l])
        h_lat = h_lat + m_lat[5] * rms_norm(y_lat, nw[3])
        h_lat = ffn_sublayer(h_lat, m_lat[6:9], nw[4], nw[5], ffn_w_in[l, 1], ffn_w_out[l, 1])
        if not last:
            y_ctx = merge_branches((ya_c, yb_c, yc_c), p_ctx[13], w_branch[l], w_out[l])
            h_ctx = h_ctx + m_ctx[5] * rms_norm(y_ctx, nw[3])
            h_ctx = ffn_sublayer(h_ctx, m_ctx[6:9], nw[4], nw[5], ffn_w_in[l, 1], ffn_w_out[l, 1])
    return h_lat
```

```python
import math
import numpy as np
from contextlib import ExitStack
import concourse.bass as bass
import concourse.mybir as mybir
from concourse.bass_utils import run_bass_kernel_spmd

F32 = mybir.dt.float32
BF16 = mybir.dt.bfloat16
AF = mybir.ActivationFunctionType
ALU = mybir.AluOpType

D = 1024
KC = 8
DFF = 2816
NFC = 22
DEPTH = 2
NB = 2
SEQ = 8192
CTX = 256
NCORE = 8
NTOK = 2112
TSEQ = CTX + SEQ
EPS = 1e-6
TILES = [(0, 64, 1), (64, 512, 0), (576, 512, 0), (1088, 512, 0), (1600, 512, 0)]
GROUPS = [[0, 1], [2], [3], [4]]
GOFF = [0, 576, 1088, 1600]
GW = [576, 512, 512, 512]
NG = 576
NFM = 11
N_DMA_SEM = 24


class Buf:
    __slots__ = ("name", "lw", "readers")

    def __init__(self, name=""):
        self.name = name
        self.lw = None
        self.readers = []


class Op:
    __slots__ = ("eng", "fn", "deps", "marked", "mark_no", "is_dma", "dsem", "dval", "dprev", "epoch", "sep")

    def __init__(self, eng, fn, is_dma):
        self.epoch = 0
        self.sep = 0
        self.eng = eng
        self.fn = fn
        self.deps = []
        self.marked = False
        self.mark_no = 0
        self.is_dma = is_dma
        self.dsem = None
        self.dval = 0
        self.dprev = None


class Prog:
    ENGS = ("sync", "act", "dve", "pool", "pe")

    def __init__(self, nc, stack):
        self.nc = nc
        self.stack = stack
        self.ops = {e: [] for e in self.ENGS}
        self.n_dma = 0
        self.dma_last = [None] * N_DMA_SEM
        self.dma_cnt = [0] * N_DMA_SEM
        self._uid = 0
        self.psum = []
        self.psum_i = 0
        self.rot = None
        self.epoch = 0
        self.sep = {e: 0 for e in self.ENGS}
        self.sep_start = {e: 0 for e in self.ENGS}

    def sb(self, name, shape, dtype=F32, persistent=False):
        if getattr(self, "arena", None) is not None and not persistent:
            return self.arena.bf16(list(shape)) if dtype == BF16 else self.arena.f32(list(shape))
        self._uid += 1
        return self.stack.enter_context(self.nc.sbuf_tensor(f"{name}_{self._uid}", list(shape), dtype))

    def ps(self, name, shape, dtype=F32):
        self._uid += 1
        return self.stack.enter_context(self.nc.psum_tensor(f"{name}_{self._uid}", list(shape), dtype))

    def init_psum(self, n=8):
        for i in range(n):
            self.psum.append((self.ps(f"bank{i}", [128, 512], F32), Buf(f"bank{i}")))

    def bank(self):
        rot = self.rot if self.rot is not None else list(range(len(self.psum)))
        t = self.psum[rot[self.psum_i % len(rot)]]
        self.psum_i += 1
        return t

    def barrier(self):
        lasts = []
        for e in self.ENGS:
            for op in reversed(self.ops[e]):
                if not op.is_dma and op.fn is not None:
                    lasts.append(op)
                    break
        lasts += [o for o in self.dma_last if o is not None]
        for e in self.ENGS:
            op = Op(e, None, False)
            op.epoch = self.epoch
            op.sep = self.sep[e]
            for d in lasts:
                op.deps.append(d)
                if not d.is_dma:
                    d.marked = True
            self.ops[e].append(op)
        self.epoch += 1
        for e in self.ENGS:
            n = sum(1 for o in self.ops[e][self.sep_start[e]:] if o.marked and not o.is_dma)
            if n > 6000:
                self.sep[e] += 1
                self.sep_start[e] = len(self.ops[e])

    def add(self, eng, fn, reads=(), writes=(), is_dma=False):
        op = Op(eng, fn, is_dma)
        op.epoch = self.epoch
        op.sep = self.sep[eng]
        deps = []
        for b in reads:
            if b.lw is not None:
                deps.append(b.lw)
        for b in writes:
            if b.lw is not None:
                deps.append(b.lw)
            deps.extend(b.readers)
        seen = set()
        for d in deps:
            if id(d) in seen or d.epoch < self.epoch:
                continue
            seen.add(id(d))
            op.deps.append(d)
            if not d.is_dma:
                d.marked = True
        for b in reads:
            b.readers.append(op)
        for b in writes:
            b.lw = op
            b.readers = []
        if is_dma:
            s = self.n_dma % N_DMA_SEM
            self.n_dma += 1
            op.dsem = s
            op.dprev = self.dma_last[s]
            self.dma_cnt[s] += 16
            op.dval = self.dma_cnt[s]
            self.dma_last[s] = op
        self.ops[eng].append(op)
        return op

    def dma(self, out_ap, in_ap, reads=(), writes=(), eng="sync"):
        return self.add(eng, lambda e: e.dma_start(out=out_ap, in_=in_ap), reads, writes, True)

    def mm(self, out, lhsT, rhs, start, stop, reads, writes):
        return self.add("pe", lambda e: e.matmul(out, lhsT, rhs, start=start, stop=stop), reads, writes)

    def tr(self, out, in_, ident, reads, writes):
        return self.add("pe", lambda e: e.transpose(out, in_, ident), reads, writes)

    def act(self, out, in_, func, reads, writes, scale=None, bias=None, accum_out=None):
        kw = {}
        if scale is not None:
            kw["scale"] = scale
        if bias is not None:
            kw["bias"] = bias
        if accum_out is not None:
            kw["accum_out"] = accum_out
        return self.add("act", lambda e: e.activation(out=out, in_=in_, func=func, **kw), reads, writes)

    def tt(self, out, in0, in1, op, reads, writes, eng="dve"):
        return self.add(eng, lambda e: e.tensor_tensor(out=out, in0=in0, in1=in1, op=op), reads, writes)

    def ts(self, out, in0, s1, op0, reads, writes, s2=None, op1=None, eng="dve", accum_out=None):
        if op1 is None:
            return self.add(eng, lambda e: e.tensor_scalar(out=out, in0=in0, scalar1=s1, scalar2=None, op0=op0,
                                                            accum_out=accum_out), reads, writes)
        return self.add(eng, lambda e: e.tensor_scalar(out=out, in0=in0, scalar1=s1, scalar2=s2, op0=op0, op1=op1,
                                                        accum_out=accum_out), reads, writes)

    def stt(self, out, in0, scalar, in1, op0, op1, reads, writes):
        return self.add("dve", lambda e: e.scalar_tensor_tensor(out=out, in0=in0, scalar=scalar, in1=in1,
                                                                 op0=op0, op1=op1), reads, writes)

    def copy(self, out, in_, reads, writes, eng="dve"):
        if eng == "act":
            return self.add("act", lambda e: e.copy(out=out, in_=in_), reads, writes)
        return self.add(eng, lambda e: e.tensor_copy(out=out, in_=in_), reads, writes)

    def memset(self, ap, val, writes, eng="dve"):
        return self.add(eng, lambda e: e.memset(ap, val), (), writes)

    def emit(self):
        nc = self.nc
        st = self.stack
        esem = {(e, ep): st.enter_context(nc.semaphore(f"s_{e}_{ep}")) for e in self.ENGS
                for ep in range(self.sep[e] + 1)}
        dsem = [st.enter_context(nc.semaphore(f"s_dma{i}")) for i in range(N_DMA_SEM)]
        for e in self.ENGS:
            cnt = {}
            for op in self.ops[e]:
                if op.marked and not op.is_dma:
                    cnt[op.sep] = cnt.get(op.sep, 0) + 1
                    op.mark_no = cnt[op.sep]
        block = st.enter_context(nc.Block())
        handles = {"sync": block.sync, "act": block.scalar, "dve": block.vector,
                   "pool": block.gpsimd, "pe": block.tensor}
        final = [(dsem[i], self.dma_cnt[i]) for i in range(N_DMA_SEM) if self.dma_cnt[i] > 0]

        def make(ename):
            ops = self.ops[ename]

            def body(eng):
                waited = {}

                def wait(sem, key, val):
                    if waited.get(key, 0) >= val:
                        return
                    waited[key] = val
                    eng.wait_ge(sem, val)

                for op in ops:
                    for d in op.deps:
                        if d.is_dma:
                            wait(dsem[d.dsem], ("d", d.dsem), d.dval)
                        else:
                            wait(esem[(d.eng, d.sep)], ("e", d.eng, d.sep), d.mark_no)
                    if op.is_dma and op.dprev is not None:
                        wait(dsem[op.dsem], ("d", op.dsem), op.dprev.dval)
                    if op.fn is None:
                        continue
                    ins = op.fn(eng)
                    if op.is_dma:
                        ins.then_inc(dsem[op.dsem], 16)
                    elif op.marked:
                        ins.then_inc(esem[(ename, op.sep)], 1)
                if ename == "sync":
                    for sem, val in final:
                        eng.wait_ge(sem, val)
            return body

        for e in self.ENGS:
            handles[e](make(e))


class Slots:
    def __init__(self, P, name, shape, dtype, n):
        self.t = [P.sb(f"{name}{i}", shape, dtype) for i in range(n)]
        self.b = [Buf(f"{name}{i}") for i in range(n)]
        self.i = 0

    def next(self):
        k = self.i % len(self.t)
        self.i += 1
        return self.t[k], self.b[k]


class TokenPhase:
    def __init__(self, mode, ext=None, d=None, mods=None, shard=0):
        self.mode = mode
        self.ext = ext
        self.shard = shard
        self.do_merge = mode in ("mid", "last")
        self.do_proj = mode in ("first", "mid")
        if ext is not None:
            self.nc = ext.nc
            self.P = ext.P
            self.d = d
            self.mods = mods
            self.sets = [x for x in ("A", "B") if x in mods]
            self.build()
            return
        nc = self.nc = bass.Bass("TRN2", target_bir_lowering=False)
        dt = nc.dram_tensor
        self.d = {}

        def din(name, shape):
            self.d[name] = dt(name, list(shape), F32, kind="ExternalInput").ap()

        def dout(name, shape, dtype=F32):
            self.d[name] = dt(name, list(shape), dtype, kind="ExternalOutput").ap()

        din("hT", [128, KC, NTOK])
        din("cT", [128, KC, 2])
        din("ones", [128, 128])
        dout("hT_out", [128, KC, NTOK])
        sets = []
        if self.do_merge:
            sets.append("A")
            din("yT", [128, 12, NTOK])
            for nm, shp in (("wg", [8, 128, 3 * 8 * 128]), ("wb", [8, 128, 12 * 128]), ("wo", [8, 128, 1024])):
                din(nm, shp)
        if self.do_proj:
            sets.append("B")
            for nm, shp in (("wp", [38, 128, 1024]), ("wt", [128, 8 * 528]), ("wq", [128, 3 * 1024]),
                            ("wkn", [128, 2 * 512]), ("wkv", [128, 2 * 512]), ("qnw", [128, 3]), ("kvnw", [128, 2]),
                            ("rope_cos", [64, NTOK]), ("rope_sin", [64, NTOK])):
                din(nm, shp)
            dout("fm", [4, NFM, 128, NTOK])
            dout("tm_ab", [NTOK, 16])
            dout("tm_mv", [NTOK, 512])
            dout("tm_hv", [NTOK, 512])
        for s in sets:
            din(f"wada{s}", [24, 128, 8 * 384])
            din(f"bada{s}", [128, 72])
            din(f"normw{s}", [128, 6, 8])
            din(f"w1{s}", [NFC, 128, 2048])
            din(f"w2{s}", [8, 128, NFC * 128])
        self.sets = sets
        with ExitStack() as st:
            self.P = P = Prog(nc, st)
            self.build()
            P.emit()

    def build(self):
        P = self.P
        d = self.d
        if self.ext is None:
            P.init_psum(8)
            self.ones = P.sb("ones", [128, 128], BF16)
            self.b_ones = Buf()
            P.dma(self.ones[:], d["ones"], writes=[self.b_ones], eng="pool")
        else:
            P.rot = None
            self.ones = self.ext.cb[:, C_ONES, :]
            self.b_ones = self.ext.b_c
        self.h = P.sb("h", [128, KC, NG], F32)
        self.hn = P.sb("hn", [128, KC, NG], BF16)
        self.a = P.sb("a", [128, NFC, NG], BF16)
        self.y = P.sb("y", [128, KC, NG], F32)
        self.bh = [Buf() for _ in range(2)]
        self.bhn = [Buf() for _ in range(2)]
        self.ba = [[Buf() for _ in range(2)] for _ in range(NFC)]
        self.by = [[Buf() for _ in range(2)] for _ in range(KC)]
        self.sq = Slots(P, "sq", [128, KC, 512], BF16, 1)
        self.tmp = Slots(P, "tmp", [128, 512], F32, 6)
        self.rstd = Slots(P, "rstd", [128, 512], F32, 2)
        self.lnt = Slots(P, "lnt", [128, 512], F32, 2)
        self.w1s = Slots(P, "w1s", [128, KC, 256], BF16, 2)
        self.w2s = Slots(P, "w2s", [128, NFC, 128], BF16, 2)
        self.wps = Slots(P, "wps", [128, KC, 128], BF16, 4)
        if self.ext is None:
            self.mods = {}
            for s in self.sets:
                self.mods[s] = self.compute_mod(s)
        if self.do_proj:
            self.setup_proj()
        if self.do_merge:
            self.setup_merge()
        for g in range(4):
            tiles = [TILES[i] for i in GROUPS[g]]
            go = GOFF[g]
            for (off, w, z) in tiles:
                ti = 0 if z == 1 else 1
                lo = 512 if z == 1 else 0
                P.dma(self.h[:, :, lo:lo + w], d["hT"][:, :, off:off + w], writes=[self.bh[ti]])
            if self.do_merge:
                self.merge(g, "A")
                self.ffn(g, "A", 1)
            if self.do_proj:
                self.ffn(g, "B", 0)
                self.proj(g, "B")
            for (off, w, z) in tiles:
                ti = 0 if z == 1 else 1
                lo = 512 if z == 1 else 0
                P.dma(d["hT_out"][:, :, off:off + w], self.h[:, :, lo:lo + w], reads=[self.bh[ti]])

    def seq_off(self, off):
        j = self.shard
        return 64 * j + off if off < 64 else 256 + 2048 * j + (off - 64)

    def fm_ap(self, hh, r, p0, p1, off, w):
        if self.ext is None:
            return self.d["fm"][hh, r, p0:p1, off:off + w]
        so = self.seq_off(off)
        return self.d["fm_seq"][hh, r, p0:p1, so:so + w]

    def tm_out(self, name, off, sw, st_, ncol, bs_):
        P = self.P
        if self.ext is None:
            P.dma(self.d[name][off:off + sw, :], st_[0:sw, 0:4 * ncol], reads=[bs_])
            return
        so = self.seq_off(off)
        for hh in range(4):
            P.dma(self.d[name + "_seq"][hh, so:so + sw, :], st_[0:sw, hh * ncol:(hh + 1) * ncol], reads=[bs_])

    def compute_mod(self, s):
        P = self.P
        d = self.d
        cT = P.sb("cT", [128, KC, 2], F32)
        b_c = Buf()
        P.dma(cT[:], d["cT"], writes=[b_c])
        sT = P.sb("sT", [128, KC, 2], BF16)
        b_s = Buf()
        P.act(sT[:], cT[:], AF.Silu, [b_c], [b_s])
        bada = P.sb("bada", [128, 72], F32)
        b_b = Buf()
        P.dma(bada[:], d[f"bada{s}"], writes=[b_b])
        nw = P.sb("nw", [128, 6, 8], F32)
        b_nw = Buf()
        P.dma(nw[:], d[f"normw{s}"], writes=[b_nw])
        M = P.sb("M", [128, 72, 2], F32)
        b_M = Buf()
        wsl = Slots(P, "wada", [128, KC, 384], BF16, 2)
        for sl in range(24):
            wt, bw = wsl.next()
            P.dma(wt[:], d[f"wada{s}"][sl].rearrange("p (k n) -> p k n", k=KC), writes=[bw], eng="pool")
            pt, bp = P.bank()
            for ci in range(3):
                for k in range(KC):
                    P.mm(pt[:, 2 * ci:2 * ci + 2], wt[:, k, ci * 128:(ci + 1) * 128], sT[:, k, :],
                         k == 0, k == KC - 1, [bw, b_s], [bp])
            P.tt(M[:, sl * 3:(sl + 1) * 3, :], pt[:, 0:6].rearrange("p (c z) -> p c z", z=2),
                 bada[:, sl * 3:(sl + 1) * 3].unsqueeze(2).broadcast_to([128, 3, 2]), ALU.add, [bp, b_b], [b_M])
        S = P.sb("S", [128, 9, 8, 2], F32)
        b_S = Buf()

        def Mi(i):
            return M[:, i * 8:(i + 1) * 8, :]

        def nwb(i):
            return nw[:, i, :].unsqueeze(2).broadcast_to([128, 8, 2])
        one = P.sb("onep", [128, 8, 2], F32)
        b_one = Buf()
        for (k, nwi, sci) in ((0, 0, 1), (3, 2, 4), (6, 4, 7)):
            P.ts(one[:], Mi(sci), 1.0, ALU.add, [b_M], [b_one])
            P.tt(S[:, k], one[:], nwb(nwi), ALU.mult, [b_one, b_nw], [b_S])
        for (k, shi) in ((1, 0), (4, 3), (7, 6)):
            P.copy(S[:, k], Mi(shi), [b_M], [b_S])
        for (k, gi, nwi, f) in ((2, 2, 1, 0.5), (5, 5, 3, 1.0), (8, 8, 5, 0.5)):
            P.ts(one[:], Mi(gi), f, ALU.mult, [b_M], [b_one])
            P.tt(S[:, k], one[:], nwb(nwi), ALU.mult, [b_one, b_nw], [b_S])
        return S, b_S

    def rstd_of(self, src, b_src, nch, w, dn):
        P = self.P
        sq, bsq = self.sq.next()
        P.act(sq[:, 0:nch, 0:w], src, AF.Square, [b_src], [bsq])
        pt, bp = P.bank()
        for c in range(nch):
            P.mm(pt[:, 0:w], self.ones[:], sq[:, c, 0:w], c == 0, c == nch - 1, [bsq, self.b_ones], [bp])
        ln, bln = self.lnt.next()
        P.act(ln[:, 0:w], pt[:, 0:w], AF.Ln, [bp], [bln], scale=1.0 / dn, bias=EPS)
        r, br = self.rstd.next()
        P.act(r[:, 0:w], ln[:, 0:w], AF.Exp, [bln], [br], scale=-0.5)
        return r, br

    def norm_mod(self, g, s, kbase):
        P = self.P
        S, bS = self.mods[s]
        go = GOFF[g]
        for tix in GROUPS[g]:
            off, w, z = TILES[tix]
            ti = 0 if z == 1 else 1
            lo = 512 if z == 1 else 0
            r, br = self.rstd_of(self.h[:, :, lo:lo + w], self.bh[ti], KC, w, D)
            for c in range(KC):
                t, bt = self.tmp.next()
                P.tt(t[:, 0:w], self.h[:, c, lo:lo + w], r[:, 0:w], ALU.mult, [self.bh[ti], br], [bt])
                P.act(self.hn[:, c, lo:lo + w], t[:, 0:w], AF.Identity, [bt, bS], [self.bhn[ti]],
                      scale=S[:, kbase, c, z:z + 1], bias=S[:, kbase + 1, c, z:z + 1])

    def post_res(self, g, s, kgate):
        P = self.P
        S, bS = self.mods[s]
        go = GOFF[g]
        for tix in GROUPS[g]:
            off, w, z = TILES[tix]
            ti = 0 if z == 1 else 1
            lo = 512 if z == 1 else 0
            by_all = [self.by[c][ti] for c in range(KC)]
            bsrc = Buf()
            sq, bsq = self.sq.next()
            P.act(sq[:, :, 0:w], self.y[:, :, lo:lo + w], AF.Square, by_all, [bsq])
            pt, bp = P.bank()
            for c in range(KC):
                P.mm(pt[:, 0:w], self.ones[:], sq[:, c, 0:w], c == 0, c == KC - 1, [bsq, self.b_ones], [bp])
            ln, bln = self.lnt.next()
            P.act(ln[:, 0:w], pt[:, 0:w], AF.Ln, [bp], [bln], scale=1.0 / D, bias=EPS)
            r, br = self.rstd.next()
            P.act(r[:, 0:w], ln[:, 0:w], AF.Exp, [bln], [br], scale=-0.5)
            for c in range(KC):
                t, bt = self.tmp.next()
                P.tt(t[:, 0:w], self.y[:, c, lo:lo + w], r[:, 0:w], ALU.mult, [self.by[c][ti], br], [bt])
                P.stt(self.h[:, c, lo:lo + w], t[:, 0:w], S[:, kgate, c, z:z + 1], self.h[:, c, lo:lo + w],
                      ALU.mult, ALU.add, [bt, bS, self.bh[ti]], [self.bh[ti]])

    def ffn(self, g, s, i):
        P = self.P
        d = self.d
        kb = 0 if i == 0 else 6
        self.norm_mod(g, s, kb)
        go = GOFF[g]
        tl = [((0 if TILES[tix][2] == 1 else 1),) + TILES[tix] for tix in GROUPS[g]]
        w1 = d[f"w1{s}"]
        w2 = d[f"w2{s}"]
        for n in range(NFC):
            wt, bw = self.w1s.next()
            P.dma(wt[:], w1[n].rearrange("p (k n) -> p k n", k=KC), writes=[bw], eng="pool")
            for (ti, off, w, z) in tl:
                lo = 512 if z == 1 else 0
                pg, bpg = P.bank()
                pu, bpu = P.bank()
                for k in range(KC):
                    P.mm(pg[:, 0:w], wt[:, k, 0:128], self.hn[:, k, lo:lo + w], k == 0, k == KC - 1,
                         [bw, self.bhn[ti]], [bpg])
                for k in range(KC):
                    P.mm(pu[:, 0:w], wt[:, k, 128:256], self.hn[:, k, lo:lo + w], k == 0, k == KC - 1,
                         [bw, self.bhn[ti]], [bpu])
                t, bt = self.tmp.next()
                P.act(t[:, 0:w], pg[:, 0:w], AF.Silu, [bpg], [bt])
                P.tt(self.a[:, n, lo:lo + w], t[:, 0:w], pu[:, 0:w], ALU.mult, [bt, bpu], [self.ba[n][ti]])
        for dc in range(KC):
            wt, bw = self.w2s.next()
            P.dma(wt[:], w2[dc].rearrange("p (k n) -> p k n", k=NFC), writes=[bw], eng="pool")
            for (ti, off, w, z) in tl:
                lo = 512 if z == 1 else 0
                py, bpy = P.bank()
                for k in range(NFC):
                    P.mm(py[:, 0:w], wt[:, k, :], self.a[:, k, lo:lo + w], k == 0, k == NFC - 1,
                         [bw, self.ba[k][ti]], [bpy])
                P.copy(self.y[:, dc, lo:lo + w], py[:, 0:w], [bpy], [self.by[dc][ti]], eng="act")
        self.post_res(g, s, kb + 2)

    def setup_proj(self):
        P = self.P
        d = self.d
        self.wt = P.sb("wt", [128, KC, 528], BF16)
        self.b_wt = Buf()
        P.dma(self.wt[:], d["wt"].rearrange("p (k n) -> p k n", k=KC), writes=[self.b_wt], eng="pool")
        self.wq = P.sb("wq", [128, 3, 1024], BF16)
        self.b_wq = Buf()
        P.dma(self.wq[:], d["wq"].rearrange("p (k n) -> p k n", k=3), writes=[self.b_wq], eng="pool")
        self.wkn = P.sb("wkn", [128, 2, 512], BF16)
        self.b_wkn = Buf()
        P.dma(self.wkn[:], d["wkn"].rearrange("p (k n) -> p k n", k=2), writes=[self.b_wkn], eng="pool")
        self.wkv = P.sb("wkv", [128, 2, 512], BF16)
        self.b_wkv = Buf()
        P.dma(self.wkv[:], d["wkv"].rearrange("p (k n) -> p k n", k=2), writes=[self.b_wkv], eng="pool")
        self.qnw = P.sb("qnw", [128, 3], F32)
        self.kvnw = P.sb("kvnw", [128, 2], F32)
        self.b_nws = Buf()
        P.dma(self.qnw[:], d["qnw"], writes=[self.b_nws])
        P.dma(self.kvnw[:], d["kvnw"], writes=[self.b_nws])
        self.cos = P.sb("cos", [64, NG], F32)
        self.sin = P.sb("sin", [64, NG], F32)
        self.b_rope = Buf()
        self.qkva = self.y[:, 0:5, :]
        self.b_qkva = self.by[0:5]
        self.qkvn = self.a[:, 0:5, :]
        self.b_qkvn5 = self.ba[0:5]
        self.krr = self.y[0:64, 5:7, :]
        self.b_krr2 = self.by[5:7]
        self.stage = Slots(P, "stage", [128, 512], F32, 3)
        self.stage2 = self.stage

    def proj(self, g, s):
        P = self.P
        d = self.d
        self.norm_mod(g, s, 3)
        go = GOFF[g]
        gw = GW[g]
        tl = [((0 if TILES[tix][2] == 1 else 1),) + TILES[tix] for tix in GROUPS[g]]
        for (ti, off, w, z) in tl:
            lo = 512 if z == 1 else 0
            P.dma(self.cos[:, lo:lo + w], d["rope_cos"][:, off:off + w], writes=[self.b_rope])
            P.dma(self.sin[:, lo:lo + w], d["rope_sin"][:, off:off + w], writes=[self.b_rope])
        ev = 0
        for ci in range(38):
            wt, bw = self.wps.next()
            P.dma(wt[:], d["wp"][ci].rearrange("p (k n) -> p k n", k=KC), writes=[bw], eng="pool")
            for (ti, off, w, z) in tl:
                lo = 512 if z == 1 else 0
                if ci < 37:
                    pt, bp = P.bank()
                    for k in range(KC):
                        P.mm(pt[:, 0:w], wt[:, k, :], self.hn[:, k, lo:lo + w], k == 0, k == KC - 1,
                             [bw, self.bhn[ti]], [bp])
                    if ci < 32:
                        hh, r = ci // 8, ci % 8
                        st_, bs_ = self.stage.next()
                        P.copy(st_[:, 0:w], pt[:, 0:w], [bp], [bs_], eng=("act" if ev % 2 == 0 else "dve"))
                        ev += 1
                        P.dma(self.fm_ap(hh, r, 0, 128, off, w), st_[:, 0:w], reads=[bs_])
                    else:
                        P.copy(self.qkva[:, ci - 32, lo:lo + w], pt[:, 0:w], [bp], [self.b_qkva[ci - 32][ti]],
                               eng=("act" if ev % 2 == 0 else "dve"))
                        ev += 1
                else:
                    for half in range(2):
                        pt, bp = P.bank()
                        for k in range(KC):
                            P.mm(pt[0:64, 0:w], wt[:, k, half * 64:(half + 1) * 64], self.hn[:, k, lo:lo + w],
                                 k == 0, k == KC - 1, [bw, self.bhn[ti]], [bp])
                        P.copy(self.krr[:, half, lo:lo + w], pt[0:64, 0:w], [bp], [self.b_krr2[half][ti]], eng="act")
        for (ti, off, w, z) in tl:
            lo = 512 if z == 1 else 0
            t1, b1 = self.stage2.next()
            t2, b2 = self.stage2.next()
            P.tt(t1[0:64, 0:w], self.krr[:, 0, lo:lo + w], self.cos[:, lo:lo + w], ALU.mult,
                 [self.b_krr2[0][ti], self.b_rope], [b1])
            P.tt(t2[0:64, 0:w], self.krr[:, 1, lo:lo + w], self.sin[:, lo:lo + w], ALU.mult,
                 [self.b_krr2[1][ti], self.b_rope], [b2])
            P.tt(t1[0:64, 0:w], t1[0:64, 0:w], t2[0:64, 0:w], ALU.add, [b1, b2], [b1])
            for hh in range(4):
                P.dma(self.fm_ap(hh, 10, 64, 128, off, w), t1[0:64, 0:w], reads=[b1])
        for (ti, off, w, z) in tl:
            lo = 512 if z == 1 else 0
            for (c0, nch, dn, nwt) in ((0, 3, 384, self.qnw), (3, 2, 256, self.kvnw)):
                bsrc = [self.b_qkva[c0 + c][ti] for c in range(nch)]
                sq, bsq = self.sq.next()
                P.act(sq[:, 0:nch, 0:w], self.qkva[:, c0:c0 + nch, lo:lo + w], AF.Square, bsrc, [bsq])
                pt, bp = P.bank()
                for c in range(nch):
                    P.mm(pt[:, 0:w], self.ones[:], sq[:, c, 0:w], c == 0, c == nch - 1, [bsq, self.b_ones], [bp])
                ln, bln = self.lnt.next()
                P.act(ln[:, 0:w], pt[:, 0:w], AF.Ln, [bp], [bln], scale=1.0 / dn, bias=EPS)
                r, br = self.rstd.next()
                P.act(r[:, 0:w], ln[:, 0:w], AF.Exp, [bln], [br], scale=-0.5)
                for c in range(nch):
                    t, bt = self.tmp.next()
                    P.tt(t[:, 0:w], self.qkva[:, c0 + c, lo:lo + w], r[:, 0:w], ALU.mult,
                         [self.b_qkva[c0 + c][ti], br], [bt])
                    P.act(self.qkvn[:, c0 + c, lo:lo + w], t[:, 0:w], AF.Identity, [bt, self.b_nws],
                          [self.b_qkvn5[c0 + c][ti]], scale=nwt[:, c:c + 1])
        for hh in range(4):
            for (ti, off, w, z) in tl:
                lo = 512 if z == 1 else 0
                pt, bp = P.bank()
                for k in range(3):
                    P.mm(pt[:, 0:w], self.wq[:, k, hh * 256:hh * 256 + 128], self.qkvn[:, k, lo:lo + w],
                         k == 0, k == 2, [self.b_wq, self.b_qkvn5[k][ti]], [bp])
                st_, bs_ = self.stage.next()
                P.copy(st_[:, 0:w], pt[:, 0:w], [bp], [bs_], eng="act")
                P.dma(self.fm_ap(hh, 8, 0, 128, off, w), st_[:, 0:w], reads=[bs_])
                pt, bp = P.bank()
                for k in range(2):
                    P.mm(pt[:, 0:w], self.wkn[:, k, hh * 128:(hh + 1) * 128], self.qkvn[:, 3 + k, lo:lo + w],
                         k == 0, k == 1, [self.b_wkn, self.b_qkvn5[3 + k][ti]], [bp])
                st_, bs_ = self.stage.next()
                P.copy(st_[:, 0:w], pt[:, 0:w], [bp], [bs_], eng="dve")
                P.dma(self.fm_ap(hh, 9, 0, 128, off, w), st_[:, 0:w], reads=[bs_])
                pr, bpr = P.bank()
                for k in range(3):
                    P.mm(pr[0:64, 0:w], self.wq[:, k, hh * 256 + 128:hh * 256 + 192], self.qkvn[:, k, lo:lo + w],
                         k == 0, k == 2, [self.b_wq, self.b_qkvn5[k][ti]], [bpr])
                ps_, bps = P.bank()
                for k in range(3):
                    P.mm(ps_[0:64, 0:w], self.wq[:, k, hh * 256 + 192:hh * 256 + 256], self.qkvn[:, k, lo:lo + w],
                         k == 0, k == 2, [self.b_wq, self.b_qkvn5[k][ti]], [bps])
                t1, b1 = self.stage2.next()
                t2, b2 = self.stage2.next()
                P.tt(t1[0:64, 0:w], pr[0:64, 0:w], self.cos[:, lo:lo + w], ALU.mult, [bpr, self.b_rope], [b1])
                P.tt(t2[0:64, 0:w], ps_[0:64, 0:w], self.sin[:, lo:lo + w], ALU.mult, [bps, self.b_rope], [b2])
                P.tt(t1[0:64, 0:w], t1[0:64, 0:w], t2[0:64, 0:w], ALU.add, [b1, b2], [b1])
                P.dma(self.fm_ap(hh, 10, 0, 64, off, w), t1[0:64, 0:w], reads=[b1])
        for (ti, off, w, z) in tl:
            lo = 512 if z == 1 else 0
            for s0 in range(0, w, 128):
                sw = min(128, w - s0)
                a0 = lo + s0
                pt, bp = P.bank()
                for k in range(KC):
                    P.mm(pt[0:sw, :], self.hn[:, k, a0:a0 + sw], self.wt[:, k, 0:512], k == 0, k == KC - 1,
                         [self.b_wt, self.bhn[ti]], [bp])
                st_, bs_ = self.stage.next()
                P.copy(st_[0:sw, :], pt[0:sw, :], [bp], [bs_], eng="act")
                self.tm_out("tm_hv", off + s0, sw, st_, 128, bs_)
                pt, bp = P.bank()
                for k in range(KC):
                    P.mm(pt[0:sw, 0:16], self.hn[:, k, a0:a0 + sw], self.wt[:, k, 512:528], k == 0, k == KC - 1,
                         [self.b_wt, self.bhn[ti]], [bp])
                st_, bs_ = self.stage.next()
                P.copy(st_[0:sw, 0:16], pt[0:sw, 0:16], [bp], [bs_], eng="dve")
                self.tm_out("tm_ab", off + s0, sw, st_, 4, bs_)
                pt, bp = P.bank()
                for k in range(2):
                    P.mm(pt[0:sw, :], self.qkvn[:, 3 + k, a0:a0 + sw], self.wkv[:, k, :], k == 0, k == 1,
                         [self.b_wkv, self.b_qkvn5[3 + k][ti]], [bp])
                st_, bs_ = self.stage.next()
                P.copy(st_[0:sw, :], pt[0:sw, :], [bp], [bs_], eng="act")
                self.tm_out("tm_mv", off + s0, sw, st_, 128, bs_)

    def setup_merge(self):
        P = self.P
        self.yT = self.a[:, 0:12, :]
        self.b_yTq = self.ba[0:12]
        self.mT = self.a[:, 12:20, :]
        self.b_mT = self.ba[12:20]
        self.wbs = Slots(P, "wbs", [128, 12, 128], BF16, 2)
        self.sg = self.tmp
        self.macc = Slots(P, "macc", [128, 512], F32, 2)

    def merge(self, g, s):
        P = self.P
        d = self.d
        S, bS = self.mods[s]
        go = GOFF[g]
        tl = [((0 if TILES[tix][2] == 1 else 1),) + TILES[tix] for tix in GROUPS[g]]
        for (ti, off, w, z) in tl:
            lo = 512 if z == 1 else 0
            if self.ext is None:
                P.dma(self.yT[:, :, lo:lo + w], d["yT"][:, :, off:off + w],
                      writes=[self.b_yTq[q][ti] for q in range(12)], eng="pool")
            else:
                so = self.seq_off(off)
                for bi in range(3):
                    for hh in range(4):
                        P.dma(self.yT[:, bi * 4 + hh, lo:lo + w], d["yseq"][hh, bi, :, so:so + w],
                              writes=[self.b_yTq[bi * 4 + hh][ti]], eng="pool")
        self.norm_mod(g, s, 3)
        for c in range(KC):
            wgl = []
            for j in range(3):
                wg, bwg = self.wps.next()
                P.dma(wg[:], d["wg"][c][:, j * 1024:(j + 1) * 1024].rearrange("p (k n) -> p k n", k=KC),
                      writes=[bwg], eng="pool")
                wgl.append((wg, bwg))
            wb, bwb = self.wbs.next()
            P.dma(wb[:], d["wb"][c].rearrange("p (q n) -> p q n", q=12), writes=[bwb], eng="pool")
            for (ti, off, w, z) in tl:
                lo = 512 if z == 1 else 0
                acc, bacc = self.macc.next()
                for j in range(3):
                    pg, bpg = P.bank()
                    wg, bwg = wgl[j]
                    for k in range(KC):
                        P.mm(pg[:, 0:w], wg[:, k, :], self.hn[:, k, lo:lo + w], k == 0, k == KC - 1,
                             [bwg, self.bhn[ti]], [bpg])
                    pb, bpb = P.bank()
                    for q in range(4):
                        P.mm(pb[:, 0:w], wb[:, j * 4 + q, :], self.yT[:, j * 4 + q, lo:lo + w], q == 0, q == 3,
                             [bwb, self.b_yTq[j * 4 + q][ti]], [bpb])
                    sg, bsg = self.sg.next()
                    P.act(sg[:, 0:w], pg[:, 0:w], AF.Sigmoid, [bpg], [bsg])
                    if j == 0:
                        P.tt(acc[:, 0:w], sg[:, 0:w], pb[:, 0:w], ALU.mult, [bsg, bpb], [bacc])
                    else:
                        P.tt(sg[:, 0:w], sg[:, 0:w], pb[:, 0:w], ALU.mult, [bsg, bpb], [bsg])
                        if j == 1:
                            P.tt(acc[:, 0:w], acc[:, 0:w], sg[:, 0:w], ALU.add, [bacc, bsg], [bacc])
                        else:
                            P.tt(self.mT[:, c, lo:lo + w], acc[:, 0:w], sg[:, 0:w], ALU.add, [bacc, bsg],
                                 [self.b_mT[c][ti]])
        for c in range(KC):
            wo, bwo = self.wps.next()
            P.dma(wo[:], d["wo"][c].rearrange("p (k n) -> p k n", k=KC), writes=[bwo], eng="pool")
            for (ti, off, w, z) in tl:
                lo = 512 if z == 1 else 0
                py, bpy = P.bank()
                for k in range(KC):
                    P.mm(py[:, 0:w], wo[:, k, :], self.mT[:, k, lo:lo + w], k == 0, k == KC - 1,
                         [bwo, self.b_mT[k][ti]], [bpy])
                P.copy(self.y[:, c, lo:lo + w], py[:, 0:w], [bpy], [self.by[c][ti]], eng="act")
        self.post_res(g, s, 5)


def _fm(v):
    return np.ascontiguousarray(v.reshape(-1, 128).T)


def _wtile(w, cols):
    K = w.shape[0]
    sub = w[:, cols].reshape(K // 128, 128, len(cols))
    return np.ascontiguousarray(sub.transpose(1, 0, 2).reshape(128, -1))


def _partner(j):
    return j + 16 if (j % 32) < 16 else j - 16


def prep_ffn_set(inp, l, i):
    out = {}
    wa = inp["w_ada"][l]
    out["wada"] = np.stack([_wtile(wa, np.arange(sl * 384, (sl + 1) * 384)) for sl in range(24)])
    out["bada"] = _fm(inp["b_ada"][l])
    out["normw"] = np.ascontiguousarray(inp["norm_w"][l].reshape(6, 8, 128).transpose(2, 0, 1))
    w1 = inp["ffn_w_in"][l, i]
    out["w1"] = np.stack([_wtile(w1, np.concatenate([np.arange(n * 128, (n + 1) * 128),
                                                     DFF + np.arange(n * 128, (n + 1) * 128)])) for n in range(NFC)])
    w2 = inp["ffn_w_out"][l, i]
    out["w2"] = np.stack([_wtile(w2, np.arange(dc * 128, (dc + 1) * 128)) for dc in range(8)])
    return out


def prep_proj(inp, l):
    out = {}
    w_in = inp["w_in"][l]
    chunks = []
    for hh in range(4):
        for base in (0, 512, 1024, 1536, 2768, 3280, 3792, 4816):
            chunks.append(np.arange(base + hh * 128, base + (hh + 1) * 128))
    for c in range(3):
        chunks.append(np.arange(2064 + c * 128, 2064 + (c + 1) * 128))
    for c in range(2):
        chunks.append(np.arange(2448 + c * 128, 2448 + (c + 1) * 128))
    kr = 2704 + np.arange(64)
    chunks.append(np.concatenate([kr, 2704 + np.array([_partner(j) for j in range(64)])]))
    out["wp"] = np.stack([_wtile(w_in, c) for c in chunks])
    abc = np.array([[2048 + hh, 2052 + hh, 2056 + hh, 2060 + hh] for hh in range(4)]).reshape(-1)
    out["wt"] = _wtile(w_in, np.concatenate([4304 + np.arange(512), abc]))
    wq = inp["mla_w_q_b"][l]
    qc = []
    for hh in range(4):
        qc += list(hh * 192 + np.arange(128)) + list(hh * 192 + 128 + np.arange(64)) + \
            [hh * 192 + 128 + _partner(j) for j in range(64)]
    out["wq"] = _wtile(wq, np.array(qc))
    wkv = inp["mla_w_kv_b"][l]
    out["wkn"] = _wtile(wkv, np.concatenate([hh * 256 + np.arange(128) for hh in range(4)]))
    out["wkv"] = _wtile(wkv, np.concatenate([hh * 256 + 128 + np.arange(128) for hh in range(4)]))
    out["qnw"] = _fm(inp["mla_q_norm"][l])
    out["kvnw"] = _fm(inp["mla_kv_norm"][l])
    return out


def prep_merge(inp, l):
    out = {}
    w_in = inp["w_in"][l]
    out["wg"] = np.stack([np.concatenate([_wtile(w_in, 5328 + j * 1024 + c * 128 + np.arange(128)) for j in range(3)],
                                         axis=1) for c in range(8)])
    wb = inp["w_branch"][l]
    out["wb"] = np.stack([np.concatenate([_wtile(wb[j], c * 128 + np.arange(128)) for j in range(3)], axis=1)
                          for c in range(8)])
    out["wo"] = np.stack([_wtile(inp["w_out"][l], c * 128 + np.arange(128)) for c in range(8)])
    return out


def rope_tables(shard):
    cos = np.ones((64, NTOK), np.float64)
    sin = np.zeros((64, NTOK), np.float64)
    t = shard * 2048 + np.arange(2048)
    row = (t // 64).astype(np.float64)
    col = (t % 64).astype(np.float64)
    inv = (10000.0 ** (-np.arange(16, dtype=np.float32) / np.float32(16))).astype(np.float32).astype(np.float64)
    for dd in range(64):
        a_, s_, f_ = dd // 32, (dd % 32) // 16, dd % 16
        pos = row if a_ == 0 else col
        ang = (pos.astype(np.float32) * inv[f_].astype(np.float32)).astype(np.float64)
        cos[dd, 64:] = np.cos(ang)
        sin[dd, 64:] = np.sin(ang) * (-1.0 if s_ == 0 else 1.0)
    return cos.astype(np.float32), sin.astype(np.float32)


def tok_to_fm(h):
    return np.ascontiguousarray(h.reshape(NTOK, 8, 128).transpose(2, 1, 0))


def fm_to_tok(hT):
    return np.ascontiguousarray(hT.transpose(2, 1, 0).reshape(NTOK, 1024))


class Arena:
    def __init__(self, P, kb=192):
        self.cap = kb * 256
        self.t = P.sb("arena", [128, self.cap], F32)
        self.off = 0

    def reset(self):
        self.off = 0

    def _shape(self, ap, shape):
        if len(shape) == 3:
            return ap.rearrange("p (a b) -> p a b", a=shape[1])
        if len(shape) == 4:
            return ap.rearrange("p (a b c) -> p a b c", a=shape[1], b=shape[2])
        return ap

    def f32(self, shape):
        n = int(np.prod(shape[1:]))
        o = self.off
        self.off += n
        assert self.off <= self.cap, ("arena overflow", self.off, self.cap)
        return self._shape(self.t[0:shape[0], o:o + n], shape)

    def bf16(self, shape):
        n = int(np.prod(shape[1:]))
        nw = (n + 1) // 2
        o = self.off
        self.off += nw
        assert self.off <= self.cap, ("arena overflow", self.off, self.cap)
        ap = self.t[0:shape[0], o:o + nw].bitcast(BF16)[:, 0:n]
        return self._shape(ap, shape)


class ASlots:
    def __init__(self, aps):
        self.t = aps
        self.b = [Buf() for _ in aps]
        self.i = 0

    def next(self):
        k = self.i % len(self.t)
        self.i += 1
        return self.t[k], self.b[k]


NBLK = TSEQ // 128
MLA_SCALE = 192 ** -0.5
C_ONES, C_IDENT, C_HM0, C_HM1, C_NEG0, C_NEG1, C_STR0, C_STR1, C_TRI0, C_TRI1, C_ROWM = range(11)
C_LVL = 11
NCONST = 25


def make_consts():
    c = np.zeros((NCONST, 128, 128), np.float32)
    i = np.arange(128)
    c[C_ONES] = 1.0
    c[C_IDENT] = np.eye(128)
    J, I = np.meshgrid(i, i, indexing="ij")
    same32 = (J // 32) == (I // 32)
    c[C_HM0] = (same32 & (I >= J))
    c[C_HM1] = (same32 & (I <= J))
    c[C_NEG0] = np.where(I >= J, 0.0, -30000.0)
    c[C_NEG1] = np.where(I <= J, 0.0, -30000.0)
    c[C_STR0] = (I > J)
    c[C_STR1] = (I < J)
    c[C_TRI0] = (J <= I)
    c[C_TRI1] = (J >= I)
    for cc in range(4):
        c[C_ROWM][:, cc] = (i // 32 == cc)
    for d in range(2):
        for l in range(7):
            b = 2 ** l
            Ii, Jj = J, I
            sameblk = (Ii // (2 * b)) == (Jj // (2 * b))
            ih = (Ii // b) % 2
            jh = (Jj // b) % 2
            if d == 0:
                m = sameblk & (ih == 1) & (jh == 0)
            else:
                m = sameblk & (ih == 0) & (jh == 1)
            c[C_LVL + d * 7 + l] = -1.0 * m
    return c


class MixerPhase:
    def __init__(self, layer, parts=("gdn", "mla", "hg"), stage=99, ext=None, d=None, small=None):
        self.layer = layer
        self.stage = stage
        if ext is not None:
            self.nc, self.P, self.d, self.A = ext.nc, ext.P, d, ext.A
            self.cf, self.cb, self.b_c, self.small = ext.cf, ext.cb, ext.b_c, small
            self.mla()
            self.hgrn2()
            self.gdn()
            return
        nc = self.nc = bass.Bass("TRN2", target_bir_lowering=False)
        self.d = {}

        def din(name, shape):
            self.d[name] = nc.dram_tensor(name, list(shape), F32, kind="ExternalInput").ap()

        din("fm", [NFM, 128, TSEQ])
        din("tm_ab", [TSEQ, 4])
        din("tm_mv", [TSEQ, 128])
        din("tm_hv", [TSEQ, 128])
        din("consts", [NCONST, 128, 128])
        din("rmask", [128, 1408])
        din("small", [128, 32])
        self.d["yT"] = nc.dram_tensor("yT", [3, 128, TSEQ], F32, kind="ExternalOutput").ap()
        with ExitStack() as st:
            self.P = P = Prog(nc, st)
            P.init_psum(8)
            self.cf = P.sb("cf", [128, NCONST, 128], F32)
            self.cb = P.sb("cb", [128, NCONST, 128], BF16)
            self.b_c = Buf()
            P.dma(self.cf[:], self.d["consts"].rearrange("n p f -> p n f"), writes=[self.b_c])
            P.dma(self.cb[:], self.d["consts"].rearrange("n p f -> p n f"), writes=[self.b_c], eng="pool")
            self.small = P.sb("small", [128, 32], F32)
            P.dma(self.small[:], self.d["small"], writes=[self.b_c])
            self.A = Arena(P, 178)
            if "mla" in parts:
                self.mla()
            if "hg" in parts:
                self.hgrn2()
            if "gdn" in parts:
                self.gdn()
            P.emit()

    def mla(self):
        P, A, d = self.P, self.A, self.d
        P.barrier()
        A.reset()
        P.rot = [0, 1, 2, 3]
        T = TSEQ
        fm = d["fm"]
        qn = A.bf16([128, T]); kn = A.bf16([128, T]); qr = A.bf16([64, T]); kr = A.bf16([64, T])
        v = A.bf16([128, NBLK, 128])
        bq, bk, bv = Buf(), Buf(), Buf()
        for c0 in range(0, T, 2112):
            P.dma(kn[:, c0:c0 + 2112], fm[9][:, c0:c0 + 2112], writes=[bk], eng="pool")
            P.dma(kr[:, c0:c0 + 2112], fm[10][64:128, c0:c0 + 2112], writes=[bk], eng="pool")
            P.dma(qn[:, c0:c0 + 2112], fm[8][:, c0:c0 + 2112], writes=[bq], eng="pool")
            P.dma(qr[:, c0:c0 + 2112], fm[10][0:64, c0:c0 + 2112], writes=[bq], eng="pool")
        for n0 in range(0, NBLK, 22):
            P.dma(v[:, n0:n0 + 22, :], d["tm_mv"][n0 * 128:(n0 + 22) * 128, :].rearrange("(n p) d -> p n d", p=128),
                  writes=[bv], eng="pool")
        pts = ASlots([A.bf16([128, 512]) for _ in range(3)])
        rcs = ASlots([A.f32([128, 512]) for _ in range(2)])
        yos = ASlots([A.f32([128, 512]) for _ in range(2)])
        ones = self.cb[:, C_ONES, :]
        blocks = [(0, 256, 2)] + [(256 + 512 * b, 512, NBLK) for b in range(16)]
        for bi, (q0, qw, nk) in enumerate(blocks):
            O, bO = P.psum[4 + bi % 2]
            L, bL = P.psum[6 + bi % 2]
            for kc in range(nk):
                S, bS = P.bank()
                P.mm(S[:, 0:qw], kn[:, kc * 128:(kc + 1) * 128], qn[:, q0:q0 + qw], True, False, [bk, bq], [bS])
                P.mm(S[:, 0:qw], kr[:, kc * 128:(kc + 1) * 128], qr[:, q0:q0 + qw], False, True, [bk, bq], [bS])
                pt, bpt = pts.next()
                P.act(pt[:, 0:qw], S[:, 0:qw], AF.Exp, [bS], [bpt], scale=MLA_SCALE)
                P.mm(O[:, 0:qw], v[:, kc, :], pt[:, 0:qw], kc == 0, kc == nk - 1, [bv, bpt], [bO])
                P.mm(L[:, 0:qw], ones, pt[:, 0:qw], kc == 0, kc == nk - 1, [self.b_c, bpt], [bL])
            rc, brc = rcs.next()
            P.add("dve", (lambda o_, i_: (lambda e: e.reciprocal(out=o_, in_=i_)))(rc[:, 0:qw], L[:, 0:qw]), [bL], [brc])
            yo, byo = yos.next()
            P.tt(yo[:, 0:qw], O[:, 0:qw], rc[:, 0:qw], ALU.mult, [bO, brc], [byo])
            P.dma(d["yT"][1, :, q0:q0 + qw], yo[:, 0:qw], reads=[byo])

    def readout(self, of, b_of_all, gate_row, nw_ap, out_idx, scr):
        P, d = self.P, self.d
        W = 384
        g_s, sq_s, ln_s, r_s, t_s = scr
        ones = self.cb[:, C_ONES, :]
        for t0 in range(0, TSEQ, W):
            g, bg = g_s.next()
            P.dma(g[:, 0:W], d["fm"][gate_row][:, t0:t0 + W], writes=[bg])
            P.act(g[:, 0:W], g[:, 0:W], AF.Silu, [bg], [bg])
            sq, bsq = sq_s.next()
            P.act(sq[:, 0:W], of[:, t0:t0 + W], AF.Square, b_of_all, [bsq])
            pt, bp = P.bank()
            P.mm(pt[:, 0:W], ones, sq[:, 0:W], True, True, [self.b_c, bsq], [bp])
            ln, bln = ln_s.next()
            P.act(ln[:, 0:W], pt[:, 0:W], AF.Ln, [bp], [bln], scale=1.0 / 128, bias=EPS)
            r, br = r_s.next()
            P.act(r[:, 0:W], ln[:, 0:W], AF.Exp, [bln], [br], scale=-0.5)
            t, bt = t_s.next()
            P.tt(t[:, 0:W], of[:, t0:t0 + W], r[:, 0:W], ALU.mult, b_of_all + [br], [bt])
            P.stt(t[:, 0:W], t[:, 0:W], nw_ap, g[:, 0:W], ALU.mult, ALU.mult, [bt, bg, self.b_c], [bt])
            P.dma(d["yT"][out_idx, :, t0:t0 + W], t[:, 0:W], reads=[bt])

    def hgrn2(self):
        P, A, d = self.P, self.A, self.d
        P.barrier()
        A.reset()
        P.rot = [0, 1, 2, 3]
        T = TSEQ
        SEG = 1408
        NSEG = 6
        fm = d["fm"]
        sm = self.small
        v128 = A.bf16([128, NBLK, 128]); b_v = Buf()
        for n0 in range(0, NBLK, 22):
            P.dma(v128[:, n0:n0 + 22, :], d["tm_hv"][n0 * 128:(n0 + 22) * 128, :].rearrange("(n p) d -> p n d", p=128),
                  writes=[b_v], eng="pool")
        of = A.f32([128, T]); b_of = [Buf() for _ in range(NBLK)]
        qt = A.bf16([128, T]); kt = A.bf16([128, T]); kh = A.bf16([128, T])
        b_qt, b_kt, b_kh = [Buf() for _ in range(6)], [Buf() for _ in range(6)], [Buf() for _ in range(6)]
        khtm = A.bf16([128, NBLK, 128]); b_khtm = [Buf() for _ in range(NBLK)]
        dec = A.f32([128, 264]); b_dec = [Buf() for _ in range(6)]
        rmask = A.f32([128, SEG]); b_rm = Buf()
        P.dma(rmask, d["rmask"], writes=[b_rm])
        sc = [A.f32([128, SEG]) for _ in range(6)]
        bsc = [Buf() for _ in range(6)]
        sA, sB, sC, sD, sT, sQ = sc
        bA, bB, bC, bD, bT, bQ = bsc
        S = [A.f32([128, 128]) for _ in range(2)]; bS = [Buf(), Buf()]
        Sb = [A.bf16([128, 128]) for _ in range(2)]; bSb = [Buf(), Buf()]
        ams = ASlots([A.bf16([128, 128]) for _ in range(3)])
        kms = ASlots([A.bf16([128, 128]) for _ in range(4)])
        lbt = A.f32([128, 8]); b_lb = Buf()
        for dd in range(2):
            if self.layer == 0:
                P.memset(lbt[:, dd:dd + 1], 0.0, [b_lb])
            else:
                P.tt(lbt[:, dd:dd + 1], sm[:, 2 + dd:3 + dd], sm[:, dd:dd + 1], ALU.subtract, [self.b_c], [b_lb])
                P.act(lbt[:, dd:dd + 1], lbt[:, dd:dd + 1], AF.Sigmoid, [b_lb], [b_lb])
            P.ts(lbt[:, 2 + dd:3 + dd], lbt[:, dd:dd + 1], -1.0, ALU.mult, [b_lb], [b_lb], s2=1.0, op1=ALU.add)
            P.ts(lbt[:, 4 + dd:5 + dd], lbt[:, dd:dd + 1], 1.0, ALU.subtract, [b_lb], [b_lb])
        identb = self.cb[:, C_IDENT, :]

        def v3(ap):
            return ap.rearrange("p (c w) -> p c w", w=32)

        for dd in range(2):
            lb, oml, noml = lbt[:, dd:dd + 1], lbt[:, 2 + dd:3 + dd], lbt[:, 4 + dd:5 + dd]
            for seg in range(NSEG):
                s0 = seg * SEG
                P.dma(sA, fm[5 + dd][:, s0:s0 + SEG], writes=[bA])
                P.dma(sQ, fm[4][:, s0:s0 + SEG], writes=[bQ])
                P.act(sA, sA, AF.Sigmoid, [bA], [bA])
                P.ts(sB, sA, noml, ALU.mult, [bA, b_lb], [bB], s2=oml, op1=ALU.add)
                P.ts(sA, sA, oml, ALU.mult, [bA, b_lb], [bA], s2=lb, op1=ALU.add)
                P.act(sA, sA, AF.Ln, [bA], [bA])
                P.add("dve", lambda e: e.tensor_tensor_scan(out=sC, data0=rmask, data1=sA, initial=0.0,
                                                             op0=ALU.mult, op1=ALU.add), [bA, b_rm], [bC])
                totb = v3(sC)[:, :, 31:32].broadcast_to([128, SEG // 32, 32])
                if dd == 0:
                    G, bG = sC, bC
                else:
                    P.tt(sD, sA, sC, ALU.subtract, [bA, bC], [bD])
                    P.tt(v3(sD), v3(sD), totb, ALU.add, [bD, bC], [bD])
                    G, bG = sD, bD
                P.act(dec[:, seg * 44:(seg + 1) * 44], v3(sC)[:, :, 31], AF.Exp, [bC], [b_dec[seg]])
                P.tt(v3(sT), totb, v3(G), ALU.subtract, [bC, bG], [bT])
                P.act(sT, sT, AF.Exp, [bT], [bT])
                P.tt(kh[:, s0:s0 + SEG], sB, sT, ALU.mult, [bB, bT], [b_kh[seg]])
                P.act(sT, G, AF.Exp, [bG], [bT], scale=-1.0)
                P.tt(kt[:, s0:s0 + SEG], sB, sT, ALU.mult, [bB, bT], [b_kt[seg]])
                P.act(sT, G, AF.Exp, [bG], [bT])
                P.stt(qt[:, s0:s0 + SEG], sQ, 128 ** -0.5, sT, ALU.mult, ALU.mult, [bQ, bT], [b_qt[seg]])
            if self.stage < 1:
                P.dma(d["yT"][2, :, 0:SEG], sT, reads=[bT])
                return
            for blk in range(NBLK):
                ptr, bptr = P.bank()
                P.mm(ptr[:, 0:128], kh[:, blk * 128:(blk + 1) * 128], identb, True, True,
                     [b_kh[blk // 11], self.b_c], [bptr])
                P.copy(khtm[:, blk, :], ptr[:, 0:128], [bptr], [b_khtm[blk]],
                       eng=("act" if blk % 2 == 0 else "dve"))
            if self.stage < 2:
                P.dma(d["yT"][2, :, 0:SEG], sT, reads=[bT] + b_khtm)
                return
            P.memset(S[0], 0.0, [bS[0]])
            P.memset(Sb[0], 0.0, [bSb[0]], eng="pool")
            n = 0
            hm = self.cf[:, C_HM0 + dd, :]
            if dd == 0:
                order = list(range(NBLK)); corder = [0, 1, 2, 3]
            else:
                order = [1, 0] + list(range(NBLK - 1, 1, -1)); corder = [3, 2, 1, 0]
            for bi, blk in enumerate(order):
                sg = blk // 11
                c0 = blk * 128
                pa, bpa = P.bank()
                P.mm(pa[:, 0:128], kt[:, c0:c0 + 128], qt[:, c0:c0 + 128], True, True, [b_kt[sg], b_qt[sg]], [bpa])
                am, bam = ams.next()
                P.tt(am, pa[:, 0:128], hm, ALU.mult, [bpa, self.b_c], [bam])
                po, bpo = P.psum[4 + bi % 2]
                P.mm(po[:, 0:128], v128[:, blk, :], am, True, False, [b_v, bam], [bpo])
                for ci, cc in enumerate(corder):
                    km, bkm = kms.next()
                    P.ts(km, khtm[:, blk, :], self.cf[:, C_ROWM, cc:cc + 1], ALU.mult, [b_khtm[blk], self.b_c], [bkm],
                         eng="pool")
                    P.mm(po[:, cc * 32:(cc + 1) * 32], Sb[n % 2], qt[:, c0 + cc * 32:c0 + (cc + 1) * 32], False, ci == 3,
                         [bSb[n % 2], b_qt[sg]], [bpo])
                    pk, bpk = P.bank()
                    P.mm(pk[:, 0:128], km, v128[:, blk, :], True, True, [bkm, b_v], [bpk])
                    P.stt(S[(n + 1) % 2], S[n % 2], dec[:, blk * 4 + cc:blk * 4 + cc + 1], pk[:, 0:128], ALU.mult, ALU.add,
                          [bS[n % 2], bpk, b_dec[sg]], [bS[(n + 1) % 2]])
                    P.copy(Sb[(n + 1) % 2], S[(n + 1) % 2], [bS[(n + 1) % 2]], [bSb[(n + 1) % 2]], eng="act")
                    n += 1
                if dd == 0:
                    P.copy(of[:, c0:c0 + 128], po[:, 0:128], [bpo], [b_of[blk]], eng="act")
                else:
                    P.tt(of[:, c0:c0 + 128], of[:, c0:c0 + 128], po[:, 0:128], ALU.add, [b_of[blk], bpo], [b_of[blk]])
        W = 384
        scr = (ASlots([sA[:, 0:W], sA[:, W:2 * W]]), ASlots([kt[:, 0:W], kt[:, W:2 * W]]),
               ASlots([sB[:, 0:W], sB[:, W:2 * W]]), ASlots([sC[:, 0:W], sC[:, W:2 * W]]),
               ASlots([sD[:, 0:W], sD[:, W:2 * W]]))
        P.barrier()
        self.readout(of, b_of, 7, sm[:, 4:5], 2, scr)


def prep_mixer_small(inp, l, hh):
    sm = np.zeros((128, 32), np.float32)
    ch = hh * 128 + np.arange(128)
    lbl = inp["hg_lb_logits"]
    sm[:, 0] = lbl[0, 0, ch]; sm[:, 1] = lbl[0, 1, ch]
    sm[:, 2] = lbl[1, 0, ch]; sm[:, 3] = lbl[1, 1, ch]
    sm[:, 4] = inp["hg_norm"][l]
    sm[:, 5] = inp["gdn_norm"][l]
    sm[:, 6] = inp["gdn_a_log"][l, 0, hh]; sm[:, 7] = inp["gdn_a_log"][l, 1, hh]
    sm[:, 8] = inp["gdn_dt_bias"][l, 0, hh]; sm[:, 9] = inp["gdn_dt_bias"][l, 1, hh]
    cw = inp["gdn_conv"][l]
    for ti in range(3):
        for tap in range(3):
            sm[:, 10 + ti * 3 + tap] = cw[tap, ti * 512 + ch]
    return sm


def token_core_inputs(inp, mode, core, h_tok, y_tok=None):
    b, j = core // 4, core % 4
    m = {"hT": tok_to_fm(h_tok), "ones": _CONST["ones"], "cT": _CONST["cT"][b]}
    if mode in ("mid", "last"):
        l = 0 if mode == "mid" else 1
        for k, v in _CONST[("ffn", l, 1)].items():
            m[k + "A"] = v
        m.update(_CONST[("merge", l)])
        m["yT"] = np.ascontiguousarray(y_tok.reshape(3, NTOK, 4, 128).transpose(3, 0, 2, 1).reshape(128, 12, NTOK))
    if mode in ("first", "mid"):
        l = 0 if mode == "first" else 1
        for k, v in _CONST[("ffn", l, 0)].items():
            m[k + "B"] = v
        m.update(_CONST[("proj", l)])
        m["rope_cos"], m["rope_sin"] = _CONST[("rope", j)]
    return m


_CONST = {}
_PROGS = {}


def _prepare(inp):
    _CONST.clear()
    _CONST["ones"] = np.ones((128, 128), np.float32)
    _CONST["cT"] = [np.ascontiguousarray(np.stack([_fm(inp["c"][b]), _fm(inp["c_ctx"])], -1)) for b in range(NB)]
    for l in range(DEPTH):
        for i in range(2):
            _CONST[("ffn", l, i)] = prep_ffn_set(inp, l, i)
        _CONST[("proj", l)] = prep_proj(inp, l)
        _CONST[("merge", l)] = prep_merge(inp, l)
    for j in range(4):
        _CONST[("rope", j)] = rope_tables(j)
    _CONST["consts"] = make_consts()
    rm = np.ones((128, 1408), np.float32)
    rm[:, ::32] = 0.0
    _CONST["rmask"] = rm


def _prog(key):
    if key not in _PROGS:
        if key[0] == "tok":
            _PROGS[key] = TokenPhase(key[1])
        else:
            _PROGS[key] = MixerPhase(key[1])
    return _PROGS[key]


def run_token(inp, mode, h_cores, y_cores=None):
    tp = _prog(("tok", mode))
    maps = [token_core_inputs(inp, mode, c, h_cores[c], None if y_cores is None else y_cores[c]) for c in range(NCORE)]
    res = run_bass_kernel_spmd(tp.nc, maps, core_ids=list(range(NCORE)))
    return res.results


def _seq_order(parts_ctx, parts_lat, axis):
    return np.concatenate(parts_ctx + parts_lat, axis=axis)


def mixer_inputs(inp, l, tok_res):
    maps = []
    for core in range(NCORE):
        b, hh = core // 4, core % 4
        rs = [tok_res[b * 4 + j] for j in range(4)]
        fm = _seq_order([r["fm"][hh][:, :, 0:64] for r in rs], [r["fm"][hh][:, :, 64:] for r in rs], 2)
        ab = _seq_order([r["tm_ab"][0:64] for r in rs], [r["tm_ab"][64:] for r in rs], 0)
        ab = np.ascontiguousarray(ab[:, hh * 4:(hh + 1) * 4])
        mv = _seq_order([r["tm_mv"][0:64, hh * 128:(hh + 1) * 128] for r in rs],
                        [r["tm_mv"][64:, hh * 128:(hh + 1) * 128] for r in rs], 0)
        hv = _seq_order([r["tm_hv"][0:64, hh * 128:(hh + 1) * 128] for r in rs],
                        [r["tm_hv"][64:, hh * 128:(hh + 1) * 128] for r in rs], 0)
        maps.append({"fm": np.ascontiguousarray(fm), "tm_ab": ab, "tm_mv": np.ascontiguousarray(mv),
                     "tm_hv": np.ascontiguousarray(hv), "consts": _CONST["consts"], "rmask": _CONST["rmask"],
                     "small": prep_mixer_small(inp, l, hh)})
    return maps


def run_mixer(inp, l, tok_res):
    mp = _prog(("mix", l))
    res = run_bass_kernel_spmd(mp.nc, mixer_inputs(inp, l, tok_res), core_ids=list(range(NCORE)))
    return res.results


def y_for_token_cores(mix_res):
    out = []
    for core in range(NCORE):
        b, j = core // 4, core % 4
        y = np.zeros((3, NTOK, 512), np.float32)
        for hh in range(4):
            yT = mix_res[b * 4 + hh]["yT"]
            ctx = yT[:, :, 64 * j:64 * j + 64]
            lat = yT[:, :, 256 + 2048 * j:256 + 2048 * (j + 1)]
            y[:, :, hh * 128:(hh + 1) * 128] = np.concatenate([ctx, lat], 2).transpose(0, 2, 1)
        out.append(y)
    return out


def _gdn(self):
    P, A, d = self.P, self.A, self.d
    P.barrier()
    A.reset()
    P.rot = [0, 1, 2, 3]
    T = TSEQ
    SEG = 1408
    fm = d["fm"]
    sm = self.small
    cf, cb = self.cf, self.cb
    b_c = self.b_c
    onesb, identb = cb[:, C_ONES, :], cb[:, C_IDENT, :]
    onesf, identf = cf[:, C_ONES, :], cf[:, C_IDENT, :]
    qT = A.bf16([128, T]); kT = A.bf16([128, T])
    b_qT = [Buf() for _ in range(6)]; b_kT = [Buf() for _ in range(6)]
    k_tm = A.bf16([128, NBLK, 128]); v_tm = A.bf16([128, NBLK, 128])
    b_ktm = [Buf() for _ in range(NBLK)]; b_vtm = [Buf() for _ in range(NBLK)]
    of = A.f32([128, T]); b_of = [Buf() for _ in range(NBLK)]
    nsm = A.f32([128, 32]); b_nsm = Buf()
    P.ts(nsm, sm[:, :], -1.0, ALU.mult, [b_c], [b_nsm])
    Xs = ASlots([A.f32([128, SEG + 2]) for _ in range(2)])
    Ys = ASlots([A.f32([128, SEG]) for _ in range(2)])
    vTs = ASlots([A.bf16([128, SEG]) for _ in range(2)])
    sqs = ASlots([A.bf16([128, 352]) for _ in range(2)])
    lns = ASlots([A.f32([128, 352]) for _ in range(2)])
    rs = ASlots([A.f32([128, 352]) for _ in range(2)])
    for ti in range(3):
        for seg in range(6):
            s0 = seg * SEG
            X, bX = Xs.next()
            lo = max(s0 - 1, 0)
            hi = min(s0 + SEG + 1, T)
            if seg == 0:
                P.memset(X[:, 0:1], 0.0, [bX])
            if seg == 5:
                P.memset(X[:, SEG + 1:SEG + 2], 0.0, [bX])
            P.dma(X[:, lo - (s0 - 1):hi - (s0 - 1)], fm[ti][:, lo:hi], writes=[bX])
            Y, bY = Ys.next()
            wc = 10 + ti * 3
            P.ts(Y, X[:, 1:SEG + 1], sm[:, wc + 1:wc + 2], ALU.mult, [bX, b_c], [bY])
            P.stt(Y, X[:, 0:SEG], sm[:, wc:wc + 1], Y, ALU.mult, ALU.add, [bX, b_c, bY], [bY])
            P.stt(Y, X[:, 2:SEG + 2], sm[:, wc + 2:wc + 3], Y, ALU.mult, ALU.add, [bX, b_c, bY], [bY])
            if seg == 0:
                P.stt(Y[:, 255:256], X[:, 257:258], nsm[:, wc + 2:wc + 3], Y[:, 255:256], ALU.mult, ALU.add,
                      [bX, b_nsm, bY], [bY])
                P.stt(Y[:, 256:257], X[:, 256:257], nsm[:, wc:wc + 1], Y[:, 256:257], ALU.mult, ALU.add,
                      [bX, b_nsm, bY], [bY])
            P.act(Y, Y, AF.Silu, [bY], [bY])
            if ti < 2:
                dst, bdst = (qT, b_qT) if ti == 0 else (kT, b_kT)
                for t0 in range(0, SEG, 352):
                    sq, bsq = sqs.next()
                    P.act(sq, Y[:, t0:t0 + 352], AF.Square, [bY], [bsq])
                    pt, bp = P.bank()
                    P.mm(pt[:, 0:352], onesb, sq, True, True, [b_c, bsq], [bp])
                    ln, bln = lns.next()
                    P.act(ln, pt[:, 0:352], AF.Ln, [bp], [bln], bias=EPS)
                    r, br = rs.next()
                    P.act(r, ln, AF.Exp, [bln], [br], scale=-0.5, bias=(math.log(128 ** -0.5) if ti == 0 else 0.0))
                    P.tt(dst[:, s0 + t0:s0 + t0 + 352], Y[:, t0:t0 + 352], r, ALU.mult, [bY, br], [bdst[seg]])
                if ti == 1:
                    for bb in range(11):
                        blk = seg * 11 + bb
                        ptr, bptr = P.bank()
                        P.mm(ptr[:, 0:128], kT[:, blk * 128:(blk + 1) * 128], identb, True, True, [b_kT[seg], b_c], [bptr])
                        P.copy(k_tm[:, blk, :], ptr[:, 0:128], [bptr], [b_ktm[blk]], eng=("act" if bb % 2 else "dve"))
            else:
                vT, bvT = vTs.next()
                P.copy(vT, Y, [bY], [bvT], eng="pool")
                for bb in range(11):
                    blk = seg * 11 + bb
                    ptr, bptr = P.bank()
                    P.mm(ptr[:, 0:128], vT[:, bb * 128:(bb + 1) * 128], identb, True, True, [bvT, b_c], [bptr])
                    P.copy(v_tm[:, blk, :], ptr[:, 0:128], [bptr], [b_vtm[blk]], eng=("act" if bb % 2 else "dve"))
    ab = A.f32([128, NBLK, 4]); b_ab = Buf()
    P.dma(ab, d["tm_ab"].rearrange("(n p) c -> p n c", p=128), writes=[b_ab])
    cols = {}
    for dd in range(2):
        g = A.f32([128, NBLK]); bg = Buf()
        tmpc = A.f32([128, NBLK]); btmp = Buf()
        nA = A.f32([128, 1]); bnA = Buf()
        P.act(tmpc, ab[:, :, dd], AF.Exp, [b_ab, b_c], [btmp], bias=sm[:, 8 + dd:9 + dd])
        P.act(tmpc, tmpc, AF.Ln, [btmp], [btmp], bias=1.0)
        P.act(nA, sm[:, 6 + dd:7 + dd], AF.Exp, [b_c], [bnA])
        P.ts(nA, nA, -1.0, ALU.mult, [bnA], [bnA])
        P.ts(g, tmpc, nA, ALU.mult, [btmp, bnA], [bg])
        beta = A.f32([128, NBLK]); bbeta = Buf()
        P.act(beta, ab[:, :, 2 + dd], AF.Sigmoid, [b_ab], [bbeta])
        gcum = A.f32([128, NBLK]); negg = A.f32([128, NBLK]); negeg = A.f32([128, NBLK])
        ekl = A.f32([128, NBLK]); decS = A.f32([128, NBLK]); bcol = Buf()
        pt, bp = P.bank()
        P.mm(pt[:, 0:NBLK], cf[:, C_TRI0 + dd, :], g, True, True, [b_c, bg], [bp])
        P.copy(gcum, pt[:, 0:NBLK], [bp], [bcol])
        P.ts(negg, gcum, -1.0, ALU.mult, [bcol], [bcol])
        P.act(negeg, gcum, AF.Exp, [bcol], [bcol])
        P.ts(negeg, negeg, -1.0, ALU.mult, [bcol], [bcol])
        pt2, bp2 = P.bank()
        P.mm(pt2[:, 0:NBLK], onesf, g, True, True, [b_c, bg], [bp2])
        P.act(decS, pt2[:, 0:NBLK], AF.Exp, [bp2], [bcol])
        P.tt(ekl, pt2[:, 0:NBLK], gcum, ALU.subtract, [bp2, bcol], [bcol])
        P.act(ekl, ekl, AF.Exp, [bcol], [bcol])
        cols[dd] = dict(gcum=gcum, negg=negg, negeg=negeg, ekl=ekl, decS=decS, beta=beta, b=[bcol, bbeta])
    def mk(n, dt, cnt):
        return ASlots([(A.f32([128, 128]) if dt == "f" else A.bf16([128, 128])) for _ in range(cnt)])
    st = {}
    for dd in range(2):
        st[dd] = dict(
            dg=mk(0, "f", 2), Dm=mk(0, "f", 2), ET=mk(0, "f", 2), EGB=mk(0, "f", 2), ETs=mk(0, "f", 2),
            attnT=mk(0, "b", 3), N=mk(0, "b", 2), qg=mk(0, "b", 3), khat=mk(0, "b", 3), NB=mk(0, "b", 2),
            Tm=mk(0, "b", 3), Um=mk(0, "b", 3), Ufin=mk(0, "b", 3), R0=mk(0, "b", 2), vnew=mk(0, "b", 2),
            S=[A.f32([128, 128]) for _ in range(2)], bS=[Buf(), Buf()],
            Sb=[A.bf16([128, 128]) for _ in range(2)], bSb=[Buf(), Buf()], n=0, pend=None)
        P.memset(st[dd]["S"][0], 0.0, [st[dd]["bS"][0]])
        P.memset(st[dd]["Sb"][0], 0.0, [st[dd]["bSb"][0]], eng="pool")
    orders = {0: list(range(NBLK)), 1: [1, 0] + list(range(NBLK - 1, 1, -1))}
    written = set()

    def offchain(dd, blk):
        s = st[dd]
        cl = cols[dd]
        bcl = cl["b"]
        sg = blk // 11
        c0 = blk * 128
        kTb, qTb = kT[:, c0:c0 + 128], qT[:, c0:c0 + 128]
        pKK, bKK = P.bank()
        P.mm(pKK[:, 0:128], kTb, kTb, True, True, [b_kT[sg]], [bKK])
        pQK, bQK = P.bank()
        P.mm(pQK[:, 0:128], kTb, qTb, True, True, [b_kT[sg], b_qT[sg]], [bQK])
        dg, bdg = s["dg"].next()
        P.ts(dg, identf, cl["gcum"][:, blk:blk + 1], ALU.mult, [b_c] + bcl, [bdg])
        pG, bG = P.bank()
        P.mm(pG[:, 0:128], onesf, dg, True, True, [b_c, bdg], [bG])
        Dm, bDm = s["Dm"].next()
        P.stt(Dm, pG[:, 0:128], cl["negg"][:, blk:blk + 1], cf[:, C_NEG0 + dd, :], ALU.add, ALU.add, [bG, b_c] + bcl, [bDm])
        ET, bET = s["ET"].next()
        P.act(ET, Dm, AF.Exp, [bDm], [bET])
        EGB, bEGB = s["EGB"].next()
        P.act(EGB, pG[:, 0:128], AF.Exp, [bG], [bEGB])
        ETs, bETs = s["ETs"].next()
        P.tt(ETs, ET, cf[:, C_STR0 + dd, :], ALU.mult, [bET, b_c], [bETs], eng="pool")
        attnT, battn = s["attnT"].next()
        P.tt(attnT, pQK[:, 0:128], ET, ALU.mult, [bQK, bET], [battn])
        N, bN = s["N"].next()
        P.stt(N, pKK[:, 0:128], cl["beta"][:, blk:blk + 1], ETs, ALU.mult, ALU.mult, [bKK, bETs] + bcl, [bN])
        qg, bqg = s["qg"].next()
        P.tt(qg, qTb, EGB, ALU.mult, [b_qT[sg], bEGB], [bqg], eng="pool")
        khat, bkhat = s["khat"].next()
        P.ts(khat, k_tm[:, blk, :], cl["ekl"][:, blk:blk + 1], ALU.mult, [b_ktm[blk]] + bcl, [bkhat], eng="pool")
        Tc, bTc = identb, b_c
        Uc, bUc = identb, b_c
        for l in range(7):
            pB, bB = P.bank()
            P.mm(pB[:, 0:128], N, Tc, True, True, [bN, bTc], [bB])
            NB, bNB = s["NB"].next()
            P.tt(NB, pB[:, 0:128], cf[:, C_LVL + dd * 7 + l, :], ALU.mult, [bB, b_c], [bNB])
            if l < 6:
                pT, bT_ = P.bank()
                P.mm(pT[:, 0:128], Uc, NB, True, False, [bUc, bNB], [bT_])
                P.mm(pT[:, 0:128], identb, Tc, False, True, [b_c, bTc], [bT_])
                Tn, bTn = s["Tm"].next()
                P.copy(Tn, pT[:, 0:128], [bT_], [bTn], eng="act")
            pU, bU_ = P.bank()
            P.mm(pU[:, 0:128], NB, Uc, True, False, [bNB, bUc], [bU_])
            P.mm(pU[:, 0:128], identb, Uc, False, True, [b_c, bUc], [bU_])
            Un, bUn = (s["Um"] if l < 6 else s["Ufin"]).next()
            P.copy(Un, pU[:, 0:128], [bU_], [bUn], eng="act")
            if l < 6:
                Tc, bTc = Tn, bTn
            Uc, bUc = Un, bUn
        s["pend"] = dict(blk=blk, U=Uc, bU=bUc, attnT=attnT, battn=battn, qg=qg, bqg=bqg, khat=khat, bkhat=bkhat)

    def inchain(dd, step):
        s = st[dd]
        pd = s["pend"]
        if pd is None:
            return
        s["pend"] = None
        cl = cols[dd]
        bcl = cl["b"]
        blk = pd["blk"]
        sg = blk // 11
        c0 = blk * 128
        n = s["n"]
        cur, nxt = n % 2, (n + 1) % 2
        pkS, bkS = P.bank()
        P.mm(pkS[:, 0:128], kT[:, c0:c0 + 128], s["Sb"][cur], True, True, [b_kT[sg], s["bSb"][cur]], [bkS])
        R0, bR0 = s["R0"].next()
        P.stt(R0, pkS[:, 0:128], cl["negeg"][:, blk:blk + 1], v_tm[:, blk, :], ALU.mult, ALU.add,
              [bkS, b_vtm[blk]] + bcl, [bR0])
        pV, bV = P.bank()
        P.mm(pV[:, 0:128], pd["U"], R0, True, True, [pd["bU"], bR0], [bV])
        vnew, bvn = s["vnew"].next()
        P.act(vnew, pV[:, 0:128], AF.Identity, [bV] + bcl, [bvn], scale=cl["beta"][:, blk:blk + 1])
        po, bpo = P.psum[4 + 2 * dd + step % 2]
        P.mm(po[:, 0:128], s["Sb"][cur], pd["qg"], True, False, [s["bSb"][cur], pd["bqg"]], [bpo])
        P.mm(po[:, 0:128], vnew, pd["attnT"], False, True, [bvn, pd["battn"]], [bpo])
        pKV, bKV = P.bank()
        P.mm(pKV[:, 0:128], pd["khat"], vnew, True, True, [pd["bkhat"], bvn], [bKV])
        P.stt(s["S"][nxt], s["S"][cur], cl["decS"][:, blk:blk + 1], pKV[:, 0:128], ALU.mult, ALU.add,
              [s["bS"][cur], bKV] + bcl, [s["bS"][nxt]])
        P.copy(s["Sb"][nxt], s["S"][nxt], [s["bS"][nxt]], [s["bSb"][nxt]], eng="act")
        if blk not in written:
            written.add(blk)
            P.copy(of[:, c0:c0 + 128], po[:, 0:128], [bpo], [b_of[blk]], eng="dve")
        else:
            P.tt(of[:, c0:c0 + 128], of[:, c0:c0 + 128], po[:, 0:128], ALU.add, [b_of[blk], bpo], [b_of[blk]])
        s["n"] = n + 1

    dirs = [0, 1]
    if self.stage == 11:
        dirs = [0]
    if self.stage == 12:
        dirs = [1]
    for step in range(NBLK + 1):
        for dd in dirs:
            inch = st[dd]["pend"]
            if step < NBLK:
                prev = st[dd]["pend"]
                st[dd]["pend"] = None
                offchain(dd, orders[dd][step])
                newp = st[dd]["pend"]
                st[dd]["pend"] = prev
                inchain(dd, step)
                st[dd]["pend"] = newp
            else:
                inchain(dd, step)
    P.barrier()
    if self.stage in (10, 11, 12):
        for c0 in range(0, T, 2112):
            P.dma(d["yT"][0, :, c0:c0 + 2112], of[:, c0:c0 + 2112], reads=b_of)
        return
    W = 384
    scrA, scrC, scrD, scrE = Xs.t[0], Ys.t[0], Xs.t[1], Ys.t[1]
    scrB = vTs.t[0]
    scr = tuple(ASlots([x[:, 0:W], x[:, W:2 * W]]) for x in (scrA, scrB, scrC, scrD, scrE))
    self.readout(of, b_of, 3, sm[:, 5:6], 0, scr)


MixerPhase.gdn = _gdn


class Fused:
    def __init__(self):
        nc = self.nc = bass.Bass("TRN2", target_bir_lowering=False)
        self.d = d = {}

        def din(name, shape):
            d[name] = nc.dram_tensor(name, list(shape), F32, kind="ExternalInput").ap()

        def dscr(name, shape, dtype=F32):
            d[name] = nc.dram_tensor(name, list(shape), dtype, kind="Internal").ap()

        din("hT", [4, 128, KC, NTOK])
        din("cT", [128, KC, 2])
        din("consts", [NCONST, 128, 128])
        din("rmask", [128, 1408])
        din("small", [8, 128, 32])
        din("rope_cos", [4, 64, NTOK])
        din("rope_sin", [4, 64, NTOK])
        for l in range(DEPTH):
            din(f"wada{l}", [24, 128, 8 * 384]); din(f"bada{l}", [128, 72]); din(f"normw{l}", [128, 6, 8])
            for i in range(2):
                din(f"w1_{l}{i}", [NFC, 128, 2048]); din(f"w2_{l}{i}", [8, 128, NFC * 128])
            for nm, shp in (("wp", [38, 128, 1024]), ("wt", [128, 8 * 528]), ("wq", [128, 3 * 1024]),
                            ("wkn", [128, 2 * 512]), ("wkv", [128, 2 * 512]), ("qnw", [128, 3]), ("kvnw", [128, 2]),
                            ("wg", [8, 128, 3 * 8 * 128]), ("wb", [8, 128, 12 * 128]), ("wo", [8, 128, 1024])):
                din(f"{nm}{l}", shp)
        d["out"] = nc.dram_tensor("out", [4, 128, KC, NTOK], F32, kind="ExternalOutput").ap()
        dscr("hbuf", [4, 128, KC, NTOK])
        dscr("fm_seq", [4, NFM, 128, TSEQ])
        dscr("tm_ab_seq", [4, TSEQ, 4])
        dscr("tm_mv_seq", [4, TSEQ, 128])
        dscr("tm_hv_seq", [4, TSEQ, 128])
        dscr("yseq", [4, 3, 128, TSEQ])
        with ExitStack() as st:
            self.P = P = Prog(nc, st)
            P.arena = None
            P.init_psum(8)
            self.cf = P.sb("cf", [128, NCONST, 128], F32, persistent=True)
            self.cb = P.sb("cb", [128, NCONST, 128], BF16, persistent=True)
            self.b_c = Buf()
            P.dma(self.cf[:], d["consts"].rearrange("n p f -> p n f"), writes=[self.b_c])
            P.dma(self.cb[:], d["consts"].rearrange("n p f -> p n f"), writes=[self.b_c], eng="pool")
            self.smalls = P.sb("smalls", [128, 8, 32], F32, persistent=True)
            P.dma(self.smalls[:], d["small"].rearrange("n p f -> p n f"), writes=[self.b_c])
            Sper = [P.sb(f"Sper{l}", [128, 9, 8, 2], F32, persistent=True) for l in range(DEPTH)]
            bSper = [Buf() for _ in range(DEPTH)]
            self.A = Arena(P, 178)
            P.arena = self.A
            for l in range(DEPTH):
                P.barrier()
                self.A.reset()
                tp = TokenPhase.__new__(TokenPhase)
                tp.P = P
                tp.d = {"cT": d["cT"], "wadaM": d[f"wada{l}"], "badaM": d[f"bada{l}"], "normwM": d[f"normw{l}"]}
                S, bS = tp.compute_mod("M")
                P.copy(Sper[l][:], S, [bS], [bSper[l]])
            mods_l = [(Sper[l], bSper[l]) for l in range(DEPTH)]

            def tok_d(mode, j, lA, lB, src, dst):
                dd = {"hT": d[src][j], "hT_out": d[dst][j], "fm_seq": d["fm_seq"], "tm_ab_seq": d["tm_ab_seq"],
                      "tm_mv_seq": d["tm_mv_seq"], "tm_hv_seq": d["tm_hv_seq"], "yseq": d["yseq"],
                      "rope_cos": d["rope_cos"][j], "rope_sin": d["rope_sin"][j]}
                if lA is not None:
                    dd.update({"w1A": d[f"w1_{lA}1"], "w2A": d[f"w2_{lA}1"], "wg": d[f"wg{lA}"], "wb": d[f"wb{lA}"],
                               "wo": d[f"wo{lA}"]})
                if lB is not None:
                    dd.update({"w1B": d[f"w1_{lB}0"], "w2B": d[f"w2_{lB}0"]})
                    for nm in ("wp", "wt", "wq", "wkn", "wkv", "qnw", "kvnw"):
                        dd[nm] = d[f"{nm}{lB}"]
                return dd

            def token_phase(mode, lA, lB, src, dst):
                for j in range(4):
                    P.barrier()
                    self.A.reset()
                    mods = {}
                    if lA is not None:
                        mods["A"] = mods_l[lA]
                    if lB is not None:
                        mods["B"] = mods_l[lB]
                    TokenPhase(mode, ext=self, d=tok_d(mode, j, lA, lB, src, dst), mods=mods, shard=j)

            def mixer_phase(l):
                for hh in range(4):
                    md = {"fm": d["fm_seq"][hh], "tm_ab": d["tm_ab_seq"][hh], "tm_mv": d["tm_mv_seq"][hh],
                          "tm_hv": d["tm_hv_seq"][hh], "yT": d["yseq"][hh], "rmask": d["rmask"]}
                    MixerPhase(l, ext=self, d=md, small=self.smalls[:, l * 4 + hh, :])

            token_phase("first", None, 0, "hT", "hbuf")
            mixer_phase(0)
            token_phase("mid", 0, 1, "hbuf", "hbuf")
            mixer_phase(1)
            token_phase("last", 1, None, "hbuf", "out")
            P.barrier()
            P.emit()


def fused_inputs(inp, b):
    m = {"cT": _CONST["cT"][b], "consts": _CONST["consts"], "rmask": _CONST["rmask"]}
    h = []
    for j in range(4):
        h.append(tok_to_fm(np.concatenate([inp["ctx"][b, 64 * j:64 * j + 64], inp["x"][b, 2048 * j:2048 * (j + 1)]], 0)))
    m["hT"] = np.stack(h)
    m["small"] = np.stack([prep_mixer_small(inp, l, hh) for l in range(DEPTH) for hh in range(4)])
    m["rope_cos"] = np.stack([_CONST[("rope", j)][0] for j in range(4)])
    m["rope_sin"] = np.stack([_CONST[("rope", j)][1] for j in range(4)])
    for l in range(DEPTH):
        f0, f1 = _CONST[("ffn", l, 0)], _CONST[("ffn", l, 1)]
        m[f"wada{l}"], m[f"bada{l}"], m[f"normw{l}"] = f0["wada"], f0["bada"], f0["normw"]
        m[f"w1_{l}0"], m[f"w2_{l}0"] = f0["w1"], f0["w2"]
        m[f"w1_{l}1"], m[f"w2_{l}1"] = f1["w1"], f1["w2"]
        for k, v in _CONST[("proj", l)].items():
            m[f"{k}{l}"] = v
        for k, v in _CONST[("merge", l)].items():
            m[f"{k}{l}"] = v
    return m


def kernel(**inp):
    inp = {k: np.asarray(v) for k, v in inp.items()}
    _prepare(inp)
    if "fused" not in _PROGS:
        _PROGS["fused"] = Fused()
    fz = _PROGS["fused"]
    maps = [fused_inputs(inp, b) for b in range(NB)]
    res = run_bass_kernel_spmd(fz.nc, maps, core_ids=list(range(NB)))
    out = np.zeros((NB, SEQ, D), np.float32)
    for b in range(NB):
        o = res.results[b]["out"]
        for j in range(4):
            out[b, 2048 * j:2048 * (j + 1)] = fm_to_tok(o[j])[64:]
    return out
```

```python
import math
import numpy as np
from contextlib import ExitStack
import concourse.bass as bass
import concourse.mybir as mybir
from concourse.bass_utils import run_bass_kernel_spmd

F32 = mybir.dt.float32
BF16 = mybir.dt.bfloat16
AF = mybir.ActivationFunctionType
ALU = mybir.AluOpType

D = 1024
KC = 8
DFF = 2816
NFC = 22
DEPTH = 2
NB = 2
SEQ = 8192
CTX = 256
NCORE = 8
NTOK = 2112
TSEQ = CTX + SEQ
EPS = 1e-6
TILES = [(0, 64, 1), (64, 512, 0), (576, 512, 0), (1088, 512, 0), (1600, 512, 0)]
GROUPS = [[0, 1], [2], [3], [4]]
GOFF = [0, 576, 1088, 1600]
GW = [576, 512, 512, 512]
NG = 576
NFM = 11
N_DMA_SEM = 24
import os as _os
WENG = _os.environ.get('TOK_WENG', 'pool')


class Buf:
    __slots__ = ("name", "lw", "readers")

    def __init__(self, name=""):
        self.name = name
        self.lw = None
        self.readers = []


class Op:
    __slots__ = ("eng", "fn", "deps", "marked", "mark_no", "is_dma", "dsem", "dval", "dprev", "epoch", "sep")

    def __init__(self, eng, fn, is_dma):
        self.epoch = 0
        self.sep = 0
        self.eng = eng
        self.fn = fn
        self.deps = []
        self.marked = False
        self.mark_no = 0
        self.is_dma = is_dma
        self.dsem = None
        self.dval = 0
        self.dprev = None


class Prog:
    ENGS = ("sync", "act", "dve", "pool", "pe")

    def __init__(self, nc, stack):
        self.nc = nc
        self.stack = stack
        self.ops = {e: [] for e in self.ENGS}
        self.n_dma = 0
        self.dma_last = [None] * N_DMA_SEM
        self.dma_cnt = [0] * N_DMA_SEM
        self._uid = 0
        self.psum = []
        self.psum_i = 0
        self.rot = None
        self.epoch = 0
        self.sep = {e: 0 for e in self.ENGS}
        self.sep_start = {e: 0 for e in self.ENGS}

    def sb(self, name, shape, dtype=F32, persistent=False):
        if getattr(self, "arena", None) is not None and not persistent:
            return self.arena.bf16(list(shape)) if dtype == BF16 else self.arena.f32(list(shape))
        self._uid += 1
        return self.stack.enter_context(self.nc.sbuf_tensor(f"{name}_{self._uid}", list(shape), dtype))

    def ps(self, name, shape, dtype=F32):
        self._uid += 1
        return self.stack.enter_context(self.nc.psum_tensor(f"{name}_{self._uid}", list(shape), dtype))

    def init_psum(self, n=8):
        for i in range(n):
            self.psum.append((self.ps(f"bank{i}", [128, 512], F32), Buf(f"bank{i}")))

    def bank(self):
        rot = self.rot if self.rot is not None else list(range(len(self.psum)))
        t = self.psum[rot[self.psum_i % len(rot)]]
        self.psum_i += 1
        return t

    def barrier(self):
        lasts = []
        for e in self.ENGS:
            for op in reversed(self.ops[e]):
                if not op.is_dma and op.fn is not None:
                    lasts.append(op)
                    break
        lasts += [o for o in self.dma_last if o is not None]
        for e in self.ENGS:
            op = Op(e, None, False)
            op.epoch = self.epoch
            op.sep = self.sep[e]
            for d in lasts:
                op.deps.append(d)
                if not d.is_dma:
                    d.marked = True
            self.ops[e].append(op)
        self.epoch += 1
        for e in self.ENGS:
            n = sum(1 for o in self.ops[e][self.sep_start[e]:] if o.marked and not o.is_dma)
            if n > 6000:
                self.sep[e] += 1
                self.sep_start[e] = len(self.ops[e])

    def add(self, eng, fn, reads=(), writes=(), is_dma=False):
        op = Op(eng, fn, is_dma)
        op.epoch = self.epoch
        op.sep = self.sep[eng]
        deps = []
        for b in reads:
            if b.lw is not None:
                deps.append(b.lw)
        for b in writes:
            if b.lw is not None:
                deps.append(b.lw)
            deps.extend(b.readers)
        seen = set()
        for d in deps:
            if id(d) in seen or d.epoch < self.epoch:
                continue
            if eng == "pe" and d.eng == "pe" and not d.is_dma:
                continue
            seen.add(id(d))
            op.deps.append(d)
            if not d.is_dma:
                d.marked = True
        for b in reads:
            b.readers.append(op)
        for b in writes:
            b.lw = op
            b.readers = []
        if is_dma:
            s = self.n_dma % N_DMA_SEM
            self.n_dma += 1
            op.dsem = s
            op.dprev = self.dma_last[s]
            self.dma_cnt[s] += 16
            op.dval = self.dma_cnt[s]
            self.dma_last[s] = op
        self.ops[eng].append(op)
        return op

    def dma(self, out_ap, in_ap, reads=(), writes=(), eng="sync"):
        return self.add(eng, lambda e: e.dma_start(out=out_ap, in_=in_ap), reads, writes, True)

    def mm(self, out, lhsT, rhs, start, stop, reads, writes):
        return self.add("pe", lambda e: e.matmul(out, lhsT, rhs, start=start, stop=stop), reads, writes)

    def tr(self, out, in_, ident, reads, writes):
        return self.add("pe", lambda e: e.transpose(out, in_, ident), reads, writes)

    def act(self, out, in_, func, reads, writes, scale=None, bias=None, accum_out=None):
        kw = {}
        if scale is not None:
            kw["scale"] = scale
        if bias is not None:
            kw["bias"] = bias
        if accum_out is not None:
            kw["accum_out"] = accum_out
        return self.add("act", lambda e: e.activation(out=out, in_=in_, func=func, **kw), reads, writes)

    def tt(self, out, in0, in1, op, reads, writes, eng="dve"):
        return self.add(eng, lambda e: e.tensor_tensor(out=out, in0=in0, in1=in1, op=op), reads, writes)

    def ts(self, out, in0, s1, op0, reads, writes, s2=None, op1=None, eng="dve", accum_out=None):
        if op1 is None:
            return self.add(eng, lambda e: e.tensor_scalar(out=out, in0=in0, scalar1=s1, scalar2=None, op0=op0,
                                                            accum_out=accum_out), reads, writes)
        return self.add(eng, lambda e: e.tensor_scalar(out=out, in0=in0, scalar1=s1, scalar2=s2, op0=op0, op1=op1,
                                                        accum_out=accum_out), reads, writes)

    def stt(self, out, in0, scalar, in1, op0, op1, reads, writes):
        return self.add("dve", lambda e: e.scalar_tensor_tensor(out=out, in0=in0, scalar=scalar, in1=in1,
                                                                 op0=op0, op1=op1), reads, writes)

    def copy(self, out, in_, reads, writes, eng="dve"):
        if eng == "act":
            return self.add("act", lambda e: e.copy(out=out, in_=in_), reads, writes)
        return self.add(eng, lambda e: e.tensor_copy(out=out, in_=in_), reads, writes)

    def memset(self, ap, val, writes, eng="dve"):
        return self.add(eng, lambda e: e.memset(ap, val), (), writes)

    def emit(self):
        nc = self.nc
        st = self.stack
        esem = {(e, ep): st.enter_context(nc.semaphore(f"s_{e}_{ep}")) for e in self.ENGS
                for ep in range(self.sep[e] + 1)}
        dsem = [st.enter_context(nc.semaphore(f"s_dma{i}")) for i in range(N_DMA_SEM)]
        for e in self.ENGS:
            cnt = {}
            for op in self.ops[e]:
                if op.marked and not op.is_dma:
                    cnt[op.sep] = cnt.get(op.sep, 0) + 1
                    op.mark_no = cnt[op.sep]
        block = st.enter_context(nc.Block())
        handles = {"sync": block.sync, "act": block.scalar, "dve": block.vector,
                   "pool": block.gpsimd, "pe": block.tensor}
        final = [(dsem[i], self.dma_cnt[i]) for i in range(N_DMA_SEM) if self.dma_cnt[i] > 0]

        def make(ename):
            ops = self.ops[ename]

            def body(eng):
                waited = {}

                def wait(sem, key, val):
                    if waited.get(key, 0) >= val:
                        return
                    waited[key] = val
                    eng.wait_ge(sem, val)

                for op in ops:
                    for d in op.deps:
                        if d.is_dma:
                            wait(dsem[d.dsem], ("d", d.dsem), d.dval)
                        else:
                            wait(esem[(d.eng, d.sep)], ("e", d.eng, d.sep), d.mark_no)
                    if op.is_dma and op.dprev is not None:
                        wait(dsem[op.dsem], ("d", op.dsem), op.dprev.dval)
                    if op.fn is None:
                        continue
                    ins = op.fn(eng)
                    if op.is_dma:
                        ins.then_inc(dsem[op.dsem], 16)
                    elif op.marked:
                        ins.then_inc(esem[(ename, op.sep)], 1)
                if ename == "sync":
                    for sem, val in final:
                        eng.wait_ge(sem, val)
            return body

        for e in self.ENGS:
            handles[e](make(e))


class Slots:
    def __init__(self, P, name, shape, dtype, n):
        self.t = [P.sb(f"{name}{i}", shape, dtype) for i in range(n)]
        self.b = [Buf(f"{name}{i}") for i in range(n)]
        self.i = 0

    def next(self):
        k = self.i % len(self.t)
        self.i += 1
        return self.t[k], self.b[k]


class TokenPhase:
    def __init__(self, mode, ext=None, d=None, mods=None, shard=0):
        self.mode = mode
        self.ext = ext
        self.shard = shard
        self.do_merge = mode in ("mid", "last")
        self.do_proj = mode in ("first", "mid")
        if ext is not None:
            self.nc = ext.nc
            self.P = ext.P
            self.d = d
            self.mods = mods
            self.sets = [x for x in ("A", "B") if x in mods]
            self.build()
            return
        nc = self.nc = bass.Bass("TRN2", target_bir_lowering=False)
        dt = nc.dram_tensor
        self.d = {}

        import os
        wbf = os.environ.get("TOK_W_BF16") == "1"

        def din(name, shape):
            isw = wbf and name[:2] in ("w1", "w2", "wp", "wt", "wq", "wk", "wg", "wb", "wo", "wa")
            self.d[name] = dt(name, list(shape), BF16 if isw else F32, kind="ExternalInput").ap()

        def dout(name, shape, dtype=F32):
            self.d[name] = dt(name, list(shape), dtype, kind="ExternalOutput").ap()

        din("hT", [128, KC, NTOK])
        din("cT", [128, KC, 2])
        din("ones", [128, 128])
        dout("hT_out", [128, KC, NTOK])
        sets = []
        if self.do_merge:
            sets.append("A")
            din("yT", [128, 12, NTOK])
            for nm, shp in (("wg", [8, 128, 3 * 8 * 128]), ("wb", [8, 128, 12 * 128]), ("wo", [8, 128, 1024])):
                din(nm, shp)
        if self.do_proj:
            sets.append("B")
            for nm, shp in (("wp", [38, 128, 1024]), ("wt", [128, 8 * 528]), ("wq", [128, 3 * 1024]),
                            ("wkn", [128, 2 * 512]), ("wkv", [128, 2 * 512]), ("qnw", [128, 3]), ("kvnw", [128, 2]),
                            ("rope_cos", [64, NTOK]), ("rope_sin", [64, NTOK])):
                din(nm, shp)
            dout("fm", [4, NFM, 128, NTOK])
            dout("tm_ab", [NTOK, 16])
            dout("tm_mv", [NTOK, 512])
            dout("tm_hv", [NTOK, 512])
        for s in sets:
            din(f"wada{s}", [24, 128, 8 * 384])
            din(f"bada{s}", [128, 72])
            din(f"normw{s}", [128, 6, 8])
            din(f"w1{s}", [NFC, 128, 2048])
            din(f"w2{s}", [8, 128, NFC * 128])
        self.sets = sets
        with ExitStack() as st:
            self.P = P = Prog(nc, st)
            self.build()
            P.emit()

    def build(self):
        P = self.P
        d = self.d
        if self.ext is None:
            P.init_psum(8)
            self.ones = P.sb("ones", [128, 128], BF16)
            self.b_ones = Buf()
            P.dma(self.ones[:], d["ones"], writes=[self.b_ones], eng="pool")
        else:
            P.rot = None
            self.ones = self.ext.cb[:, C_ONES, :]
            self.b_ones = self.ext.b_c
        self.h = P.sb("h", [128, KC, NG], F32)
        self.hn = P.sb("hn", [128, KC, NG], BF16)
        self.a = P.sb("a", [128, NFC, NG], BF16)
        self.y = P.sb("y", [128, KC, NG], F32)
        self.bh = [Buf() for _ in range(2)]
        self.bhn = [Buf() for _ in range(2)]
        self.ba = [[Buf() for _ in range(2)] for _ in range(NFC)]
        self.by = [[Buf() for _ in range(2)] for _ in range(KC)]
        self.sq = Slots(P, "sq", [128, KC, 512], BF16, 1)
        self.tmp = Slots(P, "tmp", [128, 512], F32, 6)
        self.rstd = Slots(P, "rstd", [128, 512], F32, 2)
        self.lnt = Slots(P, "lnt", [128, 512], F32, 2)
        self.w1s = Slots(P, "w1s", [128, KC, 256], BF16, 2)
        self.w2s = Slots(P, "w2s", [128, NFC, 128], BF16, 2)
        self.wps = Slots(P, "wps", [128, KC, 128], BF16, 4)
        if self.ext is None:
            self.mods = {}
            for s in self.sets:
                self.mods[s] = self.compute_mod(s)
        if self.do_proj:
            self.setup_proj()
        if self.do_merge:
            self.setup_merge()
        for g in range(4):
            tiles = [TILES[i] for i in GROUPS[g]]
            go = GOFF[g]
            for (off, w, z) in tiles:
                ti = 0 if z == 1 else 1
                lo = 512 if z == 1 else 0
                P.dma(self.h[:, :, lo:lo + w], d["hT"][:, :, off:off + w], writes=[self.bh[ti]])
            if self.do_merge:
                self.merge(g, "A")
                self.ffn(g, "A", 1)
            if self.do_proj:
                self.ffn(g, "B", 0)
                self.proj(g, "B")
            for (off, w, z) in tiles:
                ti = 0 if z == 1 else 1
                lo = 512 if z == 1 else 0
                P.dma(d["hT_out"][:, :, off:off + w], self.h[:, :, lo:lo + w], reads=[self.bh[ti]])

    def seq_off(self, off):
        j = self.shard
        return 64 * j + off if off < 64 else 256 + 2048 * j + (off - 64)

    def fm_ap(self, hh, r, p0, p1, off, w):
        if self.ext is None:
            return self.d["fm"][hh, r, p0:p1, off:off + w]
        so = self.seq_off(off)
        return self.d["fm_seq"][hh, r, p0:p1, so:so + w]

    def tm_out(self, name, off, sw, st_, ncol, bs_):
        P = self.P
        if self.ext is None:
            P.dma(self.d[name][off:off + sw, :], st_[0:sw, 0:4 * ncol], reads=[bs_])
            return
        so = self.seq_off(off)
        for hh in range(4):
            P.dma(self.d[name + "_seq"][hh, so:so + sw, :], st_[0:sw, hh * ncol:(hh + 1) * ncol], reads=[bs_])

    def compute_mod(self, s):
        P = self.P
        d = self.d
        cT = P.sb("cT", [128, KC, 2], F32)
        b_c = Buf()
        P.dma(cT[:], d["cT"], writes=[b_c])
        sT = P.sb("sT", [128, KC, 2], BF16)
        b_s = Buf()
        P.act(sT[:], cT[:], AF.Silu, [b_c], [b_s])
        bada = P.sb("bada", [128, 72], F32)
        b_b = Buf()
        P.dma(bada[:], d[f"bada{s}"], writes=[b_b])
        nw = P.sb("nw", [128, 6, 8], F32)
        b_nw = Buf()
        P.dma(nw[:], d[f"normw{s}"], writes=[b_nw])
        M = P.sb("M", [128, 72, 2], F32)
        b_M = Buf()
        wsl = Slots(P, "wada", [128, KC, 384], BF16, 2)
        for sl in range(24):
            wt, bw = wsl.next()
            P.dma(wt[:], d[f"wada{s}"][sl].rearrange("p (k n) -> p k n", k=KC), writes=[bw], eng=WENG)
            pt, bp = P.bank()
            for ci in range(3):
                for k in range(KC):
                    P.mm(pt[:, 2 * ci:2 * ci + 2], wt[:, k, ci * 128:(ci + 1) * 128], sT[:, k, :],
                         k == 0, k == KC - 1, [bw, b_s], [bp])
            P.tt(M[:, sl * 3:(sl + 1) * 3, :], pt[:, 0:6].rearrange("p (c z) -> p c z", z=2),
                 bada[:, sl * 3:(sl + 1) * 3].unsqueeze(2).broadcast_to([128, 3, 2]), ALU.add, [bp, b_b], [b_M])
        S = P.sb("S", [128, 9, 8, 2], F32)
        b_S = Buf()

        def Mi(i):
            return M[:, i * 8:(i + 1) * 8, :]

        def nwb(i):
            return nw[:, i, :].unsqueeze(2).broadcast_to([128, 8, 2])
        one = P.sb("onep", [128, 8, 2], F32)
        b_one = Buf()
        for (k, nwi, sci) in ((0, 0, 1), (3, 2, 4), (6, 4, 7)):
            P.ts(one[:], Mi(sci), 1.0, ALU.add, [b_M], [b_one])
            P.tt(S[:, k], one[:], nwb(nwi), ALU.mult, [b_one, b_nw], [b_S])
        for (k, shi) in ((1, 0), (4, 3), (7, 6)):
            P.copy(S[:, k], Mi(shi), [b_M], [b_S])
        for (k, gi, nwi, f) in ((2, 2, 1, 0.5), (5, 5, 3, 1.0), (8, 8, 5, 0.5)):
            P.ts(one[:], Mi(gi), f, ALU.mult, [b_M], [b_one])
            P.tt(S[:, k], one[:], nwb(nwi), ALU.mult, [b_one, b_nw], [b_S])
        return S, b_S

    def rstd_of(self, src, b_src, nch, w, dn):
        P = self.P
        sq, bsq = self.sq.next()
        P.act(sq[:, 0:nch, 0:w], src, AF.Square, [b_src], [bsq])
        pt, bp = P.bank()
        for c in range(nch):
            P.mm(pt[:, 0:w], self.ones[:], sq[:, c, 0:w], c == 0, c == nch - 1, [bsq, self.b_ones], [bp])
        ln, bln = self.lnt.next()
        P.act(ln[:, 0:w], pt[:, 0:w], AF.Ln, [bp], [bln], scale=1.0 / dn, bias=EPS)
        r, br = self.rstd.next()
        P.act(r[:, 0:w], ln[:, 0:w], AF.Exp, [bln], [br], scale=-0.5)
        return r, br

    def norm_mod(self, g, s, kbase):
        P = self.P
        S, bS = self.mods[s]
        go = GOFF[g]
        for tix in GROUPS[g]:
            off, w, z = TILES[tix]
            ti = 0 if z == 1 else 1
            lo = 512 if z == 1 else 0
            r, br = self.rstd_of(self.h[:, :, lo:lo + w], self.bh[ti], KC, w, D)
            for c in range(KC):
                t, bt = self.tmp.next()
                P.tt(t[:, 0:w], self.h[:, c, lo:lo + w], r[:, 0:w], ALU.mult, [self.bh[ti], br], [bt])
                P.act(self.hn[:, c, lo:lo + w], t[:, 0:w], AF.Identity, [bt, bS], [self.bhn[ti]],
                      scale=S[:, kbase, c, z:z + 1], bias=S[:, kbase + 1, c, z:z + 1])

    def post_res(self, g, s, kgate):
        P = self.P
        S, bS = self.mods[s]
        go = GOFF[g]
        for tix in GROUPS[g]:
            off, w, z = TILES[tix]
            ti = 0 if z == 1 else 1
            lo = 512 if z == 1 else 0
            by_all = [self.by[c][ti] for c in range(KC)]
            bsrc = Buf()
            sq, bsq = self.sq.next()
            P.act(sq[:, :, 0:w], self.y[:, :, lo:lo + w], AF.Square, by_all, [bsq])
            pt, bp = P.bank()
            for c in range(KC):
                P.mm(pt[:, 0:w], self.ones[:], sq[:, c, 0:w], c == 0, c == KC - 1, [bsq, self.b_ones], [bp])
            ln, bln = self.lnt.next()
            P.act(ln[:, 0:w], pt[:, 0:w], AF.Ln, [bp], [bln], scale=1.0 / D, bias=EPS)
            r, br = self.rstd.next()
            P.act(r[:, 0:w], ln[:, 0:w], AF.Exp, [bln], [br], scale=-0.5)
            for c in range(KC):
                t, bt = self.tmp.next()
                P.tt(t[:, 0:w], self.y[:, c, lo:lo + w], r[:, 0:w], ALU.mult, [self.by[c][ti], br], [bt])
                P.stt(self.h[:, c, lo:lo + w], t[:, 0:w], S[:, kgate, c, z:z + 1], self.h[:, c, lo:lo + w],
                      ALU.mult, ALU.add, [bt, bS, self.bh[ti]], [self.bh[ti]])

    def ffn(self, g, s, i):
        P = self.P
        d = self.d
        kb = 0 if i == 0 else 6
        self.norm_mod(g, s, kb)
        go = GOFF[g]
        tl = [((0 if TILES[tix][2] == 1 else 1),) + TILES[tix] for tix in GROUPS[g]]
        w1 = d[f"w1{s}"]
        w2 = d[f"w2{s}"]
        for n in range(NFC):
            wt, bw = self.w1s.next()
            P.dma(wt[:], w1[n].rearrange("p (k n) -> p k n", k=KC), writes=[bw], eng=WENG)
            for (ti, off, w, z) in tl:
                lo = 512 if z == 1 else 0
                pg, bpg = P.bank()
                pu, bpu = P.bank()
                for k in range(KC):
                    P.mm(pg[:, 0:w], wt[:, k, 0:128], self.hn[:, k, lo:lo + w], k == 0, k == KC - 1,
                         [bw, self.bhn[ti]], [bpg])
                for k in range(KC):
                    P.mm(pu[:, 0:w], wt[:, k, 128:256], self.hn[:, k, lo:lo + w], k == 0, k == KC - 1,
                         [bw, self.bhn[ti]], [bpu])
                t, bt = self.tmp.next()
                P.act(t[:, 0:w], pg[:, 0:w], AF.Silu, [bpg], [bt])
                P.tt(self.a[:, n, lo:lo + w], t[:, 0:w], pu[:, 0:w], ALU.mult, [bt, bpu], [self.ba[n][ti]])
        for dc in range(KC):
            wt, bw = self.w2s.next()
            P.dma(wt[:], w2[dc].rearrange("p (k n) -> p k n", k=NFC), writes=[bw], eng=WENG)
            for (ti, off, w, z) in tl:
                lo = 512 if z == 1 else 0
                py, bpy = P.bank()
                for k in range(NFC):
                    P.mm(py[:, 0:w], wt[:, k, :], self.a[:, k, lo:lo + w], k == 0, k == NFC - 1,
                         [bw, self.ba[k][ti]], [bpy])
                P.copy(self.y[:, dc, lo:lo + w], py[:, 0:w], [bpy], [self.by[dc][ti]], eng="act")
        self.post_res(g, s, kb + 2)

    def setup_proj(self):
        P = self.P
        d = self.d
        self.wt = P.sb("wt", [128, KC, 528], BF16)
        self.b_wt = Buf()
        P.dma(self.wt[:], d["wt"].rearrange("p (k n) -> p k n", k=KC), writes=[self.b_wt], eng=WENG)
        self.wq = P.sb("wq", [128, 3, 1024], BF16)
        self.b_wq = Buf()
        P.dma(self.wq[:], d["wq"].rearrange("p (k n) -> p k n", k=3), writes=[self.b_wq], eng=WENG)
        self.wkn = P.sb("wkn", [128, 2, 512], BF16)
        self.b_wkn = Buf()
        P.dma(self.wkn[:], d["wkn"].rearrange("p (k n) -> p k n", k=2), writes=[self.b_wkn], eng=WENG)
        self.wkv = P.sb("wkv", [128, 2, 512], BF16)
        self.b_wkv = Buf()
        P.dma(self.wkv[:], d["wkv"].rearrange("p (k n) -> p k n", k=2), writes=[self.b_wkv], eng=WENG)
        self.qnw = P.sb("qnw", [128, 3], F32)
        self.kvnw = P.sb("kvnw", [128, 2], F32)
        self.b_nws = Buf()
        P.dma(self.qnw[:], d["qnw"], writes=[self.b_nws])
        P.dma(self.kvnw[:], d["kvnw"], writes=[self.b_nws])
        self.cos = P.sb("cos", [64, NG], F32)
        self.sin = P.sb("sin", [64, NG], F32)
        self.b_rope = Buf()
        self.qkva = self.y[:, 0:5, :]
        self.b_qkva = self.by[0:5]
        self.qkvn = self.a[:, 0:5, :]
        self.b_qkvn5 = self.ba[0:5]
        self.krr = self.y[0:64, 5:7, :]
        self.b_krr2 = self.by[5:7]
        self.stage = Slots(P, "stage", [128, 512], F32, 3)
        self.stage2 = self.stage

    def proj(self, g, s):
        P = self.P
        d = self.d
        self.norm_mod(g, s, 3)
        go = GOFF[g]
        gw = GW[g]
        tl = [((0 if TILES[tix][2] == 1 else 1),) + TILES[tix] for tix in GROUPS[g]]
        for (ti, off, w, z) in tl:
            lo = 512 if z == 1 else 0
            P.dma(self.cos[:, lo:lo + w], d["rope_cos"][:, off:off + w], writes=[self.b_rope])
            P.dma(self.sin[:, lo:lo + w], d["rope_sin"][:, off:off + w], writes=[self.b_rope])
        ev = 0
        for ci in range(38):
            wt, bw = self.wps.next()
            P.dma(wt[:], d["wp"][ci].rearrange("p (k n) -> p k n", k=KC), writes=[bw], eng=WENG)
            for (ti, off, w, z) in tl:
                lo = 512 if z == 1 else 0
                if ci < 37:
                    pt, bp = P.bank()
                    for k in range(KC):
                        P.mm(pt[:, 0:w], wt[:, k, :], self.hn[:, k, lo:lo + w], k == 0, k == KC - 1,
                             [bw, self.bhn[ti]], [bp])
                    if ci < 32:
                        hh, r = ci // 8, ci % 8
                        st_, bs_ = self.stage.next()
                        P.copy(st_[:, 0:w], pt[:, 0:w], [bp], [bs_], eng=("act" if ev % 2 == 0 else "dve"))
                        ev += 1
                        P.dma(self.fm_ap(hh, r, 0, 128, off, w), st_[:, 0:w], reads=[bs_])
                    else:
                        P.copy(self.qkva[:, ci - 32, lo:lo + w], pt[:, 0:w], [bp], [self.b_qkva[ci - 32][ti]],
                               eng=("act" if ev % 2 == 0 else "dve"))
                        ev += 1
                else:
                    for half in range(2):
                        pt, bp = P.bank()
                        for k in range(KC):
                            P.mm(pt[0:64, 0:w], wt[:, k, half * 64:(half + 1) * 64], self.hn[:, k, lo:lo + w],
                                 k == 0, k == KC - 1, [bw, self.bhn[ti]], [bp])
                        P.copy(self.krr[:, half, lo:lo + w], pt[0:64, 0:w], [bp], [self.b_krr2[half][ti]], eng="act")
        for (ti, off, w, z) in tl:
            lo = 512 if z == 1 else 0
            t1, b1 = self.stage2.next()
            t2, b2 = self.stage2.next()
            P.tt(t1[0:64, 0:w], self.krr[:, 0, lo:lo + w], self.cos[:, lo:lo + w], ALU.mult,
                 [self.b_krr2[0][ti], self.b_rope], [b1])
            P.tt(t2[0:64, 0:w], self.krr[:, 1, lo:lo + w], self.sin[:, lo:lo + w], ALU.mult,
                 [self.b_krr2[1][ti], self.b_rope], [b2])
            P.tt(t1[0:64, 0:w], t1[0:64, 0:w], t2[0:64, 0:w], ALU.add, [b1, b2], [b1])
            for hh in range(4):
                P.dma(self.fm_ap(hh, 10, 64, 128, off, w), t1[0:64, 0:w], reads=[b1])
        for (ti, off, w, z) in tl:
            lo = 512 if z == 1 else 0
            for (c0, nch, dn, nwt) in ((0, 3, 384, self.qnw), (3, 2, 256, self.kvnw)):
                bsrc = [self.b_qkva[c0 + c][ti] for c in range(nch)]
                sq, bsq = self.sq.next()
                P.act(sq[:, 0:nch, 0:w], self.qkva[:, c0:c0 + nch, lo:lo + w], AF.Square, bsrc, [bsq])
                pt, bp = P.bank()
                for c in range(nch):
                    P.mm(pt[:, 0:w], self.ones[:], sq[:, c, 0:w], c == 0, c == nch - 1, [bsq, self.b_ones], [bp])
                ln, bln = self.lnt.next()
                P.act(ln[:, 0:w], pt[:, 0:w], AF.Ln, [bp], [bln], scale=1.0 / dn, bias=EPS)
                r, br = self.rstd.next()
                P.act(r[:, 0:w], ln[:, 0:w], AF.Exp, [bln], [br], scale=-0.5)
                for c in range(nch):
                    t, bt = self.tmp.next()
                    P.tt(t[:, 0:w], self.qkva[:, c0 + c, lo:lo + w], r[:, 0:w], ALU.mult,
                         [self.b_qkva[c0 + c][ti], br], [bt])
                    P.act(self.qkvn[:, c0 + c, lo:lo + w], t[:, 0:w], AF.Identity, [bt, self.b_nws],
                          [self.b_qkvn5[c0 + c][ti]], scale=nwt[:, c:c + 1])
        for hh in range(4):
            for (ti, off, w, z) in tl:
                lo = 512 if z == 1 else 0
                pt, bp = P.bank()
                for k in range(3):
                    P.mm(pt[:, 0:w], self.wq[:, k, hh * 256:hh * 256 + 128], self.qkvn[:, k, lo:lo + w],
                         k == 0, k == 2, [self.b_wq, self.b_qkvn5[k][ti]], [bp])
                st_, bs_ = self.stage.next()
                P.copy(st_[:, 0:w], pt[:, 0:w], [bp], [bs_], eng="act")
                P.dma(self.fm_ap(hh, 8, 0, 128, off, w), st_[:, 0:w], reads=[bs_])
                pt, bp = P.bank()
                for k in range(2):
                    P.mm(pt[:, 0:w], self.wkn[:, k, hh * 128:(hh + 1) * 128], self.qkvn[:, 3 + k, lo:lo + w],
                         k == 0, k == 1, [self.b_wkn, self.b_qkvn5[3 + k][ti]], [bp])
                st_, bs_ = self.stage.next()
                P.copy(st_[:, 0:w], pt[:, 0:w], [bp], [bs_], eng="dve")
                P.dma(self.fm_ap(hh, 9, 0, 128, off, w), st_[:, 0:w], reads=[bs_])
                pr, bpr = P.bank()
                for k in range(3):
                    P.mm(pr[0:64, 0:w], self.wq[:, k, hh * 256 + 128:hh * 256 + 192], self.qkvn[:, k, lo:lo + w],
                         k == 0, k == 2, [self.b_wq, self.b_qkvn5[k][ti]], [bpr])
                ps_, bps = P.bank()
                for k in range(3):
                    P.mm(ps_[0:64, 0:w], self.wq[:, k, hh * 256 + 192:hh * 256 + 256], self.qkvn[:, k, lo:lo + w],
                         k == 0, k == 2, [self.b_wq, self.b_qkvn5[k][ti]], [bps])
                t1, b1 = self.stage2.next()
                t2, b2 = self.stage2.next()
                P.tt(t1[0:64, 0:w], pr[0:64, 0:w], self.cos[:, lo:lo + w], ALU.mult, [bpr, self.b_rope], [b1])
                P.tt(t2[0:64, 0:w], ps_[0:64, 0:w], self.sin[:, lo:lo + w], ALU.mult, [bps, self.b_rope], [b2])
                P.tt(t1[0:64, 0:w], t1[0:64, 0:w], t2[0:64, 0:w], ALU.add, [b1, b2], [b1])
                P.dma(self.fm_ap(hh, 10, 0, 64, off, w), t1[0:64, 0:w], reads=[b1])
        for (ti, off, w, z) in tl:
            lo = 512 if z == 1 else 0
            for s0 in range(0, w, 128):
                sw = min(128, w - s0)
                a0 = lo + s0
                pt, bp = P.bank()
                for k in range(KC):
                    P.mm(pt[0:sw, :], self.hn[:, k, a0:a0 + sw], self.wt[:, k, 0:512], k == 0, k == KC - 1,
                         [self.b_wt, self.bhn[ti]], [bp])
                st_, bs_ = self.stage.next()
                P.copy(st_[0:sw, :], pt[0:sw, :], [bp], [bs_], eng="act")
                self.tm_out("tm_hv", off + s0, sw, st_, 128, bs_)
                pt, bp = P.bank()
                for k in range(KC):
                    P.mm(pt[0:sw, 0:16], self.hn[:, k, a0:a0 + sw], self.wt[:, k, 512:528], k == 0, k == KC - 1,
                         [self.b_wt, self.bhn[ti]], [bp])
                st_, bs_ = self.stage.next()
                P.copy(st_[0:sw, 0:16], pt[0:sw, 0:16], [bp], [bs_], eng="dve")
                self.tm_out("tm_ab", off + s0, sw, st_, 4, bs_)
                pt, bp = P.bank()
                for k in range(2):
                    P.mm(pt[0:sw, :], self.qkvn[:, 3 + k, a0:a0 + sw], self.wkv[:, k, :], k == 0, k == 1,
                         [self.b_wkv, self.b_qkvn5[3 + k][ti]], [bp])
                st_, bs_ = self.stage.next()
                P.copy(st_[0:sw, :], pt[0:sw, :], [bp], [bs_], eng="act")
                self.tm_out("tm_mv", off + s0, sw, st_, 128, bs_)

    def setup_merge(self):
        P = self.P
        self.yT = self.a[:, 0:12, :]
        self.b_yTq = self.ba[0:12]
        self.mT = self.a[:, 12:20, :]
        self.b_mT = self.ba[12:20]
        self.wbs = Slots(P, "wbs", [128, 12, 128], BF16, 2)
        self.sg = self.tmp
        self.macc = Slots(P, "macc", [128, 512], F32, 2)

    def merge(self, g, s):
        P = self.P
        d = self.d
        S, bS = self.mods[s]
        go = GOFF[g]
        tl = [((0 if TILES[tix][2] == 1 else 1),) + TILES[tix] for tix in GROUPS[g]]
        for (ti, off, w, z) in tl:
            lo = 512 if z == 1 else 0
            if self.ext is None:
                P.dma(self.yT[:, :, lo:lo + w], d["yT"][:, :, off:off + w],
                      writes=[self.b_yTq[q][ti] for q in range(12)], eng="pool")
            else:
                so = self.seq_off(off)
                for bi in range(3):
                    for hh in range(4):
                        P.dma(self.yT[:, bi * 4 + hh, lo:lo + w], d["yseq"][hh, bi, :, so:so + w],
                              writes=[self.b_yTq[bi * 4 + hh][ti]], eng="pool")
        self.norm_mod(g, s, 3)
        for c in range(KC):
            wgl = []
            for j in range(3):
                wg, bwg = self.wps.next()
                P.dma(wg[:], d["wg"][c][:, j * 1024:(j + 1) * 1024].rearrange("p (k n) -> p k n", k=KC),
                      writes=[bwg], eng=WENG)
                wgl.append((wg, bwg))
            wb, bwb = self.wbs.next()
            P.dma(wb[:], d["wb"][c].rearrange("p (q n) -> p q n", q=12), writes=[bwb], eng=WENG)
            for (ti, off, w, z) in tl:
                lo = 512 if z == 1 else 0
                acc, bacc = self.macc.next()
                for j in range(3):
                    pg, bpg = P.bank()
                    wg, bwg = wgl[j]
                    for k in range(KC):
                        P.mm(pg[:, 0:w], wg[:, k, :], self.hn[:, k, lo:lo + w], k == 0, k == KC - 1,
                             [bwg, self.bhn[ti]], [bpg])
                    pb, bpb = P.bank()
                    for q in range(4):
                        P.mm(pb[:, 0:w], wb[:, j * 4 + q, :], self.yT[:, j * 4 + q, lo:lo + w], q == 0, q == 3,
                             [bwb, self.b_yTq[j * 4 + q][ti]], [bpb])
                    sg, bsg = self.sg.next()
                    P.act(sg[:, 0:w], pg[:, 0:w], AF.Sigmoid, [bpg], [bsg])
                    if j == 0:
                        P.tt(acc[:, 0:w], sg[:, 0:w], pb[:, 0:w], ALU.mult, [bsg, bpb], [bacc])
                    else:
                        P.tt(sg[:, 0:w], sg[:, 0:w], pb[:, 0:w], ALU.mult, [bsg, bpb], [bsg])
                        if j == 1:
                            P.tt(acc[:, 0:w], acc[:, 0:w], sg[:, 0:w], ALU.add, [bacc, bsg], [bacc])
                        else:
                            P.tt(self.mT[:, c, lo:lo + w], acc[:, 0:w], sg[:, 0:w], ALU.add, [bacc, bsg],
                                 [self.b_mT[c][ti]])
        for c in range(KC):
            wo, bwo = self.wps.next()
            P.dma(wo[:], d["wo"][c].rearrange("p (k n) -> p k n", k=KC), writes=[bwo], eng=WENG)
            for (ti, off, w, z) in tl:
                lo = 512 if z == 1 else 0
                py, bpy = P.bank()
                for k in range(KC):
                    P.mm(py[:, 0:w], wo[:, k, :], self.mT[:, k, lo:lo + w], k == 0, k == KC - 1,
                         [bwo, self.b_mT[k][ti]], [bpy])
                P.copy(self.y[:, c, lo:lo + w], py[:, 0:w], [bpy], [self.by[c][ti]], eng="act")
        self.post_res(g, s, 5)


def _fm(v):
    return np.ascontiguousarray(v.reshape(-1, 128).T)


def _wtile(w, cols):
    K = w.shape[0]
    sub = w[:, cols].reshape(K // 128, 128, len(cols))
    return np.ascontiguousarray(sub.transpose(1, 0, 2).reshape(128, -1))


def _partner(j):
    return j + 16 if (j % 32) < 16 else j - 16


def prep_ffn_set(inp, l, i):
    out = {}
    wa = inp["w_ada"][l]
    out["wada"] = np.stack([_wtile(wa, np.arange(sl * 384, (sl + 1) * 384)) for sl in range(24)])
    out["bada"] = _fm(inp["b_ada"][l])
    out["normw"] = np.ascontiguousarray(inp["norm_w"][l].reshape(6, 8, 128).transpose(2, 0, 1))
    w1 = inp["ffn_w_in"][l, i]
    out["w1"] = np.stack([_wtile(w1, np.concatenate([np.arange(n * 128, (n + 1) * 128),
                                                     DFF + np.arange(n * 128, (n + 1) * 128)])) for n in range(NFC)])
    w2 = inp["ffn_w_out"][l, i]
    out["w2"] = np.stack([_wtile(w2, np.arange(dc * 128, (dc + 1) * 128)) for dc in range(8)])
    return out


def prep_proj(inp, l):
    out = {}
    w_in = inp["w_in"][l]
    chunks = []
    for hh in range(4):
        for base in (0, 512, 1024, 1536, 2768, 3280, 3792, 4816):
            chunks.append(np.arange(base + hh * 128, base + (hh + 1) * 128))
    for c in range(3):
        chunks.append(np.arange(2064 + c * 128, 2064 + (c + 1) * 128))
    for c in range(2):
        chunks.append(np.arange(2448 + c * 128, 2448 + (c + 1) * 128))
    kr = 2704 + np.arange(64)
    chunks.append(np.concatenate([kr, 2704 + np.array([_partner(j) for j in range(64)])]))
    out["wp"] = np.stack([_wtile(w_in, c) for c in chunks])
    abc = np.array([[2048 + hh, 2052 + hh, 2056 + hh, 2060 + hh] for hh in range(4)]).reshape(-1)
    out["wt"] = _wtile(w_in, np.concatenate([4304 + np.arange(512), abc]))
    wq = inp["mla_w_q_b"][l]
    qc = []
    for hh in range(4):
        qc += list(hh * 192 + np.arange(128)) + list(hh * 192 + 128 + np.arange(64)) + \
            [hh * 192 + 128 + _partner(j) for j in range(64)]
    out["wq"] = _wtile(wq, np.array(qc))
    wkv = inp["mla_w_kv_b"][l]
    out["wkn"] = _wtile(wkv, np.concatenate([hh * 256 + np.arange(128) for hh in range(4)]))
    out["wkv"] = _wtile(wkv, np.concatenate([hh * 256 + 128 + np.arange(128) for hh in range(4)]))
    out["qnw"] = _fm(inp["mla_q_norm"][l])
    out["kvnw"] = _fm(inp["mla_kv_norm"][l])
    return out


def prep_merge(inp, l):
    out = {}
    w_in = inp["w_in"][l]
    out["wg"] = np.stack([np.concatenate([_wtile(w_in, 5328 + j * 1024 + c * 128 + np.arange(128)) for j in range(3)],
                                         axis=1) for c in range(8)])
    wb = inp["w_branch"][l]
    out["wb"] = np.stack([np.concatenate([_wtile(wb[j], c * 128 + np.arange(128)) for j in range(3)], axis=1)
                          for c in range(8)])
    out["wo"] = np.stack([_wtile(inp["w_out"][l], c * 128 + np.arange(128)) for c in range(8)])
    return out


def rope_tables(shard):
    cos = np.ones((64, NTOK), np.float64)
    sin = np.zeros((64, NTOK), np.float64)
    t = shard * 2048 + np.arange(2048)
    row = (t // 64).astype(np.float64)
    col = (t % 64).astype(np.float64)
    inv = (10000.0 ** (-np.arange(16, dtype=np.float32) / np.float32(16))).astype(np.float32).astype(np.float64)
    for dd in range(64):
        a_, s_, f_ = dd // 32, (dd % 32) // 16, dd % 16
        pos = row if a_ == 0 else col
        ang = (pos.astype(np.float32) * inv[f_].astype(np.float32)).astype(np.float64)
        cos[dd, 64:] = np.cos(ang)
        sin[dd, 64:] = np.sin(ang) * (-1.0 if s_ == 0 else 1.0)
    return cos.astype(np.float32), sin.astype(np.float32)


def tok_to_fm(h):
    return np.ascontiguousarray(h.reshape(NTOK, 8, 128).transpose(2, 1, 0))


def fm_to_tok(hT):
    return np.ascontiguousarray(hT.transpose(2, 1, 0).reshape(NTOK, 1024))


class Arena:
    def __init__(self, P, kb=192):
        self.cap = kb * 256
        self.t = P.sb("arena", [128, self.cap], F32)
        self.off = 0

    def reset(self):
        self.off = 0

    def _shape(self, ap, shape):
        if len(shape) == 3:
            return ap.rearrange("p (a b) -> p a b", a=shape[1])
        if len(shape) == 4:
            return ap.rearrange("p (a b c) -> p a b c", a=shape[1], b=shape[2])
        return ap

    def f32(self, shape):
        n = int(np.prod(shape[1:]))
        o = self.off
        self.off += n
        assert self.off <= self.cap, ("arena overflow", self.off, self.cap)
        return self._shape(self.t[0:shape[0], o:o + n], shape)

    def bf16(self, shape):
        n = int(np.prod(shape[1:]))
        nw = (n + 1) // 2
        o = self.off
        self.off += nw
        assert self.off <= self.cap, ("arena overflow", self.off, self.cap)
        ap = self.t[0:shape[0], o:o + nw].bitcast(BF16)[:, 0:n]
        return self._shape(ap, shape)


class ASlots:
    def __init__(self, aps):
        self.t = aps
        self.b = [Buf() for _ in aps]
        self.i = 0

    def next(self):
        k = self.i % len(self.t)
        self.i += 1
        return self.t[k], self.b[k]


NBLK = TSEQ // 128
MLA_SCALE = 192 ** -0.5
C_ONES, C_IDENT, C_HM0, C_HM1, C_NEG0, C_NEG1, C_STR0, C_STR1, C_TRI0, C_TRI1, C_ROWM = range(11)
C_LVL = 11
NCONST = 25


def make_consts():
    c = np.zeros((NCONST, 128, 128), np.float32)
    i = np.arange(128)
    c[C_ONES] = 1.0
    c[C_IDENT] = np.eye(128)
    J, I = np.meshgrid(i, i, indexing="ij")
    same32 = (J // 32) == (I // 32)
    c[C_HM0] = (same32 & (I >= J))
    c[C_HM1] = (same32 & (I <= J))
    c[C_NEG0] = np.where(I >= J, 0.0, -30000.0)
    c[C_NEG1] = np.where(I <= J, 0.0, -30000.0)
    c[C_STR0] = (I > J)
    c[C_STR1] = (I < J)
    c[C_TRI0] = (J <= I)
    c[C_TRI1] = (J >= I)
    for cc in range(4):
        c[C_ROWM][:, cc] = (i // 32 == cc)
    for d in range(2):
        for l in range(7):
            b = 2 ** l
            Ii, Jj = J, I
            sameblk = (Ii // (2 * b)) == (Jj // (2 * b))
            ih = (Ii // b) % 2
            jh = (Jj // b) % 2
            if d == 0:
                m = sameblk & (ih == 1) & (jh == 0)
            else:
                m = sameblk & (ih == 0) & (jh == 1)
            c[C_LVL + d * 7 + l] = -1.0 * m
    return c


class MixerPhase:
    def __init__(self, layer, parts=("gdn", "mla", "hg"), stage=99, ext=None, d=None, small=None):
        self.layer = layer
        self.stage = stage
        if ext is not None:
            self.nc, self.P, self.d, self.A = ext.nc, ext.P, d, ext.A
            self.cf, self.cb, self.b_c, self.small = ext.cf, ext.cb, ext.b_c, small
            self.mla()
            self.hgrn2()
            self.gdn()
            return
        nc = self.nc = bass.Bass("TRN2", target_bir_lowering=False)
        self.d = {}

        def din(name, shape):
            self.d[name] = nc.dram_tensor(name, list(shape), F32, kind="ExternalInput").ap()

        din("fm", [NFM, 128, TSEQ])
        din("tm_ab", [TSEQ, 4])
        din("tm_mv", [TSEQ, 128])
        din("tm_hv", [TSEQ, 128])
        din("consts", [NCONST, 128, 128])
        din("rmask", [128, 1408])
        din("small", [128, 32])
        self.d["yT"] = nc.dram_tensor("yT", [3, 128, TSEQ], F32, kind="ExternalOutput").ap()
        with ExitStack() as st:
            self.P = P = Prog(nc, st)
            P.init_psum(8)
            self.cf = P.sb("cf", [128, NCONST, 128], F32)
            self.cb = P.sb("cb", [128, NCONST, 128], BF16)
            self.b_c = Buf()
            P.dma(self.cf[:], self.d["consts"].rearrange("n p f -> p n f"), writes=[self.b_c])
            P.dma(self.cb[:], self.d["consts"].rearrange("n p f -> p n f"), writes=[self.b_c], eng="pool")
            self.small = P.sb("small", [128, 32], F32)
            P.dma(self.small[:], self.d["small"], writes=[self.b_c])
            self.A = Arena(P, 178)
            if "mla" in parts:
                self.mla()
            if "hg" in parts:
                self.hgrn2()
            if "gdn" in parts:
                self.gdn()
            P.emit()

    def mla(self):
        P, A, d = self.P, self.A, self.d
        P.barrier()
        A.reset()
        P.rot = [0, 1, 2, 3]
        T = TSEQ
        fm = d["fm"]
        qn = A.bf16([128, T]); kn = A.bf16([128, T]); qr = A.bf16([64, T]); kr = A.bf16([64, T])
        v = A.bf16([128, NBLK, 128])
        bq, bk, bv = Buf(), Buf(), Buf()
        for c0 in range(0, T, 2112):
            P.dma(kn[:, c0:c0 + 2112], fm[9][:, c0:c0 + 2112], writes=[bk], eng="pool")
            P.dma(kr[:, c0:c0 + 2112], fm[10][64:128, c0:c0 + 2112], writes=[bk], eng="pool")
            P.dma(qn[:, c0:c0 + 2112], fm[8][:, c0:c0 + 2112], writes=[bq], eng="pool")
            P.dma(qr[:, c0:c0 + 2112], fm[10][0:64, c0:c0 + 2112], writes=[bq], eng="pool")
        for n0 in range(0, NBLK, 22):
            P.dma(v[:, n0:n0 + 22, :], d["tm_mv"][n0 * 128:(n0 + 22) * 128, :].rearrange("(n p) d -> p n d", p=128),
                  writes=[bv], eng="pool")
        pts = ASlots([A.bf16([128, 512]) for _ in range(4)])
        rcs = ASlots([A.f32([128, 512]) for _ in range(2)])
        yos = ASlots([A.f32([128, 512]) for _ in range(2)])
        ones = self.cb[:, C_ONES, :]
        blocks = [(0, 256, 2)] + [(256 + 512 * b, 512, NBLK) for b in range(16)]
        for bi, (q0, qw, nk) in enumerate(blocks):
            O, bO = P.psum[4 + bi % 2]
            L, bL = P.psum[6 + bi % 2]
            LOOK = 2
            pend = []

            def scores(kc):
                S, bS = P.bank()
                P.mm(S[:, 0:qw], kn[:, kc * 128:(kc + 1) * 128], qn[:, q0:q0 + qw], True, False, [bk, bq], [bS])
                P.mm(S[:, 0:qw], kr[:, kc * 128:(kc + 1) * 128], qr[:, q0:q0 + qw], False, True, [bk, bq], [bS])
                pt, bpt = pts.next()
                P.act(pt[:, 0:qw], S[:, 0:qw], AF.Exp, [bS], [bpt], scale=MLA_SCALE)
                pend.append((kc, pt, bpt))

            def consume():
                kc, pt, bpt = pend.pop(0)
                P.mm(O[:, 0:qw], v[:, kc, :], pt[:, 0:qw], kc == 0, kc == nk - 1, [bv, bpt], [bO])
                P.mm(L[:, 0:qw], ones, pt[:, 0:qw], kc == 0, kc == nk - 1, [self.b_c, bpt], [bL])

            for kc in range(nk):
                scores(kc)
                if len(pend) > LOOK:
                    consume()
            while pend:
                consume()
            rc, brc = rcs.next()
            P.add("dve", (lambda o_, i_: (lambda e: e.reciprocal(out=o_, in_=i_)))(rc[:, 0:qw], L[:, 0:qw]), [bL], [brc])
            yo, byo = yos.next()
            P.tt(yo[:, 0:qw], O[:, 0:qw], rc[:, 0:qw], ALU.mult, [bO, brc], [byo])
            P.dma(d["yT"][1, :, q0:q0 + qw], yo[:, 0:qw], reads=[byo])

    def readout(self, of, b_of_all, gate_row, nw_ap, out_idx, scr):
        P, d = self.P, self.d
        W = 384
        g_s, sq_s, ln_s, r_s, t_s = scr
        ones = self.cb[:, C_ONES, :]
        for t0 in range(0, TSEQ, W):
            g, bg = g_s.next()
            P.dma(g[:, 0:W], d["fm"][gate_row][:, t0:t0 + W], writes=[bg])
            P.act(g[:, 0:W], g[:, 0:W], AF.Silu, [bg], [bg])
            sq, bsq = sq_s.next()
            P.act(sq[:, 0:W], of[:, t0:t0 + W], AF.Square, b_of_all, [bsq])
            pt, bp = P.bank()
            P.mm(pt[:, 0:W], ones, sq[:, 0:W], True, True, [self.b_c, bsq], [bp])
            ln, bln = ln_s.next()
            P.act(ln[:, 0:W], pt[:, 0:W], AF.Ln, [bp], [bln], scale=1.0 / 128, bias=EPS)
            r, br = r_s.next()
            P.act(r[:, 0:W], ln[:, 0:W], AF.Exp, [bln], [br], scale=-0.5)
            t, bt = t_s.next()
            P.tt(t[:, 0:W], of[:, t0:t0 + W], r[:, 0:W], ALU.mult, b_of_all + [br], [bt])
            P.stt(t[:, 0:W], t[:, 0:W], nw_ap, g[:, 0:W], ALU.mult, ALU.mult, [bt, bg, self.b_c], [bt])
            P.dma(d["yT"][out_idx, :, t0:t0 + W], t[:, 0:W], reads=[bt])

    def hgrn2(self):
        P, A, d = self.P, self.A, self.d
        P.barrier()
        A.reset()
        P.rot = [0, 1, 2, 3]
        T = TSEQ
        SEG = 1408
        NSEG = 6
        fm = d["fm"]
        sm = self.small
        v128 = A.bf16([128, NBLK, 128]); b_v = Buf()
        for n0 in range(0, NBLK, 22):
            P.dma(v128[:, n0:n0 + 22, :], d["tm_hv"][n0 * 128:(n0 + 22) * 128, :].rearrange("(n p) d -> p n d", p=128),
                  writes=[b_v], eng="pool")
        of = A.f32([128, T]); b_of = [Buf() for _ in range(NBLK)]
        qt = A.bf16([128, T]); kt = A.bf16([128, T]); kh = A.bf16([128, T])
        b_qt, b_kt, b_kh = [Buf() for _ in range(6)], [Buf() for _ in range(6)], [Buf() for _ in range(6)]
        khtm = A.bf16([128, NBLK, 128]); b_khtm = [Buf() for _ in range(NBLK)]
        dec = A.f32([128, 264]); b_dec = [Buf() for _ in range(6)]
        rmask = A.f32([128, SEG]); b_rm = Buf()
        P.dma(rmask, d["rmask"], writes=[b_rm])
        sc = [A.f32([128, SEG]) for _ in range(6)]
        bsc = [Buf() for _ in range(6)]
        sA, sB, sC, sD, sT, sQ = sc
        bA, bB, bC, bD, bT, bQ = bsc
        S = [A.f32([128, 128]) for _ in range(2)]; bS = [Buf(), Buf()]
        Sb = [A.bf16([128, 128]) for _ in range(2)]; bSb = [Buf(), Buf()]
        ams = ASlots([A.bf16([128, 128]) for _ in range(3)])
        kms = ASlots([A.bf16([128, 128]) for _ in range(4)])
        lbt = A.f32([128, 8]); b_lb = Buf()
        for dd in range(2):
            if self.layer == 0:
                P.memset(lbt[:, dd:dd + 1], 0.0, [b_lb])
            else:
                P.tt(lbt[:, dd:dd + 1], sm[:, 2 + dd:3 + dd], sm[:, dd:dd + 1], ALU.subtract, [self.b_c], [b_lb])
                P.act(lbt[:, dd:dd + 1], lbt[:, dd:dd + 1], AF.Sigmoid, [b_lb], [b_lb])
            P.ts(lbt[:, 2 + dd:3 + dd], lbt[:, dd:dd + 1], -1.0, ALU.mult, [b_lb], [b_lb], s2=1.0, op1=ALU.add)
            P.ts(lbt[:, 4 + dd:5 + dd], lbt[:, dd:dd + 1], 1.0, ALU.subtract, [b_lb], [b_lb])
        identb = self.cb[:, C_IDENT, :]

        def v3(ap):
            return ap.rearrange("p (c w) -> p c w", w=32)

        for dd in range(2):
            lb, oml, noml = lbt[:, dd:dd + 1], lbt[:, 2 + dd:3 + dd], lbt[:, 4 + dd:5 + dd]
            for seg in range(NSEG):
                s0 = seg * SEG
                P.dma(sA, fm[5 + dd][:, s0:s0 + SEG], writes=[bA])
                P.dma(sQ, fm[4][:, s0:s0 + SEG], writes=[bQ])
                P.act(sA, sA, AF.Sigmoid, [bA], [bA])
                P.ts(sB, sA, noml, ALU.mult, [bA, b_lb], [bB], s2=oml, op1=ALU.add)
                P.ts(sA, sA, oml, ALU.mult, [bA, b_lb], [bA], s2=lb, op1=ALU.add)
                P.act(sA, sA, AF.Ln, [bA], [bA])
                P.add("dve", lambda e: e.tensor_tensor_scan(out=sC, data0=rmask, data1=sA, initial=0.0,
                                                             op0=ALU.mult, op1=ALU.add), [bA, b_rm], [bC])
                totb = v3(sC)[:, :, 31:32].broadcast_to([128, SEG // 32, 32])
                if dd == 0:
                    G, bG = sC, bC
                else:
                    P.tt(sD, sA, sC, ALU.subtract, [bA, bC], [bD])
                    P.tt(v3(sD), v3(sD), totb, ALU.add, [bD, bC], [bD])
                    G, bG = sD, bD
                P.act(dec[:, seg * 44:(seg + 1) * 44], v3(sC)[:, :, 31], AF.Exp, [bC], [b_dec[seg]])
                P.tt(v3(sT), totb, v3(G), ALU.subtract, [bC, bG], [bT])
                P.act(sT, sT, AF.Exp, [bT], [bT])
                P.tt(kh[:, s0:s0 + SEG], sB, sT, ALU.mult, [bB, bT], [b_kh[seg]])
                P.act(sT, G, AF.Exp, [bG], [bT], scale=-1.0)
                P.tt(kt[:, s0:s0 + SEG], sB, sT, ALU.mult, [bB, bT], [b_kt[seg]])
                P.act(sT, G, AF.Exp, [bG], [bT])
                P.stt(qt[:, s0:s0 + SEG], sQ, 128 ** -0.5, sT, ALU.mult, ALU.mult, [bQ, bT], [b_qt[seg]])
            if self.stage < 1:
                P.dma(d["yT"][2, :, 0:SEG], sT, reads=[bT])
                return
            for blk in range(NBLK):
                ptr, bptr = P.bank()
                P.mm(ptr[:, 0:128], kh[:, blk * 128:(blk + 1) * 128], identb, True, True,
                     [b_kh[blk // 11], self.b_c], [bptr])
                P.copy(khtm[:, blk, :], ptr[:, 0:128], [bptr], [b_khtm[blk]],
                       eng=("act" if blk % 2 == 0 else "dve"))
            if self.stage < 2:
                P.dma(d["yT"][2, :, 0:SEG], sT, reads=[bT] + b_khtm)
                return
            P.memset(S[0], 0.0, [bS[0]])
            P.memset(Sb[0], 0.0, [bSb[0]], eng="pool")
            n = 0
            hm = self.cf[:, C_HM0 + dd, :]
            if dd == 0:
                order = list(range(NBLK)); corder = [0, 1, 2, 3]
            else:
                order = [1, 0] + list(range(NBLK - 1, 1, -1)); corder = [3, 2, 1, 0]
            for bi, blk in enumerate(order):
                sg = blk // 11
                c0 = blk * 128
                pa, bpa = P.bank()
                P.mm(pa[:, 0:128], kt[:, c0:c0 + 128], qt[:, c0:c0 + 128], True, True, [b_kt[sg], b_qt[sg]], [bpa])
                am, bam = ams.next()
                P.tt(am, pa[:, 0:128], hm, ALU.mult, [bpa, self.b_c], [bam])
                po, bpo = P.psum[4 + bi % 2]
                P.mm(po[:, 0:128], v128[:, blk, :], am, True, False, [b_v, bam], [bpo])
                for ci, cc in enumerate(corder):
                    km, bkm = kms.next()
                    P.ts(km, khtm[:, blk, :], self.cf[:, C_ROWM, cc:cc + 1], ALU.mult, [b_khtm[blk], self.b_c], [bkm],
                         eng="pool")
                    P.mm(po[:, cc * 32:(cc + 1) * 32], Sb[n % 2], qt[:, c0 + cc * 32:c0 + (cc + 1) * 32], False, ci == 3,
                         [bSb[n % 2], b_qt[sg]], [bpo])
                    pk, bpk = P.bank()
                    P.mm(pk[:, 0:128], km, v128[:, blk, :], True, True, [bkm, b_v], [bpk])
                    P.stt(S[(n + 1) % 2], S[n % 2], dec[:, blk * 4 + cc:blk * 4 + cc + 1], pk[:, 0:128], ALU.mult, ALU.add,
                          [bS[n % 2], bpk, b_dec[sg]], [bS[(n + 1) % 2]])
                    P.copy(Sb[(n + 1) % 2], S[(n + 1) % 2], [bS[(n + 1) % 2]], [bSb[(n + 1) % 2]], eng="act")
                    n += 1
                if dd == 0:
                    P.copy(of[:, c0:c0 + 128], po[:, 0:128], [bpo], [b_of[blk]], eng="act")
                else:
                    P.tt(of[:, c0:c0 + 128], of[:, c0:c0 + 128], po[:, 0:128], ALU.add, [b_of[blk], bpo], [b_of[blk]])
        W = 384
        scr = (ASlots([sA[:, 0:W], sA[:, W:2 * W]]), ASlots([kt[:, 0:W], kt[:, W:2 * W]]),
               ASlots([sB[:, 0:W], sB[:, W:2 * W]]), ASlots([sC[:, 0:W], sC[:, W:2 * W]]),
               ASlots([sD[:, 0:W], sD[:, W:2 * W]]))
        P.barrier()
        self.readout(of, b_of, 7, sm[:, 4:5], 2, scr)


def prep_mixer_small(inp, l, hh):
    sm = np.zeros((128, 32), np.float32)
    ch = hh * 128 + np.arange(128)
    lbl = inp["hg_lb_logits"]
    sm[:, 0] = lbl[0, 0, ch]; sm[:, 1] = lbl[0, 1, ch]
    sm[:, 2] = lbl[1, 0, ch]; sm[:, 3] = lbl[1, 1, ch]
    sm[:, 4] = inp["hg_norm"][l]
    sm[:, 5] = inp["gdn_norm"][l]
    sm[:, 6] = inp["gdn_a_log"][l, 0, hh]; sm[:, 7] = inp["gdn_a_log"][l, 1, hh]
    sm[:, 8] = inp["gdn_dt_bias"][l, 0, hh]; sm[:, 9] = inp["gdn_dt_bias"][l, 1, hh]
    cw = inp["gdn_conv"][l]
    for ti in range(3):
        for tap in range(3):
            sm[:, 10 + ti * 3 + tap] = cw[tap, ti * 512 + ch]
    return sm


def token_core_inputs(inp, mode, core, h_tok, y_tok=None):
    b, j = core // 4, core % 4
    m = {"hT": tok_to_fm(h_tok), "ones": _CONST["ones"], "cT": _CONST["cT"][b]}
    if mode in ("mid", "last"):
        l = 0 if mode == "mid" else 1
        for k, v in _CONST[("ffn", l, 1)].items():
            m[k + "A"] = v
        m.update(_CONST[("merge", l)])
        m["yT"] = np.ascontiguousarray(y_tok.reshape(3, NTOK, 4, 128).transpose(3, 0, 2, 1).reshape(128, 12, NTOK))
    if mode in ("first", "mid"):
        l = 0 if mode == "first" else 1
        for k, v in _CONST[("ffn", l, 0)].items():
            m[k + "B"] = v
        m.update(_CONST[("proj", l)])
        m["rope_cos"], m["rope_sin"] = _CONST[("rope", j)]
    return m


_CONST = {}
_PROGS = {}


def _prepare(inp):
    _CONST.clear()
    _CONST["ones"] = np.ones((128, 128), np.float32)
    _CONST["cT"] = [np.ascontiguousarray(np.stack([_fm(inp["c"][b]), _fm(inp["c_ctx"])], -1)) for b in range(NB)]
    for l in range(DEPTH):
        for i in range(2):
            _CONST[("ffn", l, i)] = prep_ffn_set(inp, l, i)
        _CONST[("proj", l)] = prep_proj(inp, l)
        _CONST[("merge", l)] = prep_merge(inp, l)
    for j in range(4):
        _CONST[("rope", j)] = rope_tables(j)
    _CONST["consts"] = make_consts()
    rm = np.ones((128, 1408), np.float32)
    rm[:, ::32] = 0.0
    _CONST["rmask"] = rm


def _prog(key):
    if key not in _PROGS:
        if key[0] == "tok":
            _PROGS[key] = TokenPhase(key[1])
        else:
            _PROGS[key] = MixerPhase(key[1])
    return _PROGS[key]


def run_token(inp, mode, h_cores, y_cores=None):
    tp = _prog(("tok", mode))
    maps = [token_core_inputs(inp, mode, c, h_cores[c], None if y_cores is None else y_cores[c]) for c in range(NCORE)]
    res = run_bass_kernel_spmd(tp.nc, maps, core_ids=list(range(NCORE)))
    return res.results


def _seq_order(parts_ctx, parts_lat, axis):
    return np.concatenate(parts_ctx + parts_lat, axis=axis)


def mixer_inputs(inp, l, tok_res):
    maps = []
    for core in range(NCORE):
        b, hh = core // 4, core % 4
        rs = [tok_res[b * 4 + j] for j in range(4)]
        fm = _seq_order([r["fm"][hh][:, :, 0:64] for r in rs], [r["fm"][hh][:, :, 64:] for r in rs], 2)
        ab = _seq_order([r["tm_ab"][0:64] for r in rs], [r["tm_ab"][64:] for r in rs], 0)
        ab = np.ascontiguousarray(ab[:, hh * 4:(hh + 1) * 4])
        mv = _seq_order([r["tm_mv"][0:64, hh * 128:(hh + 1) * 128] for r in rs],
                        [r["tm_mv"][64:, hh * 128:(hh + 1) * 128] for r in rs], 0)
        hv = _seq_order([r["tm_hv"][0:64, hh * 128:(hh + 1) * 128] for r in rs],
                        [r["tm_hv"][64:, hh * 128:(hh + 1) * 128] for r in rs], 0)
        maps.append({"fm": np.ascontiguousarray(fm), "tm_ab": ab, "tm_mv": np.ascontiguousarray(mv),
                     "tm_hv": np.ascontiguousarray(hv), "consts": _CONST["consts"], "rmask": _CONST["rmask"],
                     "small": prep_mixer_small(inp, l, hh)})
    return maps


def run_mixer(inp, l, tok_res):
    mp = _prog(("mix", l))
    res = run_bass_kernel_spmd(mp.nc, mixer_inputs(inp, l, tok_res), core_ids=list(range(NCORE)))
    return res.results


def y_for_token_cores(mix_res):
    out = []
    for core in range(NCORE):
        b, j = core // 4, core % 4
        y = np.zeros((3, NTOK, 512), np.float32)
        for hh in range(4):
            yT = mix_res[b * 4 + hh]["yT"]
            ctx = yT[:, :, 64 * j:64 * j + 64]
            lat = yT[:, :, 256 + 2048 * j:256 + 2048 * (j + 1)]
            y[:, :, hh * 128:(hh + 1) * 128] = np.concatenate([ctx, lat], 2).transpose(0, 2, 1)
        out.append(y)
    return out


def _gdn(self):
    P, A, d = self.P, self.A, self.d
    P.barrier()
    A.reset()
    P.rot = [0, 1, 2, 3]
    T = TSEQ
    SEG = 1408
    fm = d["fm"]
    sm = self.small
    cf, cb = self.cf, self.cb
    b_c = self.b_c
    onesb, identb = cb[:, C_ONES, :], cb[:, C_IDENT, :]
    onesf, identf = cf[:, C_ONES, :], cf[:, C_IDENT, :]
    qT = A.bf16([128, T]); kT = A.bf16([128, T])
    b_qT = [Buf() for _ in range(6)]; b_kT = [Buf() for _ in range(6)]
    k_tm = A.bf16([128, NBLK, 128]); v_tm = A.bf16([128, NBLK, 128])
    b_ktm = [Buf() for _ in range(NBLK)]; b_vtm = [Buf() for _ in range(NBLK)]
    of = A.f32([128, T]); b_of = [Buf() for _ in range(NBLK)]
    nsm = A.f32([128, 32]); b_nsm = Buf()
    P.ts(nsm, sm[:, :], -1.0, ALU.mult, [b_c], [b_nsm])
    Xs = ASlots([A.f32([128, SEG + 2]) for _ in range(2)])
    Ys = ASlots([A.f32([128, SEG]) for _ in range(2)])
    vTs = ASlots([A.bf16([128, SEG]) for _ in range(2)])
    sqs = ASlots([A.bf16([128, 352]) for _ in range(2)])
    lns = ASlots([A.f32([128, 352]) for _ in range(2)])
    rs = ASlots([A.f32([128, 352]) for _ in range(2)])
    for ti in range(3):
        for seg in range(6):
            s0 = seg * SEG
            X, bX = Xs.next()
            lo = max(s0 - 1, 0)
            hi = min(s0 + SEG + 1, T)
            if seg == 0:
                P.memset(X[:, 0:1], 0.0, [bX])
            if seg == 5:
                P.memset(X[:, SEG + 1:SEG + 2], 0.0, [bX])
            P.dma(X[:, lo - (s0 - 1):hi - (s0 - 1)], fm[ti][:, lo:hi], writes=[bX])
            Y, bY = Ys.next()
            wc = 10 + ti * 3
            P.ts(Y, X[:, 1:SEG + 1], sm[:, wc + 1:wc + 2], ALU.mult, [bX, b_c], [bY])
            P.stt(Y, X[:, 0:SEG], sm[:, wc:wc + 1], Y, ALU.mult, ALU.add, [bX, b_c, bY], [bY])
            P.stt(Y, X[:, 2:SEG + 2], sm[:, wc + 2:wc + 3], Y, ALU.mult, ALU.add, [bX, b_c, bY], [bY])
            if seg == 0:
                P.stt(Y[:, 255:256], X[:, 257:258], nsm[:, wc + 2:wc + 3], Y[:, 255:256], ALU.mult, ALU.add,
                      [bX, b_nsm, bY], [bY])
                P.stt(Y[:, 256:257], X[:, 256:257], nsm[:, wc:wc + 1], Y[:, 256:257], ALU.mult, ALU.add,
                      [bX, b_nsm, bY], [bY])
            P.act(Y, Y, AF.Silu, [bY], [bY])
            if ti < 2:
                dst, bdst = (qT, b_qT) if ti == 0 else (kT, b_kT)
                for t0 in range(0, SEG, 352):
                    sq, bsq = sqs.next()
                    P.act(sq, Y[:, t0:t0 + 352], AF.Square, [bY], [bsq])
                    pt, bp = P.bank()
                    P.mm(pt[:, 0:352], onesb, sq, True, True, [b_c, bsq], [bp])
                    ln, bln = lns.next()
                    P.act(ln, pt[:, 0:352], AF.Ln, [bp], [bln], bias=EPS)
                    r, br = rs.next()
                    P.act(r, ln, AF.Exp, [bln], [br], scale=-0.5, bias=(math.log(128 ** -0.5) if ti == 0 else 0.0))
                    P.tt(dst[:, s0 + t0:s0 + t0 + 352], Y[:, t0:t0 + 352], r, ALU.mult, [bY, br], [bdst[seg]])
                if ti == 1:
                    for bb in range(11):
                        blk = seg * 11 + bb
                        ptr, bptr = P.bank()
                        P.mm(ptr[:, 0:128], kT[:, blk * 128:(blk + 1) * 128], identb, True, True, [b_kT[seg], b_c], [bptr])
                        P.copy(k_tm[:, blk, :], ptr[:, 0:128], [bptr], [b_ktm[blk]], eng=("act" if bb % 2 else "dve"))
            else:
                vT, bvT = vTs.next()
                P.copy(vT, Y, [bY], [bvT], eng="pool")
                for bb in range(11):
                    blk = seg * 11 + bb
                    ptr, bptr = P.bank()
                    P.mm(ptr[:, 0:128], vT[:, bb * 128:(bb + 1) * 128], identb, True, True, [bvT, b_c], [bptr])
                    P.copy(v_tm[:, blk, :], ptr[:, 0:128], [bptr], [b_vtm[blk]], eng=("act" if bb % 2 else "dve"))
    ab = A.f32([128, NBLK, 4]); b_ab = Buf()
    P.dma(ab, d["tm_ab"].rearrange("(n p) c -> p n c", p=128), writes=[b_ab])
    cols = {}
    for dd in range(2):
        g = A.f32([128, NBLK]); bg = Buf()
        tmpc = A.f32([128, NBLK]); btmp = Buf()
        nA = A.f32([128, 1]); bnA = Buf()
        P.act(tmpc, ab[:, :, dd], AF.Exp, [b_ab, b_c], [btmp], bias=sm[:, 8 + dd:9 + dd])
        P.act(tmpc, tmpc, AF.Ln, [btmp], [btmp], bias=1.0)
        P.act(nA, sm[:, 6 + dd:7 + dd], AF.Exp, [b_c], [bnA])
        P.ts(nA, nA, -1.0, ALU.mult, [bnA], [bnA])
        P.ts(g, tmpc, nA, ALU.mult, [btmp, bnA], [bg])
        beta = A.f32([128, NBLK]); bbeta = Buf()
        P.act(beta, ab[:, :, 2 + dd], AF.Sigmoid, [b_ab], [bbeta])
        gcum = A.f32([128, NBLK]); negg = A.f32([128, NBLK]); negeg = A.f32([128, NBLK])
        ekl = A.f32([128, NBLK]); decS = A.f32([128, NBLK]); bcol = Buf()
        pt, bp = P.bank()
        P.mm(pt[:, 0:NBLK], cf[:, C_TRI0 + dd, :], g, True, True, [b_c, bg], [bp])
        P.copy(gcum, pt[:, 0:NBLK], [bp], [bcol])
        P.ts(negg, gcum, -1.0, ALU.mult, [bcol], [bcol])
        P.act(negeg, gcum, AF.Exp, [bcol], [bcol])
        P.ts(negeg, negeg, -1.0, ALU.mult, [bcol], [bcol])
        pt2, bp2 = P.bank()
        P.mm(pt2[:, 0:NBLK], onesf, g, True, True, [b_c, bg], [bp2])
        P.act(decS, pt2[:, 0:NBLK], AF.Exp, [bp2], [bcol])
        P.tt(ekl, pt2[:, 0:NBLK], gcum, ALU.subtract, [bp2, bcol], [bcol])
        P.act(ekl, ekl, AF.Exp, [bcol], [bcol])
        cols[dd] = dict(gcum=gcum, negg=negg, negeg=negeg, ekl=ekl, decS=decS, beta=beta, b=[bcol, bbeta])
    def mk(n, dt, cnt):
        return ASlots([(A.f32([128, 128]) if dt == "f" else A.bf16([128, 128])) for _ in range(cnt)])
    st = {}
    for dd in range(2):
        st[dd] = dict(
            dg=mk(0, "f", 2), Dm=mk(0, "f", 2), ET=mk(0, "f", 2), EGB=mk(0, "f", 2), ETs=mk(0, "f", 2),
            attnT=mk(0, "b", 3), N=mk(0, "b", 2), qg=mk(0, "b", 3), khat=mk(0, "b", 3), NB=mk(0, "b", 2),
            Tm=mk(0, "b", 3), Um=mk(0, "b", 3), Ufin=mk(0, "b", 3), R0=mk(0, "b", 2), vnew=mk(0, "b", 2),
            S=[A.f32([128, 128]) for _ in range(2)], bS=[Buf(), Buf()],
            Sb=[A.bf16([128, 128]) for _ in range(2)], bSb=[Buf(), Buf()], n=0, pend=None)
        P.memset(st[dd]["S"][0], 0.0, [st[dd]["bS"][0]])
        P.memset(st[dd]["Sb"][0], 0.0, [st[dd]["bSb"][0]], eng="pool")
    orders = {0: list(range(NBLK)), 1: [1, 0] + list(range(NBLK - 1, 1, -1))}
    written = set()

    def offchain(dd, blk):
        s = st[dd]
        cl = cols[dd]
        bcl = cl["b"]
        sg = blk // 11
        c0 = blk * 128
        kTb, qTb = kT[:, c0:c0 + 128], qT[:, c0:c0 + 128]
        pKK, bKK = P.bank()
        P.mm(pKK[:, 0:128], kTb, kTb, True, True, [b_kT[sg]], [bKK])
        pQK, bQK = P.bank()
        P.mm(pQK[:, 0:128], kTb, qTb, True, True, [b_kT[sg], b_qT[sg]], [bQK])
        dg, bdg = s["dg"].next()
        P.ts(dg, identf, cl["gcum"][:, blk:blk + 1], ALU.mult, [b_c] + bcl, [bdg])
        pG, bG = P.bank()
        P.mm(pG[:, 0:128], onesf, dg, True, True, [b_c, bdg], [bG])
        Dm, bDm = s["Dm"].next()
        P.stt(Dm, pG[:, 0:128], cl["negg"][:, blk:blk + 1], cf[:, C_NEG0 + dd, :], ALU.add, ALU.add, [bG, b_c] + bcl, [bDm])
        ET, bET = s["ET"].next()
        P.act(ET, Dm, AF.Exp, [bDm], [bET])
        EGB, bEGB = s["EGB"].next()
        P.act(EGB, pG[:, 0:128], AF.Exp, [bG], [bEGB])
        ETs, bETs = s["ETs"].next()
        P.tt(ETs, ET, cf[:, C_STR0 + dd, :], ALU.mult, [bET, b_c], [bETs], eng="pool")
        attnT, battn = s["attnT"].next()
        P.tt(attnT, pQK[:, 0:128], ET, ALU.mult, [bQK, bET], [battn])
        N, bN = s["N"].next()
        P.stt(N, pKK[:, 0:128], cl["beta"][:, blk:blk + 1], ETs, ALU.mult, ALU.mult, [bKK, bETs] + bcl, [bN])
        qg, bqg = s["qg"].next()
        P.tt(qg, qTb, EGB, ALU.mult, [b_qT[sg], bEGB], [bqg], eng="pool")
        khat, bkhat = s["khat"].next()
        P.ts(khat, k_tm[:, blk, :], cl["ekl"][:, blk:blk + 1], ALU.mult, [b_ktm[blk]] + bcl, [bkhat], eng="pool")
        Tc, bTc = identb, b_c
        Uc, bUc = identb, b_c
        for l in range(7):
            pB, bB = P.bank()
            P.mm(pB[:, 0:128], N, Tc, True, True, [bN, bTc], [bB])
            NB, bNB = s["NB"].next()
            P.tt(NB, pB[:, 0:128], cf[:, C_LVL + dd * 7 + l, :], ALU.mult, [bB, b_c], [bNB])
            if l < 6:
                pT, bT_ = P.bank()
                P.mm(pT[:, 0:128], Uc, NB, True, False, [bUc, bNB], [bT_])
                P.mm(pT[:, 0:128], identb, Tc, False, True, [b_c, bTc], [bT_])
                Tn, bTn = s["Tm"].next()
                P.copy(Tn, pT[:, 0:128], [bT_], [bTn], eng="act")
            pU, bU_ = P.bank()
            P.mm(pU[:, 0:128], NB, Uc, True, False, [bNB, bUc], [bU_])
            P.mm(pU[:, 0:128], identb, Uc, False, True, [b_c, bUc], [bU_])
            Un, bUn = (s["Um"] if l < 6 else s["Ufin"]).next()
            P.copy(Un, pU[:, 0:128], [bU_], [bUn], eng="act")
            if l < 6:
                Tc, bTc = Tn, bTn
            Uc, bUc = Un, bUn
        s["pend"] = dict(blk=blk, U=Uc, bU=bUc, attnT=attnT, battn=battn, qg=qg, bqg=bqg, khat=khat, bkhat=bkhat)

    def inchain(dd, step):
        s = st[dd]
        pd = s["pend"]
        if pd is None:
            return
        s["pend"] = None
        cl = cols[dd]
        bcl = cl["b"]
        blk = pd["blk"]
        sg = blk // 11
        c0 = blk * 128
        n = s["n"]
        cur, nxt = n % 2, (n + 1) % 2
        pkS, bkS = P.bank()
        P.mm(pkS[:, 0:128], kT[:, c0:c0 + 128], s["Sb"][cur], True, True, [b_kT[sg], s["bSb"][cur]], [bkS])
        R0, bR0 = s["R0"].next()
        P.stt(R0, pkS[:, 0:128], cl["negeg"][:, blk:blk + 1], v_tm[:, blk, :], ALU.mult, ALU.add,
              [bkS, b_vtm[blk]] + bcl, [bR0])
        pV, bV = P.bank()
        P.mm(pV[:, 0:128], pd["U"], R0, True, True, [pd["bU"], bR0], [bV])
        vnew, bvn = s["vnew"].next()
        P.act(vnew, pV[:, 0:128], AF.Identity, [bV] + bcl, [bvn], scale=cl["beta"][:, blk:blk + 1])
        po, bpo = P.psum[4 + 2 * dd + step % 2]
        P.mm(po[:, 0:128], s["Sb"][cur], pd["qg"], True, False, [s["bSb"][cur], pd["bqg"]], [bpo])
        P.mm(po[:, 0:128], vnew, pd["attnT"], False, True, [bvn, pd["battn"]], [bpo])
        pKV, bKV = P.bank()
        P.mm(pKV[:, 0:128], pd["khat"], vnew, True, True, [pd["bkhat"], bvn], [bKV])
        P.stt(s["S"][nxt], s["S"][cur], cl["decS"][:, blk:blk + 1], pKV[:, 0:128], ALU.mult, ALU.add,
              [s["bS"][cur], bKV] + bcl, [s["bS"][nxt]])
        P.copy(s["Sb"][nxt], s["S"][nxt], [s["bS"][nxt]], [s["bSb"][nxt]], eng="act")
        if blk not in written:
            written.add(blk)
            P.copy(of[:, c0:c0 + 128], po[:, 0:128], [bpo], [b_of[blk]], eng="dve")
        else:
            P.tt(of[:, c0:c0 + 128], of[:, c0:c0 + 128], po[:, 0:128], ALU.add, [b_of[blk], bpo], [b_of[blk]])
        s["n"] = n + 1

    dirs = [0, 1]
    if self.stage == 11:
        dirs = [0]
    if self.stage == 12:
        dirs = [1]
    for step in range(NBLK + 1):
        for dd in dirs:
            inch = st[dd]["pend"]
            if step < NBLK:
                prev = st[dd]["pend"]
                st[dd]["pend"] = None
                offchain(dd, orders[dd][step])
                newp = st[dd]["pend"]
                st[dd]["pend"] = prev
                inchain(dd, step)
                st[dd]["pend"] = newp
            else:
                inchain(dd, step)
    P.barrier()
    if self.stage in (10, 11, 12):
        for c0 in range(0, T, 2112):
            P.dma(d["yT"][0, :, c0:c0 + 2112], of[:, c0:c0 + 2112], reads=b_of)
        return
    W = 384
    scrA, scrC, scrD, scrE = Xs.t[0], Ys.t[0], Xs.t[1], Ys.t[1]
    scrB = vTs.t[0]
    scr = tuple(ASlots([x[:, 0:W], x[:, W:2 * W]]) for x in (scrA, scrB, scrC, scrD, scrE))
    self.readout(of, b_of, 3, sm[:, 5:6], 0, scr)


MixerPhase.gdn = _gdn


class Fused:
    def __init__(self):
        nc = self.nc = bass.Bass("TRN2", target_bir_lowering=False)
        self.d = d = {}

        def din(name, shape):
            d[name] = nc.dram_tensor(name, list(shape), F32, kind="ExternalInput").ap()

        def dscr(name, shape, dtype=F32):
            d[name] = nc.dram_tensor(name, list(shape), dtype, kind="Internal").ap()

        din("hT", [4, 128, KC, NTOK])
        din("cT", [128, KC, 2])
        din("consts", [NCONST, 128, 128])
        din("rmask", [128, 1408])
        din("small", [8, 128, 32])
        din("rope_cos", [4, 64, NTOK])
        din("rope_sin", [4, 64, NTOK])
        for l in range(DEPTH):
            din(f"wada{l}", [24, 128, 8 * 384]); din(f"bada{l}", [128, 72]); din(f"normw{l}", [128, 6, 8])
            for i in range(2):
                din(f"w1_{l}{i}", [NFC, 128, 2048]); din(f"w2_{l}{i}", [8, 128, NFC * 128])
            for nm, shp in (("wp", [38, 128, 1024]), ("wt", [128, 8 * 528]), ("wq", [128, 3 * 1024]),
                            ("wkn", [128, 2 * 512]), ("wkv", [128, 2 * 512]), ("qnw", [128, 3]), ("kvnw", [128, 2]),
                            ("wg", [8, 128, 3 * 8 * 128]), ("wb", [8, 128, 12 * 128]), ("wo", [8, 128, 1024])):
                din(f"{nm}{l}", shp)
        d["out"] = nc.dram_tensor("out", [4, 128, KC, NTOK], F32, kind="ExternalOutput").ap()
        dscr("hbuf", [4, 128, KC, NTOK])
        dscr("fm_seq", [4, NFM, 128, TSEQ])
        dscr("tm_ab_seq", [4, TSEQ, 4])
        dscr("tm_mv_seq", [4, TSEQ, 128])
        dscr("tm_hv_seq", [4, TSEQ, 128])
        dscr("yseq", [4, 3, 128, TSEQ])
        with ExitStack() as st:
            self.P = P = Prog(nc, st)
            P.arena = None
            P.init_psum(8)
            self.cf = P.sb("cf", [128, NCONST, 128], F32, persistent=True)
            self.cb = P.sb("cb", [128, NCONST, 128], BF16, persistent=True)
            self.b_c = Buf()
            P.dma(self.cf[:], d["consts"].rearrange("n p f -> p n f"), writes=[self.b_c])
            P.dma(self.cb[:], d["consts"].rearrange("n p f -> p n f"), writes=[self.b_c], eng="pool")
            self.smalls = P.sb("smalls", [128, 8, 32], F32, persistent=True)
            P.dma(self.smalls[:], d["small"].rearrange("n p f -> p n f"), writes=[self.b_c])
            Sper = [P.sb(f"Sper{l}", [128, 9, 8, 2], F32, persistent=True) for l in range(DEPTH)]
            bSper = [Buf() for _ in range(DEPTH)]
            self.A = Arena(P, 178)
            P.arena = self.A
            for l in range(DEPTH):
                P.barrier()
                self.A.reset()
                tp = TokenPhase.__new__(TokenPhase)
                tp.P = P
                tp.d = {"cT": d["cT"], "wadaM": d[f"wada{l}"], "badaM": d[f"bada{l}"], "normwM": d[f"normw{l}"]}
                S, bS = tp.compute_mod("M")
                P.copy(Sper[l][:], S, [bS], [bSper[l]])
            mods_l = [(Sper[l], bSper[l]) for l in range(DEPTH)]

            def tok_d(mode, j, lA, lB, src, dst):
                dd = {"hT": d[src][j], "hT_out": d[dst][j], "fm_seq": d["fm_seq"], "tm_ab_seq": d["tm_ab_seq"],
                      "tm_mv_seq": d["tm_mv_seq"], "tm_hv_seq": d["tm_hv_seq"], "yseq": d["yseq"],
                      "rope_cos": d["rope_cos"][j], "rope_sin": d["rope_sin"][j]}
                if lA is not None:
                    dd.update({"w1A": d[f"w1_{lA}1"], "w2A": d[f"w2_{lA}1"], "wg": d[f"wg{lA}"], "wb": d[f"wb{lA}"],
                               "wo": d[f"wo{lA}"]})
                if lB is not None:
                    dd.update({"w1B": d[f"w1_{lB}0"], "w2B": d[f"w2_{lB}0"]})
                    for nm in ("wp", "wt", "wq", "wkn", "wkv", "qnw", "kvnw"):
                        dd[nm] = d[f"{nm}{lB}"]
                return dd

            def token_phase(mode, lA, lB, src, dst):
                for j in range(4):
                    P.barrier()
                    self.A.reset()
                    mods = {}
                    if lA is not None:
                        mods["A"] = mods_l[lA]
                    if lB is not None:
                        mods["B"] = mods_l[lB]
                    TokenPhase(mode, ext=self, d=tok_d(mode, j, lA, lB, src, dst), mods=mods, shard=j)

            def mixer_phase(l):
                for hh in range(4):
                    md = {"fm": d["fm_seq"][hh], "tm_ab": d["tm_ab_seq"][hh], "tm_mv": d["tm_mv_seq"][hh],
                          "tm_hv": d["tm_hv_seq"][hh], "yT": d["yseq"][hh], "rmask": d["rmask"]}
                    MixerPhase(l, ext=self, d=md, small=self.smalls[:, l * 4 + hh, :])

            token_phase("first", None, 0, "hT", "hbuf")
            mixer_phase(0)
            token_phase("mid", 0, 1, "hbuf", "hbuf")
            mixer_phase(1)
            token_phase("last", 1, None, "hbuf", "out")
            P.barrier()
            P.emit()


def fused_inputs(inp, b):
    m = {"cT": _CONST["cT"][b], "consts": _CONST["consts"], "rmask": _CONST["rmask"]}
    h = []
    for j in range(4):
        h.append(tok_to_fm(np.concatenate([inp["ctx"][b, 64 * j:64 * j + 64], inp["x"][b, 2048 * j:2048 * (j + 1)]], 0)))
    m["hT"] = np.stack(h)
    m["small"] = np.stack([prep_mixer_small(inp, l, hh) for l in range(DEPTH) for hh in range(4)])
    m["rope_cos"] = np.stack([_CONST[("rope", j)][0] for j in range(4)])
    m["rope_sin"] = np.stack([_CONST[("rope", j)][1] for j in range(4)])
    for l in range(DEPTH):
        f0, f1 = _CONST[("ffn", l, 0)], _CONST[("ffn", l, 1)]
        m[f"wada{l}"], m[f"bada{l}"], m[f"normw{l}"] = f0["wada"], f0["bada"], f0["normw"]
        m[f"w1_{l}0"], m[f"w2_{l}0"] = f0["w1"], f0["w2"]
        m[f"w1_{l}1"], m[f"w2_{l}1"] = f1["w1"], f1["w2"]
        for k, v in _CONST[("proj", l)].items():
            m[f"{k}{l}"] = v
        for k, v in _CONST[("merge", l)].items():
            m[f"{k}{l}"] = v
    return m


def kernel(**inp):
    inp = {k: np.asarray(v) for k, v in inp.items()}
    _prepare(inp)
    if "fused" not in _PROGS:
        _PROGS["fused"] = Fused()
    fz = _PROGS["fused"]
    maps = [fused_inputs(inp, b) for b in range(NB)]
    res = run_bass_kernel_spmd(fz.nc, maps, core_ids=list(range(NB)))
    out = np.zeros((NB, SEQ, D), np.float32)
    for b in range(NB):
        o = res.results[b]["out"]
        for j in range(4):
            out[b, 2048 * j:2048 * (j + 1)] = fm_to_tok(o[j])[64:]
    return out
```

```python
import math
import numpy as np
from contextlib import ExitStack
import concourse.bass as bass
import concourse.mybir as mybir
from concourse.bass_utils import run_bass_kernel_spmd

F32 = mybir.dt.float32
BF16 = mybir.dt.bfloat16
AF = mybir.ActivationFunctionType
ALU = mybir.AluOpType

D = 1024
KC = 8
DFF = 2816
NFC = 22
DEPTH = 2
NB = 2
SEQ = 8192
CTX = 256
NCORE = 8
NTOK = 2112
TSEQ = CTX + SEQ
EPS = 1e-6
TILES = [(0, 64, 1), (64, 512, 0), (576, 512, 0), (1088, 512, 0), (1600, 512, 0)]
GROUPS = [[0, 1], [2], [3], [4]]
GOFF = [0, 576, 1088, 1600]
GW = [576, 512, 512, 512]
NG = 576
NFM = 11
N_DMA_SEM = 24
import os as _os
WENG = _os.environ.get('TOK_WENG', 'pool')


class Buf:
    __slots__ = ("name", "lw", "readers")

    def __init__(self, name=""):
        self.name = name
        self.lw = None
        self.readers = []


class Op:
    __slots__ = ("eng", "fn", "deps", "marked", "mark_no", "is_dma", "dsem", "dval", "dprev", "epoch", "sep")

    def __init__(self, eng, fn, is_dma):
        self.epoch = 0
        self.sep = 0
        self.eng = eng
        self.fn = fn
        self.deps = []
        self.marked = False
        self.mark_no = 0
        self.is_dma = is_dma
        self.dsem = None
        self.dval = 0
        self.dprev = None


class Prog:
    ENGS = ("sync", "act", "dve", "pool", "pe")

    def __init__(self, nc, stack):
        self.nc = nc
        self.stack = stack
        self.ops = {e: [] for e in self.ENGS}
        self.n_dma = 0
        self.dma_last = [None] * N_DMA_SEM
        self.dma_cnt = [0] * N_DMA_SEM
        self._uid = 0
        self.psum = []
        self.psum_i = 0
        self.rot = None
        self.epoch = 0
        self.sep = {e: 0 for e in self.ENGS}
        self.sep_start = {e: 0 for e in self.ENGS}

    def sb(self, name, shape, dtype=F32, persistent=False):
        if getattr(self, "arena", None) is not None and not persistent:
            return self.arena.bf16(list(shape)) if dtype == BF16 else self.arena.f32(list(shape))
        self._uid += 1
        return self.stack.enter_context(self.nc.sbuf_tensor(f"{name}_{self._uid}", list(shape), dtype))

    def ps(self, name, shape, dtype=F32):
        self._uid += 1
        return self.stack.enter_context(self.nc.psum_tensor(f"{name}_{self._uid}", list(shape), dtype))

    def init_psum(self, n=8):
        for i in range(n):
            self.psum.append((self.ps(f"bank{i}", [128, 512], F32), Buf(f"bank{i}")))

    def bank(self):
        rot = self.rot if self.rot is not None else list(range(len(self.psum)))
        t = self.psum[rot[self.psum_i % len(rot)]]
        self.psum_i += 1
        return t

    def barrier(self):
        lasts = []
        for e in self.ENGS:
            for op in reversed(self.ops[e]):
                if not op.is_dma and op.fn is not None:
                    lasts.append(op)
                    break
        lasts += [o for o in self.dma_last if o is not None]
        for e in self.ENGS:
            op = Op(e, None, False)
            op.epoch = self.epoch
            op.sep = self.sep[e]
            for d in lasts:
                op.deps.append(d)
                if not d.is_dma:
                    d.marked = True
            self.ops[e].append(op)
        self.epoch += 1
        for e in self.ENGS:
            n = sum(1 for o in self.ops[e][self.sep_start[e]:] if o.marked and not o.is_dma)
            if n > 6000:
                self.sep[e] += 1
                self.sep_start[e] = len(self.ops[e])

    def add(self, eng, fn, reads=(), writes=(), is_dma=False):
        op = Op(eng, fn, is_dma)
        op.epoch = self.epoch
        op.sep = self.sep[eng]
        deps = []
        for b in reads:
            if b.lw is not None:
                deps.append(b.lw)
        for b in writes:
            if b.lw is not None:
                deps.append(b.lw)
            deps.extend(b.readers)
        seen = set()
        for d in deps:
            if id(d) in seen or d.epoch < self.epoch:
                continue
            if eng == "pe" and d.eng == "pe" and not d.is_dma:
                continue
            seen.add(id(d))
            op.deps.append(d)
            if not d.is_dma:
                d.marked = True
        for b in reads:
            b.readers.append(op)
        for b in writes:
            b.lw = op
            b.readers = []
        if is_dma:
            s = self.n_dma % N_DMA_SEM
            self.n_dma += 1
            op.dsem = s
            op.dprev = self.dma_last[s]
            self.dma_cnt[s] += 16
            op.dval = self.dma_cnt[s]
            self.dma_last[s] = op
        self.ops[eng].append(op)
        return op

    def dma(self, out_ap, in_ap, reads=(), writes=(), eng="sync"):
        return self.add(eng, lambda e: e.dma_start(out=out_ap, in_=in_ap), reads, writes, True)

    def mm(self, out, lhsT, rhs, start, stop, reads, writes):
        return self.add("pe", lambda e: e.matmul(out, lhsT, rhs, start=start, stop=stop), reads, writes)

    def tr(self, out, in_, ident, reads, writes):
        return self.add("pe", lambda e: e.transpose(out, in_, ident), reads, writes)

    def act(self, out, in_, func, reads, writes, scale=None, bias=None, accum_out=None):
        kw = {}
        if scale is not None:
            kw["scale"] = scale
        if bias is not None:
            kw["bias"] = bias
        if accum_out is not None:
            kw["accum_out"] = accum_out
        return self.add("act", lambda e: e.activation(out=out, in_=in_, func=func, **kw), reads, writes)

    def tt(self, out, in0, in1, op, reads, writes, eng="dve"):
        return self.add(eng, lambda e: e.tensor_tensor(out=out, in0=in0, in1=in1, op=op), reads, writes)

    def ts(self, out, in0, s1, op0, reads, writes, s2=None, op1=None, eng="dve", accum_out=None):
        if op1 is None:
            return self.add(eng, lambda e: e.tensor_scalar(out=out, in0=in0, scalar1=s1, scalar2=None, op0=op0,
                                                            accum_out=accum_out), reads, writes)
        return self.add(eng, lambda e: e.tensor_scalar(out=out, in0=in0, scalar1=s1, scalar2=s2, op0=op0, op1=op1,
                                                        accum_out=accum_out), reads, writes)

    def stt(self, out, in0, scalar, in1, op0, op1, reads, writes):
        return self.add("dve", lambda e: e.scalar_tensor_tensor(out=out, in0=in0, scalar=scalar, in1=in1,
                                                                 op0=op0, op1=op1), reads, writes)

    def copy(self, out, in_, reads, writes, eng="dve"):
        if eng == "act":
            return self.add("act", lambda e: e.copy(out=out, in_=in_), reads, writes)
        return self.add(eng, lambda e: e.tensor_copy(out=out, in_=in_), reads, writes)

    def memset(self, ap, val, writes, eng="dve"):
        return self.add(eng, lambda e: e.memset(ap, val), (), writes)

    def emit(self):
        nc = self.nc
        st = self.stack
        esem = {(e, ep): st.enter_context(nc.semaphore(f"s_{e}_{ep}")) for e in self.ENGS
                for ep in range(self.sep[e] + 1)}
        dsem = [st.enter_context(nc.semaphore(f"s_dma{i}")) for i in range(N_DMA_SEM)]
        for e in self.ENGS:
            cnt = {}
            for op in self.ops[e]:
                if op.marked and not op.is_dma:
                    cnt[op.sep] = cnt.get(op.sep, 0) + 1
                    op.mark_no = cnt[op.sep]
        block = st.enter_context(nc.Block())
        handles = {"sync": block.sync, "act": block.scalar, "dve": block.vector,
                   "pool": block.gpsimd, "pe": block.tensor}
        final = [(dsem[i], self.dma_cnt[i]) for i in range(N_DMA_SEM) if self.dma_cnt[i] > 0]

        def make(ename):
            ops = self.ops[ename]

            def body(eng):
                waited = {}

                def wait(sem, key, val):
                    if waited.get(key, 0) >= val:
                        return
                    waited[key] = val
                    eng.wait_ge(sem, val)

                for op in ops:
                    for d in op.deps:
                        if d.is_dma:
                            wait(dsem[d.dsem], ("d", d.dsem), d.dval)
                        else:
                            wait(esem[(d.eng, d.sep)], ("e", d.eng, d.sep), d.mark_no)
                    if op.is_dma and op.dprev is not None:
                        wait(dsem[op.dsem], ("d", op.dsem), op.dprev.dval)
                    if op.fn is None:
                        continue
                    ins = op.fn(eng)
                    if op.is_dma:
                        ins.then_inc(dsem[op.dsem], 16)
                    elif op.marked:
                        ins.then_inc(esem[(ename, op.sep)], 1)
                if ename == "sync":
                    for sem, val in final:
                        eng.wait_ge(sem, val)
            return body

        for e in self.ENGS:
            handles[e](make(e))


class Slots:
    def __init__(self, P, name, shape, dtype, n):
        self.t = [P.sb(f"{name}{i}", shape, dtype) for i in range(n)]
        self.b = [Buf(f"{name}{i}") for i in range(n)]
        self.i = 0

    def next(self):
        k = self.i % len(self.t)
        self.i += 1
        return self.t[k], self.b[k]


class TokenPhase:
    def __init__(self, mode, ext=None, d=None, mods=None, shard=0):
        self.mode = mode
        self.ext = ext
        self.shard = shard
        self.do_merge = mode in ("mid", "last")
        self.do_proj = mode in ("first", "mid")
        if ext is not None:
            self.nc = ext.nc
            self.P = ext.P
            self.d = d
            self.mods = mods
            self.sets = [x for x in ("A", "B") if x in mods]
            self.build()
            return
        nc = self.nc = bass.Bass("TRN2", target_bir_lowering=False)
        dt = nc.dram_tensor
        self.d = {}

        import os
        wbf = os.environ.get("TOK_W_BF16") == "1"

        def din(name, shape):
            isw = wbf and name[:2] in ("w1", "w2", "wp", "wt", "wq", "wk", "wg", "wb", "wo", "wa")
            self.d[name] = dt(name, list(shape), BF16 if isw else F32, kind="ExternalInput").ap()

        def dout(name, shape, dtype=F32):
            self.d[name] = dt(name, list(shape), dtype, kind="ExternalOutput").ap()

        din("hT", [128, KC, NTOK])
        din("cT", [128, KC, 2])
        din("ones", [128, 128])
        dout("hT_out", [128, KC, NTOK])
        sets = []
        if self.do_merge:
            sets.append("A")
            din("yT", [128, 12, NTOK])
            for nm, shp in (("wg", [8, 128, 3 * 8 * 128]), ("wb", [8, 128, 12 * 128]), ("wo", [8, 128, 1024])):
                din(nm, shp)
        if self.do_proj:
            sets.append("B")
            for nm, shp in (("wp", [38, 128, 1024]), ("wt", [128, 8 * 528]), ("wq", [128, 3 * 1024]),
                            ("wkn", [128, 2 * 512]), ("wkv", [128, 2 * 512]), ("qnw", [128, 3]), ("kvnw", [128, 2]),
                            ("rope_cos", [64, NTOK]), ("rope_sin", [64, NTOK])):
                din(nm, shp)
            dout("fm", [4, NFM, 128, NTOK])
            dout("tm_ab", [NTOK, 16])
            dout("tm_mv", [NTOK, 512])
            dout("tm_hv", [NTOK, 512])
        for s in sets:
            din(f"wada{s}", [24, 128, 8 * 384])
            din(f"bada{s}", [128, 72])
            din(f"normw{s}", [128, 6, 8])
            din(f"w1{s}", [NFC, 128, 2048])
            din(f"w2{s}", [8, 128, NFC * 128])
        self.sets = sets
        with ExitStack() as st:
            self.P = P = Prog(nc, st)
            self.build()
            P.emit()

    def build(self):
        P = self.P
        d = self.d
        if self.ext is None:
            P.init_psum(8)
            self.ones = P.sb("ones", [128, 128], BF16)
            self.b_ones = Buf()
            P.dma(self.ones[:], d["ones"], writes=[self.b_ones], eng="pool")
        else:
            P.rot = None
            self.ones = self.ext.cb[:, C_ONES, :]
            self.b_ones = self.ext.b_c
        self.h = P.sb("h", [128, KC, NG], F32)
        self.hn = P.sb("hn", [128, KC, NG], BF16)
        self.a = P.sb("a", [128, NFC, NG], BF16)
        self.y = P.sb("y", [128, KC, NG], F32)
        self.bh = [Buf() for _ in range(2)]
        self.bhn = [Buf() for _ in range(2)]
        self.ba = [[Buf() for _ in range(2)] for _ in range(NFC)]
        self.by = [[Buf() for _ in range(2)] for _ in range(KC)]
        self.sq = Slots(P, "sq", [128, KC, 512], BF16, 1)
        self.tmp = Slots(P, "tmp", [128, 512], F32, 6)
        self.rstd = Slots(P, "rstd", [128, 512], F32, 2)
        self.lnt = Slots(P, "lnt", [128, 512], F32, 2)
        self.w1s = Slots(P, "w1s", [128, KC, 256], BF16, 2)
        self.w2s = Slots(P, "w2s", [128, NFC, 128], BF16, 2)
        self.wps = Slots(P, "wps", [128, KC, 128], BF16, 4)
        if self.ext is None:
            self.mods = {}
            for s in self.sets:
                self.mods[s] = self.compute_mod(s)
        if self.do_proj:
            self.setup_proj()
        if self.do_merge:
            self.setup_merge()
        for g in range(4):
            tiles = [TILES[i] for i in GROUPS[g]]
            go = GOFF[g]
            for (off, w, z) in tiles:
                ti = 0 if z == 1 else 1
                lo = 512 if z == 1 else 0
                P.dma(self.h[:, :, lo:lo + w], d["hT"][:, :, off:off + w], writes=[self.bh[ti]])
            if self.do_merge:
                self.merge(g, "A")
                self.ffn(g, "A", 1)
            if self.do_proj:
                self.ffn(g, "B", 0)
                self.proj(g, "B")
            for (off, w, z) in tiles:
                ti = 0 if z == 1 else 1
                lo = 512 if z == 1 else 0
                P.dma(d["hT_out"][:, :, off:off + w], self.h[:, :, lo:lo + w], reads=[self.bh[ti]])

    def seq_off(self, off):
        j = self.shard
        return 64 * j + off if off < 64 else 256 + 2048 * j + (off - 64)

    def fm_ap(self, hh, r, p0, p1, off, w):
        if self.ext is None:
            return self.d["fm"][hh, r, p0:p1, off:off + w]
        so = self.seq_off(off)
        return self.d["fm_seq"][hh, r, p0:p1, so:so + w]

    def tm_out(self, name, off, sw, st_, ncol, bs_):
        P = self.P
        if self.ext is None:
            P.dma(self.d[name][off:off + sw, :], st_[0:sw, 0:4 * ncol], reads=[bs_])
            return
        so = self.seq_off(off)
        for hh in range(4):
            P.dma(self.d[name + "_seq"][hh, so:so + sw, :], st_[0:sw, hh * ncol:(hh + 1) * ncol], reads=[bs_])

    def compute_mod(self, s):
        P = self.P
        d = self.d
        cT = P.sb("cT", [128, KC, 2], F32)
        b_c = Buf()
        P.dma(cT[:], d["cT"], writes=[b_c])
        sT = P.sb("sT", [128, KC, 2], BF16)
        b_s = Buf()
        P.act(sT[:], cT[:], AF.Silu, [b_c], [b_s])
        bada = P.sb("bada", [128, 72], F32)
        b_b = Buf()
        P.dma(bada[:], d[f"bada{s}"], writes=[b_b])
        nw = P.sb("nw", [128, 6, 8], F32)
        b_nw = Buf()
        P.dma(nw[:], d[f"normw{s}"], writes=[b_nw])
        M = P.sb("M", [128, 72, 2], F32)
        b_M = Buf()
        wsl = Slots(P, "wada", [128, KC, 384], BF16, 2)
        for sl in range(24):
            wt, bw = wsl.next()
            P.dma(wt[:], d[f"wada{s}"][sl].rearrange("p (k n) -> p k n", k=KC), writes=[bw], eng=WENG)
            pt, bp = P.bank()
            for ci in range(3):
                for k in range(KC):
                    P.mm(pt[:, 2 * ci:2 * ci + 2], wt[:, k, ci * 128:(ci + 1) * 128], sT[:, k, :],
                         k == 0, k == KC - 1, [bw, b_s], [bp])
            P.tt(M[:, sl * 3:(sl + 1) * 3, :], pt[:, 0:6].rearrange("p (c z) -> p c z", z=2),
                 bada[:, sl * 3:(sl + 1) * 3].unsqueeze(2).broadcast_to([128, 3, 2]), ALU.add, [bp, b_b], [b_M])
        S = P.sb("S", [128, 9, 8, 2], F32)
        b_S = Buf()

        def Mi(i):
            return M[:, i * 8:(i + 1) * 8, :]

        def nwb(i):
            return nw[:, i, :].unsqueeze(2).broadcast_to([128, 8, 2])
        one = P.sb("onep", [128, 8, 2], F32)
        b_one = Buf()
        for (k, nwi, sci) in ((0, 0, 1), (3, 2, 4), (6, 4, 7)):
            P.ts(one[:], Mi(sci), 1.0, ALU.add, [b_M], [b_one])
            P.tt(S[:, k], one[:], nwb(nwi), ALU.mult, [b_one, b_nw], [b_S])
        for (k, shi) in ((1, 0), (4, 3), (7, 6)):
            P.copy(S[:, k], Mi(shi), [b_M], [b_S])
        for (k, gi, nwi, f) in ((2, 2, 1, 0.5), (5, 5, 3, 1.0), (8, 8, 5, 0.5)):
            P.ts(one[:], Mi(gi), f, ALU.mult, [b_M], [b_one])
            P.tt(S[:, k], one[:], nwb(nwi), ALU.mult, [b_one, b_nw], [b_S])
        return S, b_S

    def rstd_of(self, src, b_src, nch, w, dn):
        P = self.P
        sq, bsq = self.sq.next()
        P.act(sq[:, 0:nch, 0:w], src, AF.Square, [b_src], [bsq])
        pt, bp = P.bank()
        for c in range(nch):
            P.mm(pt[:, 0:w], self.ones[:], sq[:, c, 0:w], c == 0, c == nch - 1, [bsq, self.b_ones], [bp])
        ln, bln = self.lnt.next()
        P.act(ln[:, 0:w], pt[:, 0:w], AF.Ln, [bp], [bln], scale=1.0 / dn, bias=EPS)
        r, br = self.rstd.next()
        P.act(r[:, 0:w], ln[:, 0:w], AF.Exp, [bln], [br], scale=-0.5)
        return r, br

    def norm_mod(self, g, s, kbase):
        P = self.P
        S, bS = self.mods[s]
        go = GOFF[g]
        for tix in GROUPS[g]:
            off, w, z = TILES[tix]
            ti = 0 if z == 1 else 1
            lo = 512 if z == 1 else 0
            r, br = self.rstd_of(self.h[:, :, lo:lo + w], self.bh[ti], KC, w, D)
            for c in range(KC):
                t, bt = self.tmp.next()
                P.tt(t[:, 0:w], self.h[:, c, lo:lo + w], r[:, 0:w], ALU.mult, [self.bh[ti], br], [bt])
                P.act(self.hn[:, c, lo:lo + w], t[:, 0:w], AF.Identity, [bt, bS], [self.bhn[ti]],
                      scale=S[:, kbase, c, z:z + 1], bias=S[:, kbase + 1, c, z:z + 1])

    def post_res(self, g, s, kgate):
        P = self.P
        S, bS = self.mods[s]
        go = GOFF[g]
        for tix in GROUPS[g]:
            off, w, z = TILES[tix]
            ti = 0 if z == 1 else 1
            lo = 512 if z == 1 else 0
            by_all = [self.by[c][ti] for c in range(KC)]
            bsrc = Buf()
            sq, bsq = self.sq.next()
            P.act(sq[:, :, 0:w], self.y[:, :, lo:lo + w], AF.Square, by_all, [bsq])
            pt, bp = P.bank()
            for c in range(KC):
                P.mm(pt[:, 0:w], self.ones[:], sq[:, c, 0:w], c == 0, c == KC - 1, [bsq, self.b_ones], [bp])
            ln, bln = self.lnt.next()
            P.act(ln[:, 0:w], pt[:, 0:w], AF.Ln, [bp], [bln], scale=1.0 / D, bias=EPS)
            r, br = self.rstd.next()
            P.act(r[:, 0:w], ln[:, 0:w], AF.Exp, [bln], [br], scale=-0.5)
            for c in range(KC):
                t, bt = self.tmp.next()
                P.tt(t[:, 0:w], self.y[:, c, lo:lo + w], r[:, 0:w], ALU.mult, [self.by[c][ti], br], [bt])
                P.stt(self.h[:, c, lo:lo + w], t[:, 0:w], S[:, kgate, c, z:z + 1], self.h[:, c, lo:lo + w],
                      ALU.mult, ALU.add, [bt, bS, self.bh[ti]], [self.bh[ti]])

    def ffn(self, g, s, i):
        P = self.P
        d = self.d
        kb = 0 if i == 0 else 6
        self.norm_mod(g, s, kb)
        go = GOFF[g]
        tl = [((0 if TILES[tix][2] == 1 else 1),) + TILES[tix] for tix in GROUPS[g]]
        w1 = d[f"w1{s}"]
        w2 = d[f"w2{s}"]
        for n in range(NFC):
            wt, bw = self.w1s.next()
            P.dma(wt[:], w1[n].rearrange("p (k n) -> p k n", k=KC), writes=[bw], eng=WENG)
            for (ti, off, w, z) in tl:
                lo = 512 if z == 1 else 0
                pg, bpg = P.bank()
                pu, bpu = P.bank()
                for k in range(KC):
                    P.mm(pg[:, 0:w], wt[:, k, 0:128], self.hn[:, k, lo:lo + w], k == 0, k == KC - 1,
                         [bw, self.bhn[ti]], [bpg])
                for k in range(KC):
                    P.mm(pu[:, 0:w], wt[:, k, 128:256], self.hn[:, k, lo:lo + w], k == 0, k == KC - 1,
                         [bw, self.bhn[ti]], [bpu])
                t, bt = self.tmp.next()
                P.act(t[:, 0:w], pg[:, 0:w], AF.Silu, [bpg], [bt])
                P.tt(self.a[:, n, lo:lo + w], t[:, 0:w], pu[:, 0:w], ALU.mult, [bt, bpu], [self.ba[n][ti]])
        for dc in range(KC):
            wt, bw = self.w2s.next()
            P.dma(wt[:], w2[dc].rearrange("p (k n) -> p k n", k=NFC), writes=[bw], eng=WENG)
            for (ti, off, w, z) in tl:
                lo = 512 if z == 1 else 0
                py, bpy = P.bank()
                for k in range(NFC):
                    P.mm(py[:, 0:w], wt[:, k, :], self.a[:, k, lo:lo + w], k == 0, k == NFC - 1,
                         [bw, self.ba[k][ti]], [bpy])
                P.copy(self.y[:, dc, lo:lo + w], py[:, 0:w], [bpy], [self.by[dc][ti]], eng="act")
        self.post_res(g, s, kb + 2)

    def setup_proj(self):
        P = self.P
        d = self.d
        self.wt = P.sb("wt", [128, KC, 528], BF16)
        self.b_wt = Buf()
        P.dma(self.wt[:], d["wt"].rearrange("p (k n) -> p k n", k=KC), writes=[self.b_wt], eng=WENG)
        self.wq = P.sb("wq", [128, 3, 1024], BF16)
        self.b_wq = Buf()
        P.dma(self.wq[:], d["wq"].rearrange("p (k n) -> p k n", k=3), writes=[self.b_wq], eng=WENG)
        self.wkn = P.sb("wkn", [128, 2, 512], BF16)
        self.b_wkn = Buf()
        P.dma(self.wkn[:], d["wkn"].rearrange("p (k n) -> p k n", k=2), writes=[self.b_wkn], eng=WENG)
        self.wkv = P.sb("wkv", [128, 2, 512], BF16)
        self.b_wkv = Buf()
        P.dma(self.wkv[:], d["wkv"].rearrange("p (k n) -> p k n", k=2), writes=[self.b_wkv], eng=WENG)
        self.qnw = P.sb("qnw", [128, 3], F32)
        self.kvnw = P.sb("kvnw", [128, 2], F32)
        self.b_nws = Buf()
        P.dma(self.qnw[:], d["qnw"], writes=[self.b_nws])
        P.dma(self.kvnw[:], d["kvnw"], writes=[self.b_nws])
        self.cos = P.sb("cos", [64, NG], F32)
        self.sin = P.sb("sin", [64, NG], F32)
        self.b_rope = Buf()
        self.qkva = self.y[:, 0:5, :]
        self.b_qkva = self.by[0:5]
        self.qkvn = self.a[:, 0:5, :]
        self.b_qkvn5 = self.ba[0:5]
        self.krr = self.y[0:64, 5:7, :]
        self.b_krr2 = self.by[5:7]
        self.stage = Slots(P, "stage", [128, 512], F32, 3)
        self.stage2 = self.stage

    def proj(self, g, s):
        P = self.P
        d = self.d
        self.norm_mod(g, s, 3)
        go = GOFF[g]
        gw = GW[g]
        tl = [((0 if TILES[tix][2] == 1 else 1),) + TILES[tix] for tix in GROUPS[g]]
        for (ti, off, w, z) in tl:
            lo = 512 if z == 1 else 0
            P.dma(self.cos[:, lo:lo + w], d["rope_cos"][:, off:off + w], writes=[self.b_rope])
            P.dma(self.sin[:, lo:lo + w], d["rope_sin"][:, off:off + w], writes=[self.b_rope])
        ev = 0
        for ci in range(38):
            wt, bw = self.wps.next()
            P.dma(wt[:], d["wp"][ci].rearrange("p (k n) -> p k n", k=KC), writes=[bw], eng=WENG)
            for (ti, off, w, z) in tl:
                lo = 512 if z == 1 else 0
                if ci < 37:
                    pt, bp = P.bank()
                    for k in range(KC):
                        P.mm(pt[:, 0:w], wt[:, k, :], self.hn[:, k, lo:lo + w], k == 0, k == KC - 1,
                             [bw, self.bhn[ti]], [bp])
                    if ci < 32:
                        hh, r = ci // 8, ci % 8
                        st_, bs_ = self.stage.next()
                        P.copy(st_[:, 0:w], pt[:, 0:w], [bp], [bs_], eng=("act" if ev % 2 == 0 else "dve"))
                        ev += 1
                        P.dma(self.fm_ap(hh, r, 0, 128, off, w), st_[:, 0:w], reads=[bs_])
                    else:
                        P.copy(self.qkva[:, ci - 32, lo:lo + w], pt[:, 0:w], [bp], [self.b_qkva[ci - 32][ti]],
                               eng=("act" if ev % 2 == 0 else "dve"))
                        ev += 1
                else:
                    for half in range(2):
                        pt, bp = P.bank()
                        for k in range(KC):
                            P.mm(pt[0:64, 0:w], wt[:, k, half * 64:(half + 1) * 64], self.hn[:, k, lo:lo + w],
                                 k == 0, k == KC - 1, [bw, self.bhn[ti]], [bp])
                        P.copy(self.krr[:, half, lo:lo + w], pt[0:64, 0:w], [bp], [self.b_krr2[half][ti]], eng="act")
        for (ti, off, w, z) in tl:
            lo = 512 if z == 1 else 0
            t1, b1 = self.stage2.next()
            t2, b2 = self.stage2.next()
            P.tt(t1[0:64, 0:w], self.krr[:, 0, lo:lo + w], self.cos[:, lo:lo + w], ALU.mult,
                 [self.b_krr2[0][ti], self.b_rope], [b1])
            P.tt(t2[0:64, 0:w], self.krr[:, 1, lo:lo + w], self.sin[:, lo:lo + w], ALU.mult,
                 [self.b_krr2[1][ti], self.b_rope], [b2])
            P.tt(t1[0:64, 0:w], t1[0:64, 0:w], t2[0:64, 0:w], ALU.add, [b1, b2], [b1])
            for hh in range(4):
                P.dma(self.fm_ap(hh, 10, 64, 128, off, w), t1[0:64, 0:w], reads=[b1])
        for (ti, off, w, z) in tl:
            lo = 512 if z == 1 else 0
            for (c0, nch, dn, nwt) in ((0, 3, 384, self.qnw), (3, 2, 256, self.kvnw)):
                bsrc = [self.b_qkva[c0 + c][ti] for c in range(nch)]
                sq, bsq = self.sq.next()
                P.act(sq[:, 0:nch, 0:w], self.qkva[:, c0:c0 + nch, lo:lo + w], AF.Square, bsrc, [bsq])
                pt, bp = P.bank()
                for c in range(nch):
                    P.mm(pt[:, 0:w], self.ones[:], sq[:, c, 0:w], c == 0, c == nch - 1, [bsq, self.b_ones], [bp])
                ln, bln = self.lnt.next()
                P.act(ln[:, 0:w], pt[:, 0:w], AF.Ln, [bp], [bln], scale=1.0 / dn, bias=EPS)
                r, br = self.rstd.next()
                P.act(r[:, 0:w], ln[:, 0:w], AF.Exp, [bln], [br], scale=-0.5)
                for c in range(nch):
                    t, bt = self.tmp.next()
                    P.tt(t[:, 0:w], self.qkva[:, c0 + c, lo:lo + w], r[:, 0:w], ALU.mult,
                         [self.b_qkva[c0 + c][ti], br], [bt])
                    P.act(self.qkvn[:, c0 + c, lo:lo + w], t[:, 0:w], AF.Identity, [bt, self.b_nws],
                          [self.b_qkvn5[c0 + c][ti]], scale=nwt[:, c:c + 1])
        for hh in range(4):
            for (ti, off, w, z) in tl:
                lo = 512 if z == 1 else 0
                pt, bp = P.bank()
                for k in range(3):
                    P.mm(pt[:, 0:w], self.wq[:, k, hh * 256:hh * 256 + 128], self.qkvn[:, k, lo:lo + w],
                         k == 0, k == 2, [self.b_wq, self.b_qkvn5[k][ti]], [bp])
                st_, bs_ = self.stage.next()
                P.copy(st_[:, 0:w], pt[:, 0:w], [bp], [bs_], eng="act")
                P.dma(self.fm_ap(hh, 8, 0, 128, off, w), st_[:, 0:w], reads=[bs_])
                pt, bp = P.bank()
                for k in range(2):
                    P.mm(pt[:, 0:w], self.wkn[:, k, hh * 128:(hh + 1) * 128], self.qkvn[:, 3 + k, lo:lo + w],
                         k == 0, k == 1, [self.b_wkn, self.b_qkvn5[3 + k][ti]], [bp])
                st_, bs_ = self.stage.next()
                P.copy(st_[:, 0:w], pt[:, 0:w], [bp], [bs_], eng="dve")
                P.dma(self.fm_ap(hh, 9, 0, 128, off, w), st_[:, 0:w], reads=[bs_])
                pr, bpr = P.bank()
                for k in range(3):
                    P.mm(pr[0:64, 0:w], self.wq[:, k, hh * 256 + 128:hh * 256 + 192], self.qkvn[:, k, lo:lo + w],
                         k == 0, k == 2, [self.b_wq, self.b_qkvn5[k][ti]], [bpr])
                ps_, bps = P.bank()
                for k in range(3):
                    P.mm(ps_[0:64, 0:w], self.wq[:, k, hh * 256 + 192:hh * 256 + 256], self.qkvn[:, k, lo:lo + w],
                         k == 0, k == 2, [self.b_wq, self.b_qkvn5[k][ti]], [bps])
                t1, b1 = self.stage2.next()
                t2, b2 = self.stage2.next()
                P.tt(t1[0:64, 0:w], pr[0:64, 0:w], self.cos[:, lo:lo + w], ALU.mult, [bpr, self.b_rope], [b1])
                P.tt(t2[0:64, 0:w], ps_[0:64, 0:w], self.sin[:, lo:lo + w], ALU.mult, [bps, self.b_rope], [b2])
                P.tt(t1[0:64, 0:w], t1[0:64, 0:w], t2[0:64, 0:w], ALU.add, [b1, b2], [b1])
                P.dma(self.fm_ap(hh, 10, 0, 64, off, w), t1[0:64, 0:w], reads=[b1])
        for (ti, off, w, z) in tl:
            lo = 512 if z == 1 else 0
            for s0 in range(0, w, 128):
                sw = min(128, w - s0)
                a0 = lo + s0
                pt, bp = P.bank()
                for k in range(KC):
                    P.mm(pt[0:sw, :], self.hn[:, k, a0:a0 + sw], self.wt[:, k, 0:512], k == 0, k == KC - 1,
                         [self.b_wt, self.bhn[ti]], [bp])
                st_, bs_ = self.stage.next()
                P.copy(st_[0:sw, :], pt[0:sw, :], [bp], [bs_], eng="act")
                self.tm_out("tm_hv", off + s0, sw, st_, 128, bs_)
                pt, bp = P.bank()
                for k in range(KC):
                    P.mm(pt[0:sw, 0:16], self.hn[:, k, a0:a0 + sw], self.wt[:, k, 512:528], k == 0, k == KC - 1,
                         [self.b_wt, self.bhn[ti]], [bp])
                st_, bs_ = self.stage.next()
                P.copy(st_[0:sw, 0:16], pt[0:sw, 0:16], [bp], [bs_], eng="dve")
                self.tm_out("tm_ab", off + s0, sw, st_, 4, bs_)
                pt, bp = P.bank()
                for k in range(2):
                    P.mm(pt[0:sw, :], self.qkvn[:, 3 + k, a0:a0 + sw], self.wkv[:, k, :], k == 0, k == 1,
                         [self.b_wkv, self.b_qkvn5[3 + k][ti]], [bp])
                st_, bs_ = self.stage.next()
                P.copy(st_[0:sw, :], pt[0:sw, :], [bp], [bs_], eng="act")
                self.tm_out("tm_mv", off + s0, sw, st_, 128, bs_)

    def setup_merge(self):
        P = self.P
        self.yT = self.a[:, 0:12, :]
        self.b_yTq = self.ba[0:12]
        self.mT = self.a[:, 12:20, :]
        self.b_mT = self.ba[12:20]
        self.wbs = Slots(P, "wbs", [128, 12, 128], BF16, 2)
        self.sg = self.tmp
        self.macc = Slots(P, "macc", [128, 512], F32, 2)

    def merge(self, g, s):
        P = self.P
        d = self.d
        S, bS = self.mods[s]
        go = GOFF[g]
        tl = [((0 if TILES[tix][2] == 1 else 1),) + TILES[tix] for tix in GROUPS[g]]
        for (ti, off, w, z) in tl:
            lo = 512 if z == 1 else 0
            if self.ext is None:
                P.dma(self.yT[:, :, lo:lo + w], d["yT"][:, :, off:off + w],
                      writes=[self.b_yTq[q][ti] for q in range(12)], eng="pool")
            else:
                so = self.seq_off(off)
                for bi in range(3):
                    for hh in range(4):
                        P.dma(self.yT[:, bi * 4 + hh, lo:lo + w], d["yseq"][hh, bi, :, so:so + w],
                              writes=[self.b_yTq[bi * 4 + hh][ti]], eng="pool")
        self.norm_mod(g, s, 3)
        for c in range(KC):
            wgl = []
            for j in range(3):
                wg, bwg = self.wps.next()
                P.dma(wg[:], d["wg"][c][:, j * 1024:(j + 1) * 1024].rearrange("p (k n) -> p k n", k=KC),
                      writes=[bwg], eng=WENG)
                wgl.append((wg, bwg))
            wb, bwb = self.wbs.next()
            P.dma(wb[:], d["wb"][c].rearrange("p (q n) -> p q n", q=12), writes=[bwb], eng=WENG)
            for (ti, off, w, z) in tl:
                lo = 512 if z == 1 else 0
                acc, bacc = self.macc.next()
                for j in range(3):
                    pg, bpg = P.bank()
                    wg, bwg = wgl[j]
                    for k in range(KC):
                        P.mm(pg[:, 0:w], wg[:, k, :], self.hn[:, k, lo:lo + w], k == 0, k == KC - 1,
                             [bwg, self.bhn[ti]], [bpg])
                    pb, bpb = P.bank()
                    for q in range(4):
                        P.mm(pb[:, 0:w], wb[:, j * 4 + q, :], self.yT[:, j * 4 + q, lo:lo + w], q == 0, q == 3,
                             [bwb, self.b_yTq[j * 4 + q][ti]], [bpb])
                    sg, bsg = self.sg.next()
                    P.act(sg[:, 0:w], pg[:, 0:w], AF.Sigmoid, [bpg], [bsg])
                    if j == 0:
                        P.tt(acc[:, 0:w], sg[:, 0:w], pb[:, 0:w], ALU.mult, [bsg, bpb], [bacc])
                    else:
                        P.tt(sg[:, 0:w], sg[:, 0:w], pb[:, 0:w], ALU.mult, [bsg, bpb], [bsg])
                        if j == 1:
                            P.tt(acc[:, 0:w], acc[:, 0:w], sg[:, 0:w], ALU.add, [bacc, bsg], [bacc])
                        else:
                            P.tt(self.mT[:, c, lo:lo + w], acc[:, 0:w], sg[:, 0:w], ALU.add, [bacc, bsg],
                                 [self.b_mT[c][ti]])
        for c in range(KC):
            wo, bwo = self.wps.next()
            P.dma(wo[:], d["wo"][c].rearrange("p (k n) -> p k n", k=KC), writes=[bwo], eng=WENG)
            for (ti, off, w, z) in tl:
                lo = 512 if z == 1 else 0
                py, bpy = P.bank()
                for k in range(KC):
                    P.mm(py[:, 0:w], wo[:, k, :], self.mT[:, k, lo:lo + w], k == 0, k == KC - 1,
                         [bwo, self.b_mT[k][ti]], [bpy])
                P.copy(self.y[:, c, lo:lo + w], py[:, 0:w], [bpy], [self.by[c][ti]], eng="act")
        self.post_res(g, s, 5)


def _fm(v):
    return np.ascontiguousarray(v.reshape(-1, 128).T)


def _wtile(w, cols):
    K = w.shape[0]
    sub = w[:, cols].reshape(K // 128, 128, len(cols))
    return np.ascontiguousarray(sub.transpose(1, 0, 2).reshape(128, -1))


def _partner(j):
    return j + 16 if (j % 32) < 16 else j - 16


def prep_ffn_set(inp, l, i):
    out = {}
    wa = inp["w_ada"][l]
    out["wada"] = np.stack([_wtile(wa, np.arange(sl * 384, (sl + 1) * 384)) for sl in range(24)])
    out["bada"] = _fm(inp["b_ada"][l])
    out["normw"] = np.ascontiguousarray(inp["norm_w"][l].reshape(6, 8, 128).transpose(2, 0, 1))
    w1 = inp["ffn_w_in"][l, i]
    out["w1"] = np.stack([_wtile(w1, np.concatenate([np.arange(n * 128, (n + 1) * 128),
                                                     DFF + np.arange(n * 128, (n + 1) * 128)])) for n in range(NFC)])
    w2 = inp["ffn_w_out"][l, i]
    out["w2"] = np.stack([_wtile(w2, np.arange(dc * 128, (dc + 1) * 128)) for dc in range(8)])
    return out


def prep_proj(inp, l):
    out = {}
    w_in = inp["w_in"][l]
    chunks = []
    for hh in range(4):
        for base in (0, 512, 1024, 1536, 2768, 3280, 3792, 4816):
            chunks.append(np.arange(base + hh * 128, base + (hh + 1) * 128))
    for c in range(3):
        chunks.append(np.arange(2064 + c * 128, 2064 + (c + 1) * 128))
    for c in range(2):
        chunks.append(np.arange(2448 + c * 128, 2448 + (c + 1) * 128))
    kr = 2704 + np.arange(64)
    chunks.append(np.concatenate([kr, 2704 + np.array([_partner(j) for j in range(64)])]))
    out["wp"] = np.stack([_wtile(w_in, c) for c in chunks])
    abc = np.array([[2048 + hh, 2052 + hh, 2056 + hh, 2060 + hh] for hh in range(4)]).reshape(-1)
    out["wt"] = _wtile(w_in, np.concatenate([4304 + np.arange(512), abc]))
    wq = inp["mla_w_q_b"][l]
    qc = []
    for hh in range(4):
        qc += list(hh * 192 + np.arange(128)) + list(hh * 192 + 128 + np.arange(64)) + \
            [hh * 192 + 128 + _partner(j) for j in range(64)]
    out["wq"] = _wtile(wq, np.array(qc))
    wkv = inp["mla_w_kv_b"][l]
    out["wkn"] = _wtile(wkv, np.concatenate([hh * 256 + np.arange(128) for hh in range(4)]))
    out["wkv"] = _wtile(wkv, np.concatenate([hh * 256 + 128 + np.arange(128) for hh in range(4)]))
    out["qnw"] = _fm(inp["mla_q_norm"][l])
    out["kvnw"] = _fm(inp["mla_kv_norm"][l])
    return out


def prep_merge(inp, l):
    out = {}
    w_in = inp["w_in"][l]
    out["wg"] = np.stack([np.concatenate([_wtile(w_in, 5328 + j * 1024 + c * 128 + np.arange(128)) for j in range(3)],
                                         axis=1) for c in range(8)])
    wb = inp["w_branch"][l]
    out["wb"] = np.stack([np.concatenate([_wtile(wb[j], c * 128 + np.arange(128)) for j in range(3)], axis=1)
                          for c in range(8)])
    out["wo"] = np.stack([_wtile(inp["w_out"][l], c * 128 + np.arange(128)) for c in range(8)])
    return out


def rope_tables(shard):
    cos = np.ones((64, NTOK), np.float64)
    sin = np.zeros((64, NTOK), np.float64)
    t = shard * 2048 + np.arange(2048)
    row = (t // 64).astype(np.float64)
    col = (t % 64).astype(np.float64)
    inv = (10000.0 ** (-np.arange(16, dtype=np.float32) / np.float32(16))).astype(np.float32).astype(np.float64)
    for dd in range(64):
        a_, s_, f_ = dd // 32, (dd % 32) // 16, dd % 16
        pos = row if a_ == 0 else col
        ang = (pos.astype(np.float32) * inv[f_].astype(np.float32)).astype(np.float64)
        cos[dd, 64:] = np.cos(ang)
        sin[dd, 64:] = np.sin(ang) * (-1.0 if s_ == 0 else 1.0)
    return cos.astype(np.float32), sin.astype(np.float32)


def tok_to_fm(h):
    return np.ascontiguousarray(h.reshape(NTOK, 8, 128).transpose(2, 1, 0))


def fm_to_tok(hT):
    return np.ascontiguousarray(hT.transpose(2, 1, 0).reshape(NTOK, 1024))


class Arena:
    def __init__(self, P, kb=192):
        self.cap = kb * 256
        self.t = P.sb("arena", [128, self.cap], F32)
        self.off = 0

    def reset(self):
        self.off = 0

    def _shape(self, ap, shape):
        if len(shape) == 3:
            return ap.rearrange("p (a b) -> p a b", a=shape[1])
        if len(shape) == 4:
            return ap.rearrange("p (a b c) -> p a b c", a=shape[1], b=shape[2])
        return ap

    def f32(self, shape):
        n = int(np.prod(shape[1:]))
        o = self.off
        self.off += n
        assert self.off <= self.cap, ("arena overflow", self.off, self.cap)
        return self._shape(self.t[0:shape[0], o:o + n], shape)

    def bf16(self, shape):
        n = int(np.prod(shape[1:]))
        nw = (n + 1) // 2
        o = self.off
        self.off += nw
        assert self.off <= self.cap, ("arena overflow", self.off, self.cap)
        ap = self.t[0:shape[0], o:o + nw].bitcast(BF16)[:, 0:n]
        return self._shape(ap, shape)


class ASlots:
    def __init__(self, aps):
        self.t = aps
        self.b = [Buf() for _ in aps]
        self.i = 0

    def next(self):
        k = self.i % len(self.t)
        self.i += 1
        return self.t[k], self.b[k]


NBLK = TSEQ // 128
MLA_SCALE = 192 ** -0.5
C_ONES, C_IDENT, C_HM0, C_HM1, C_NEG0, C_NEG1, C_STR0, C_STR1, C_TRI0, C_TRI1, C_ROWM = range(11)
C_LVL = 11
NCONST = 25


def make_consts():
    c = np.zeros((NCONST, 128, 128), np.float32)
    i = np.arange(128)
    c[C_ONES] = 1.0
    c[C_IDENT] = np.eye(128)
    J, I = np.meshgrid(i, i, indexing="ij")
    same32 = (J // 32) == (I // 32)
    c[C_HM0] = (same32 & (I >= J))
    c[C_HM1] = (same32 & (I <= J))
    c[C_NEG0] = np.where(I >= J, 0.0, -30000.0)
    c[C_NEG1] = np.where(I <= J, 0.0, -30000.0)
    c[C_STR0] = (I > J)
    c[C_STR1] = (I < J)
    c[C_TRI0] = (J <= I)
    c[C_TRI1] = (J >= I)
    for cc in range(4):
        c[C_ROWM][:, cc] = (i // 32 == cc)
    for d in range(2):
        for l in range(7):
            b = 2 ** l
            Ii, Jj = J, I
            sameblk = (Ii // (2 * b)) == (Jj // (2 * b))
            ih = (Ii // b) % 2
            jh = (Jj // b) % 2
            if d == 0:
                m = sameblk & (ih == 1) & (jh == 0)
            else:
                m = sameblk & (ih == 0) & (jh == 1)
            c[C_LVL + d * 7 + l] = -1.0 * m
    return c


class MixerPhase:
    def __init__(self, layer, parts=("gdn", "mla", "hg"), stage=99, ext=None, d=None, small=None):
        self.layer = layer
        self.stage = stage
        if ext is not None:
            self.nc, self.P, self.d, self.A = ext.nc, ext.P, d, ext.A
            self.cf, self.cb, self.b_c, self.small = ext.cf, ext.cb, ext.b_c, small
            self.mla()
            self.hgrn2()
            self.gdn()
            return
        nc = self.nc = bass.Bass("TRN2", target_bir_lowering=False)
        self.d = {}

        def din(name, shape):
            self.d[name] = nc.dram_tensor(name, list(shape), F32, kind="ExternalInput").ap()

        din("fm", [NFM, 128, TSEQ])
        din("tm_ab", [TSEQ, 4])
        din("tm_mv", [TSEQ, 128])
        din("tm_hv", [TSEQ, 128])
        din("consts", [NCONST, 128, 128])
        din("rmask", [128, 1408])
        din("small", [128, 32])
        self.d["yT"] = nc.dram_tensor("yT", [3, 128, TSEQ], F32, kind="ExternalOutput").ap()
        with ExitStack() as st:
            self.P = P = Prog(nc, st)
            P.init_psum(8)
            self.cf = P.sb("cf", [128, NCONST, 128], F32)
            self.cb = P.sb("cb", [128, NCONST, 128], BF16)
            self.b_c = Buf()
            P.dma(self.cf[:], self.d["consts"].rearrange("n p f -> p n f"), writes=[self.b_c])
            P.dma(self.cb[:], self.d["consts"].rearrange("n p f -> p n f"), writes=[self.b_c], eng="pool")
            self.small = P.sb("small", [128, 32], F32)
            P.dma(self.small[:], self.d["small"], writes=[self.b_c])
            self.A = Arena(P, 178)
            if "mla" in parts:
                self.mla()
            if "hg" in parts:
                self.hgrn2()
            if "gdn" in parts:
                self.gdn()
            P.emit()

    def mla(self):
        P, A, d = self.P, self.A, self.d
        P.barrier()
        A.reset()
        P.rot = [0, 1, 2, 3]
        T = TSEQ
        fm = d["fm"]
        qn = A.bf16([128, T]); kn = A.bf16([128, T]); qr = A.bf16([64, T]); kr = A.bf16([64, T])
        v = A.bf16([128, NBLK, 128])
        bq, bk, bv = Buf(), Buf(), Buf()
        for c0 in range(0, T, 2112):
            P.dma(kn[:, c0:c0 + 2112], fm[9][:, c0:c0 + 2112], writes=[bk], eng="pool")
            P.dma(kr[:, c0:c0 + 2112], fm[10][64:128, c0:c0 + 2112], writes=[bk], eng="pool")
            P.dma(qn[:, c0:c0 + 2112], fm[8][:, c0:c0 + 2112], writes=[bq], eng="pool")
            P.dma(qr[:, c0:c0 + 2112], fm[10][0:64, c0:c0 + 2112], writes=[bq], eng="pool")
        for n0 in range(0, NBLK, 22):
            P.dma(v[:, n0:n0 + 22, :], d["tm_mv"][n0 * 128:(n0 + 22) * 128, :].rearrange("(n p) d -> p n d", p=128),
                  writes=[bv], eng="pool")
        pts = ASlots([A.bf16([128, 512]) for _ in range(4)])
        rcs = ASlots([A.f32([128, 512]) for _ in range(2)])
        yos = ASlots([A.f32([128, 512]) for _ in range(2)])
        ones = self.cb[:, C_ONES, :]
        blocks = [(0, 256, 2)] + [(256 + 512 * b, 512, NBLK) for b in range(16)]
        for bi, (q0, qw, nk) in enumerate(blocks):
            O, bO = P.psum[4 + bi % 2]
            L, bL = P.psum[6 + bi % 2]
            LOOK = 2
            pend = []

            def scores(kc):
                S, bS = P.bank()
                P.mm(S[:, 0:qw], kn[:, kc * 128:(kc + 1) * 128], qn[:, q0:q0 + qw], True, False, [bk, bq], [bS])
                P.mm(S[:, 0:qw], kr[:, kc * 128:(kc + 1) * 128], qr[:, q0:q0 + qw], False, True, [bk, bq], [bS])
                pt, bpt = pts.next()
                P.act(pt[:, 0:qw], S[:, 0:qw], AF.Exp, [bS], [bpt], scale=MLA_SCALE)
                pend.append((kc, pt, bpt))

            def consume():
                kc, pt, bpt = pend.pop(0)
                P.mm(O[:, 0:qw], v[:, kc, :], pt[:, 0:qw], kc == 0, kc == nk - 1, [bv, bpt], [bO])
                P.mm(L[:, 0:qw], ones, pt[:, 0:qw], kc == 0, kc == nk - 1, [self.b_c, bpt], [bL])

            for kc in range(nk):
                scores(kc)
                if len(pend) > LOOK:
                    consume()
            while pend:
                consume()
            rc, brc = rcs.next()
            P.add("dve", (lambda o_, i_: (lambda e: e.reciprocal(out=o_, in_=i_)))(rc[:, 0:qw], L[:, 0:qw]), [bL], [brc])
            yo, byo = yos.next()
            P.tt(yo[:, 0:qw], O[:, 0:qw], rc[:, 0:qw], ALU.mult, [bO, brc], [byo])
            P.dma(d["yT"][1, :, q0:q0 + qw], yo[:, 0:qw], reads=[byo])

    def readout(self, of, b_of_all, gate_row, nw_ap, out_idx, scr):
        P, d = self.P, self.d
        W = 384
        g_s, sq_s, ln_s, r_s, t_s = scr
        ones = self.cb[:, C_ONES, :]
        for t0 in range(0, TSEQ, W):
            g, bg = g_s.next()
            P.dma(g[:, 0:W], d["fm"][gate_row][:, t0:t0 + W], writes=[bg])
            P.act(g[:, 0:W], g[:, 0:W], AF.Silu, [bg], [bg])
            sq, bsq = sq_s.next()
            P.act(sq[:, 0:W], of[:, t0:t0 + W], AF.Square, b_of_all, [bsq])
            pt, bp = P.bank()
            P.mm(pt[:, 0:W], ones, sq[:, 0:W], True, True, [self.b_c, bsq], [bp])
            ln, bln = ln_s.next()
            P.act(ln[:, 0:W], pt[:, 0:W], AF.Ln, [bp], [bln], scale=1.0 / 128, bias=EPS)
            r, br = r_s.next()
            P.act(r[:, 0:W], ln[:, 0:W], AF.Exp, [bln], [br], scale=-0.5)
            t, bt = t_s.next()
            P.tt(t[:, 0:W], of[:, t0:t0 + W], r[:, 0:W], ALU.mult, b_of_all + [br], [bt])
            P.stt(t[:, 0:W], t[:, 0:W], nw_ap, g[:, 0:W], ALU.mult, ALU.mult, [bt, bg, self.b_c], [bt])
            P.dma(d["yT"][out_idx, :, t0:t0 + W], t[:, 0:W], reads=[bt])

    def hgrn2(self):
        P, A, d = self.P, self.A, self.d
        P.barrier()
        A.reset()
        P.rot = [0, 1, 2, 3, 6, 7]
        T = TSEQ
        SEG = 1408
        NSEG = 6
        fm = d["fm"]
        sm = self.small
        v128 = A.bf16([128, NBLK, 128]); b_v = Buf()
        for n0 in range(0, NBLK, 22):
            P.dma(v128[:, n0:n0 + 22, :], d["tm_hv"][n0 * 128:(n0 + 22) * 128, :].rearrange("(n p) d -> p n d", p=128),
                  writes=[b_v], eng="pool")
        of = A.f32([128, T]); b_of = [Buf() for _ in range(NBLK)]
        qt = A.bf16([128, T]); kt = A.bf16([128, T]); kh = A.bf16([128, T])
        b_qt, b_kt, b_kh = [Buf() for _ in range(6)], [Buf() for _ in range(6)], [Buf() for _ in range(6)]
        khtm = A.bf16([128, NBLK, 128]); b_khtm = [Buf() for _ in range(NBLK)]
        dec = A.f32([128, 264]); b_dec = [Buf() for _ in range(6)]
        rmask = A.f32([128, SEG]); b_rm = Buf()
        P.dma(rmask, d["rmask"], writes=[b_rm])
        sc = [A.f32([128, SEG]) for _ in range(6)]
        bsc = [Buf() for _ in range(6)]
        sA, sB, sC, sD, sT, sQ = sc
        bA, bB, bC, bD, bT, bQ = bsc
        S = [A.f32([128, 128]) for _ in range(2)]; bS = [Buf(), Buf()]
        Sb = [A.bf16([128, 128]) for _ in range(2)]; bSb = [Buf(), Buf()]
        ams = ASlots([A.bf16([128, 128]) for _ in range(3)])
        kms = ASlots([A.bf16([128, 128]) for _ in range(8)])
        lbt = A.f32([128, 8]); b_lb = Buf()
        for dd in range(2):
            if self.layer == 0:
                P.memset(lbt[:, dd:dd + 1], 0.0, [b_lb])
            else:
                P.tt(lbt[:, dd:dd + 1], sm[:, 2 + dd:3 + dd], sm[:, dd:dd + 1], ALU.subtract, [self.b_c], [b_lb])
                P.act(lbt[:, dd:dd + 1], lbt[:, dd:dd + 1], AF.Sigmoid, [b_lb], [b_lb])
            P.ts(lbt[:, 2 + dd:3 + dd], lbt[:, dd:dd + 1], -1.0, ALU.mult, [b_lb], [b_lb], s2=1.0, op1=ALU.add)
            P.ts(lbt[:, 4 + dd:5 + dd], lbt[:, dd:dd + 1], 1.0, ALU.subtract, [b_lb], [b_lb])
        identb = self.cb[:, C_IDENT, :]

        def v3(ap):
            return ap.rearrange("p (c w) -> p c w", w=32)

        for dd in range(2):
            lb, oml, noml = lbt[:, dd:dd + 1], lbt[:, 2 + dd:3 + dd], lbt[:, 4 + dd:5 + dd]
            for seg in range(NSEG):
                s0 = seg * SEG
                P.dma(sA, fm[5 + dd][:, s0:s0 + SEG], writes=[bA])
                P.dma(sQ, fm[4][:, s0:s0 + SEG], writes=[bQ])
                P.act(sA, sA, AF.Sigmoid, [bA], [bA])
                P.ts(sB, sA, noml, ALU.mult, [bA, b_lb], [bB], s2=oml, op1=ALU.add)
                P.ts(sA, sA, oml, ALU.mult, [bA, b_lb], [bA], s2=lb, op1=ALU.add)
                P.act(sA, sA, AF.Ln, [bA], [bA])
                P.add("dve", lambda e: e.tensor_tensor_scan(out=sC, data0=rmask, data1=sA, initial=0.0,
                                                             op0=ALU.mult, op1=ALU.add), [bA, b_rm], [bC])
                totb = v3(sC)[:, :, 31:32].broadcast_to([128, SEG // 32, 32])
                if dd == 0:
                    G, bG = sC, bC
                else:
                    P.tt(sD, sA, sC, ALU.subtract, [bA, bC], [bD])
                    P.tt(v3(sD), v3(sD), totb, ALU.add, [bD, bC], [bD])
                    G, bG = sD, bD
                P.act(dec[:, seg * 44:(seg + 1) * 44], v3(sC)[:, :, 31], AF.Exp, [bC], [b_dec[seg]])
                P.tt(v3(sT), totb, v3(G), ALU.subtract, [bC, bG], [bT])
                P.act(sT, sT, AF.Exp, [bT], [bT])
                P.tt(kh[:, s0:s0 + SEG], sB, sT, ALU.mult, [bB, bT], [b_kh[seg]])
                P.act(sT, G, AF.Exp, [bG], [bT], scale=-1.0)
                P.tt(kt[:, s0:s0 + SEG], sB, sT, ALU.mult, [bB, bT], [b_kt[seg]])
                P.act(sT, G, AF.Exp, [bG], [bT])
                P.stt(qt[:, s0:s0 + SEG], sQ, 128 ** -0.5, sT, ALU.mult, ALU.mult, [bQ, bT], [b_qt[seg]])
            if self.stage < 1:
                P.dma(d["yT"][2, :, 0:SEG], sT, reads=[bT])
                return
            for blk in range(NBLK):
                ptr, bptr = P.bank()
                P.mm(ptr[:, 0:128], kh[:, blk * 128:(blk + 1) * 128], identb, True, True,
                     [b_kh[blk // 11], self.b_c], [bptr])
                P.copy(khtm[:, blk, :], ptr[:, 0:128], [bptr], [b_khtm[blk]],
                       eng=("act" if blk % 2 == 0 else "dve"))
            if self.stage < 2:
                P.dma(d["yT"][2, :, 0:SEG], sT, reads=[bT] + b_khtm)
                return
            P.memset(S[0], 0.0, [bS[0]])
            P.memset(Sb[0], 0.0, [bSb[0]], eng="pool")
            n = 0
            hm = self.cf[:, C_HM0 + dd, :]
            if dd == 0:
                order = list(range(NBLK)); corder = [0, 1, 2, 3]
            else:
                order = [1, 0] + list(range(NBLK - 1, 1, -1)); corder = [3, 2, 1, 0]
            for bi, blk in enumerate(order):
                sg = blk // 11
                c0 = blk * 128
                pa, bpa = P.bank()
                P.mm(pa[:, 0:128], kt[:, c0:c0 + 128], qt[:, c0:c0 + 128], True, True, [b_kt[sg], b_qt[sg]], [bpa])
                am, bam = ams.next()
                P.tt(am, pa[:, 0:128], hm, ALU.mult, [bpa, self.b_c], [bam])
                po, bpo = P.psum[4 + bi % 2]
                P.mm(po[:, 0:128], v128[:, blk, :], am, True, False, [b_v, bam], [bpo])
                kvs = []
                for ci, cc in enumerate(corder):
                    km, bkm = kms.next()
                    if (n + ci) % 2 == 0:
                        P.ts(km, khtm[:, blk, :], self.cf[:, C_ROWM, cc:cc + 1], ALU.mult, [b_khtm[blk], self.b_c], [bkm])
                    else:
                        P.act(km, khtm[:, blk, :], AF.Identity, [b_khtm[blk], self.b_c], [bkm],
                              scale=self.cf[:, C_ROWM, cc:cc + 1])
                    pk, bpk = P.bank()
                    P.mm(pk[:, 0:128], km, v128[:, blk, :], True, True, [bkm, b_v], [bpk])
                    kvs.append((pk, bpk))
                for ci, cc in enumerate(corder):
                    pk, bpk = kvs[ci]
                    P.mm(po[:, cc * 32:(cc + 1) * 32], Sb[n % 2], qt[:, c0 + cc * 32:c0 + (cc + 1) * 32], False, ci == 3,
                         [bSb[n % 2], b_qt[sg]], [bpo])
                    P.stt(S[(n + 1) % 2], S[n % 2], dec[:, blk * 4 + cc:blk * 4 + cc + 1], pk[:, 0:128], ALU.mult, ALU.add,
                          [bS[n % 2], bpk, b_dec[sg]], [bS[(n + 1) % 2]])
                    P.copy(Sb[(n + 1) % 2], S[(n + 1) % 2], [bS[(n + 1) % 2]], [bSb[(n + 1) % 2]], eng="act")
                    n += 1
                if dd == 0:
                    P.copy(of[:, c0:c0 + 128], po[:, 0:128], [bpo], [b_of[blk]], eng="act")
                else:
                    P.tt(of[:, c0:c0 + 128], of[:, c0:c0 + 128], po[:, 0:128], ALU.add, [b_of[blk], bpo], [b_of[blk]])
        W = 384
        scr = (ASlots([sA[:, 0:W], sA[:, W:2 * W]]), ASlots([kt[:, 0:W], kt[:, W:2 * W]]),
               ASlots([sB[:, 0:W], sB[:, W:2 * W]]), ASlots([sC[:, 0:W], sC[:, W:2 * W]]),
               ASlots([sD[:, 0:W], sD[:, W:2 * W]]))
        P.barrier()
        self.readout(of, b_of, 7, sm[:, 4:5], 2, scr)


def prep_mixer_small(inp, l, hh):
    sm = np.zeros((128, 32), np.float32)
    ch = hh * 128 + np.arange(128)
    lbl = inp["hg_lb_logits"]
    sm[:, 0] = lbl[0, 0, ch]; sm[:, 1] = lbl[0, 1, ch]
    sm[:, 2] = lbl[1, 0, ch]; sm[:, 3] = lbl[1, 1, ch]
    sm[:, 4] = inp["hg_norm"][l]
    sm[:, 5] = inp["gdn_norm"][l]
    sm[:, 6] = inp["gdn_a_log"][l, 0, hh]; sm[:, 7] = inp["gdn_a_log"][l, 1, hh]
    sm[:, 8] = inp["gdn_dt_bias"][l, 0, hh]; sm[:, 9] = inp["gdn_dt_bias"][l, 1, hh]
    cw = inp["gdn_conv"][l]
    for ti in range(3):
        for tap in range(3):
            sm[:, 10 + ti * 3 + tap] = cw[tap, ti * 512 + ch]
    return sm


def token_core_inputs(inp, mode, core, h_tok, y_tok=None):
    b, j = core // 4, core % 4
    m = {"hT": tok_to_fm(h_tok), "ones": _CONST["ones"], "cT": _CONST["cT"][b]}
    if mode in ("mid", "last"):
        l = 0 if mode == "mid" else 1
        for k, v in _CONST[("ffn", l, 1)].items():
            m[k + "A"] = v
        m.update(_CONST[("merge", l)])
        m["yT"] = np.ascontiguousarray(y_tok.reshape(3, NTOK, 4, 128).transpose(3, 0, 2, 1).reshape(128, 12, NTOK))
    if mode in ("first", "mid"):
        l = 0 if mode == "first" else 1
        for k, v in _CONST[("ffn", l, 0)].items():
            m[k + "B"] = v
        m.update(_CONST[("proj", l)])
        m["rope_cos"], m["rope_sin"] = _CONST[("rope", j)]
    return m


_CONST = {}
_PROGS = {}


def _prepare(inp):
    _CONST.clear()
    _CONST["ones"] = np.ones((128, 128), np.float32)
    _CONST["cT"] = [np.ascontiguousarray(np.stack([_fm(inp["c"][b]), _fm(inp["c_ctx"])], -1)) for b in range(NB)]
    for l in range(DEPTH):
        for i in range(2):
            _CONST[("ffn", l, i)] = prep_ffn_set(inp, l, i)
        _CONST[("proj", l)] = prep_proj(inp, l)
        _CONST[("merge", l)] = prep_merge(inp, l)
    for j in range(4):
        _CONST[("rope", j)] = rope_tables(j)
    _CONST["consts"] = make_consts()
    rm = np.ones((128, 1408), np.float32)
    rm[:, ::32] = 0.0
    _CONST["rmask"] = rm


def _prog(key):
    if key not in _PROGS:
        if key[0] == "tok":
            _PROGS[key] = TokenPhase(key[1])
        else:
            _PROGS[key] = MixerPhase(key[1])
    return _PROGS[key]


def run_token(inp, mode, h_cores, y_cores=None):
    tp = _prog(("tok", mode))
    maps = [token_core_inputs(inp, mode, c, h_cores[c], None if y_cores is None else y_cores[c]) for c in range(NCORE)]
    res = run_bass_kernel_spmd(tp.nc, maps, core_ids=list(range(NCORE)))
    return res.results


def _seq_order(parts_ctx, parts_lat, axis):
    return np.concatenate(parts_ctx + parts_lat, axis=axis)


def mixer_inputs(inp, l, tok_res):
    maps = []
    for core in range(NCORE):
        b, hh = core // 4, core % 4
        rs = [tok_res[b * 4 + j] for j in range(4)]
        fm = _seq_order([r["fm"][hh][:, :, 0:64] for r in rs], [r["fm"][hh][:, :, 64:] for r in rs], 2)
        ab = _seq_order([r["tm_ab"][0:64] for r in rs], [r["tm_ab"][64:] for r in rs], 0)
        ab = np.ascontiguousarray(ab[:, hh * 4:(hh + 1) * 4])
        mv = _seq_order([r["tm_mv"][0:64, hh * 128:(hh + 1) * 128] for r in rs],
                        [r["tm_mv"][64:, hh * 128:(hh + 1) * 128] for r in rs], 0)
        hv = _seq_order([r["tm_hv"][0:64, hh * 128:(hh + 1) * 128] for r in rs],
                        [r["tm_hv"][64:, hh * 128:(hh + 1) * 128] for r in rs], 0)
        maps.append({"fm": np.ascontiguousarray(fm), "tm_ab": ab, "tm_mv": np.ascontiguousarray(mv),
                     "tm_hv": np.ascontiguousarray(hv), "consts": _CONST["consts"], "rmask": _CONST["rmask"],
                     "small": prep_mixer_small(inp, l, hh)})
    return maps


def run_mixer(inp, l, tok_res):
    mp = _prog(("mix", l))
    res = run_bass_kernel_spmd(mp.nc, mixer_inputs(inp, l, tok_res), core_ids=list(range(NCORE)))
    return res.results


def y_for_token_cores(mix_res):
    out = []
    for core in range(NCORE):
        b, j = core // 4, core % 4
        y = np.zeros((3, NTOK, 512), np.float32)
        for hh in range(4):
            yT = mix_res[b * 4 + hh]["yT"]
            ctx = yT[:, :, 64 * j:64 * j + 64]
            lat = yT[:, :, 256 + 2048 * j:256 + 2048 * (j + 1)]
            y[:, :, hh * 128:(hh + 1) * 128] = np.concatenate([ctx, lat], 2).transpose(0, 2, 1)
        out.append(y)
    return out


def _gdn(self):
    P, A, d = self.P, self.A, self.d
    P.barrier()
    A.reset()
    P.rot = [0, 1, 2, 3, 4, 5]
    T = TSEQ
    SEG = 1408
    fm = d["fm"]
    sm = self.small
    cf, cb = self.cf, self.cb
    b_c = self.b_c
    onesb, identb = cb[:, C_ONES, :], cb[:, C_IDENT, :]
    onesf, identf = cf[:, C_ONES, :], cf[:, C_IDENT, :]
    qT = A.bf16([128, T]); kT = A.bf16([128, T])
    b_qT = [Buf() for _ in range(6)]; b_kT = [Buf() for _ in range(6)]
    k_tm = A.bf16([128, NBLK, 128]); v_tm = A.bf16([128, NBLK, 128])
    b_ktm = [Buf() for _ in range(NBLK)]; b_vtm = [Buf() for _ in range(NBLK)]
    of = A.f32([128, T]); b_of = [Buf() for _ in range(NBLK)]
    nsm = A.f32([128, 32]); b_nsm = Buf()
    P.ts(nsm, sm[:, :], -1.0, ALU.mult, [b_c], [b_nsm])
    Xs = ASlots([A.f32([128, SEG + 2]) for _ in range(2)])
    Ys = ASlots([A.f32([128, SEG]) for _ in range(2)])
    vTs = ASlots([A.bf16([128, SEG]) for _ in range(2)])
    sqs = ASlots([A.bf16([128, 352]) for _ in range(2)])
    lns = ASlots([A.f32([128, 352]) for _ in range(2)])
    rs = ASlots([A.f32([128, 352]) for _ in range(2)])
    for ti in range(3):
        for seg in range(6):
            s0 = seg * SEG
            X, bX = Xs.next()
            lo = max(s0 - 1, 0)
            hi = min(s0 + SEG + 1, T)
            if seg == 0:
                P.memset(X[:, 0:1], 0.0, [bX])
            if seg == 5:
                P.memset(X[:, SEG + 1:SEG + 2], 0.0, [bX])
            P.dma(X[:, lo - (s0 - 1):hi - (s0 - 1)], fm[ti][:, lo:hi], writes=[bX])
            Y, bY = Ys.next()
            wc = 10 + ti * 3
            P.ts(Y, X[:, 1:SEG + 1], sm[:, wc + 1:wc + 2], ALU.mult, [bX, b_c], [bY])
            P.stt(Y, X[:, 0:SEG], sm[:, wc:wc + 1], Y, ALU.mult, ALU.add, [bX, b_c, bY], [bY])
            P.stt(Y, X[:, 2:SEG + 2], sm[:, wc + 2:wc + 3], Y, ALU.mult, ALU.add, [bX, b_c, bY], [bY])
            if seg == 0:
                P.stt(Y[:, 255:256], X[:, 257:258], nsm[:, wc + 2:wc + 3], Y[:, 255:256], ALU.mult, ALU.add,
                      [bX, b_nsm, bY], [bY])
                P.stt(Y[:, 256:257], X[:, 256:257], nsm[:, wc:wc + 1], Y[:, 256:257], ALU.mult, ALU.add,
                      [bX, b_nsm, bY], [bY])
            P.act(Y, Y, AF.Silu, [bY], [bY])
            if ti < 2:
                dst, bdst = (qT, b_qT) if ti == 0 else (kT, b_kT)
                for t0 in range(0, SEG, 352):
                    sq, bsq = sqs.next()
                    P.act(sq, Y[:, t0:t0 + 352], AF.Square, [bY], [bsq])
                    pt, bp = P.bank()
                    P.mm(pt[:, 0:352], onesb, sq, True, True, [b_c, bsq], [bp])
                    ln, bln = lns.next()
                    P.act(ln, pt[:, 0:352], AF.Ln, [bp], [bln], bias=EPS)
                    r, br = rs.next()
                    P.act(r, ln, AF.Exp, [bln], [br], scale=-0.5, bias=(math.log(128 ** -0.5) if ti == 0 else 0.0))
                    P.tt(dst[:, s0 + t0:s0 + t0 + 352], Y[:, t0:t0 + 352], r, ALU.mult, [bY, br], [bdst[seg]])
                if ti == 1:
                    for bb in range(11):
                        blk = seg * 11 + bb
                        ptr, bptr = P.bank()
                        P.mm(ptr[:, 0:128], kT[:, blk * 128:(blk + 1) * 128], identb, True, True, [b_kT[seg], b_c], [bptr])
                        P.copy(k_tm[:, blk, :], ptr[:, 0:128], [bptr], [b_ktm[blk]], eng=("act" if bb % 2 else "dve"))
            else:
                vT, bvT = vTs.next()
                P.copy(vT, Y, [bY], [bvT], eng="pool")
                for bb in range(11):
                    blk = seg * 11 + bb
                    ptr, bptr = P.bank()
                    P.mm(ptr[:, 0:128], vT[:, bb * 128:(bb + 1) * 128], identb, True, True, [bvT, b_c], [bptr])
                    P.copy(v_tm[:, blk, :], ptr[:, 0:128], [bptr], [b_vtm[blk]], eng=("act" if bb % 2 else "dve"))
    ab = A.f32([128, NBLK, 4]); b_ab = Buf()
    P.dma(ab, d["tm_ab"].rearrange("(n p) c -> p n c", p=128), writes=[b_ab])
    cols = {}
    for dd in range(2):
        g = A.f32([128, NBLK]); bg = Buf()
        tmpc = A.f32([128, NBLK]); btmp = Buf()
        nA = A.f32([128, 1]); bnA = Buf()
        P.act(tmpc, ab[:, :, dd], AF.Exp, [b_ab, b_c], [btmp], bias=sm[:, 8 + dd:9 + dd])
        P.act(tmpc, tmpc, AF.Ln, [btmp], [btmp], bias=1.0)
        P.act(nA, sm[:, 6 + dd:7 + dd], AF.Exp, [b_c], [bnA])
        P.ts(nA, nA, -1.0, ALU.mult, [bnA], [bnA])
        P.ts(g, tmpc, nA, ALU.mult, [btmp, bnA], [bg])
        beta = A.f32([128, NBLK]); bbeta = Buf()
        P.act(beta, ab[:, :, 2 + dd], AF.Sigmoid, [b_ab], [bbeta])
        gcum = A.f32([128, NBLK]); negg = A.f32([128, NBLK]); negeg = A.f32([128, NBLK])
        ekl = A.f32([128, NBLK]); decS = A.f32([128, NBLK]); bcol = Buf()
        pt, bp = P.bank()
        P.mm(pt[:, 0:NBLK], cf[:, C_TRI0 + dd, :], g, True, True, [b_c, bg], [bp])
        P.copy(gcum, pt[:, 0:NBLK], [bp], [bcol])
        P.ts(negg, gcum, -1.0, ALU.mult, [bcol], [bcol])
        P.act(negeg, gcum, AF.Exp, [bcol], [bcol])
        P.ts(negeg, negeg, -1.0, ALU.mult, [bcol], [bcol])
        pt2, bp2 = P.bank()
        P.mm(pt2[:, 0:NBLK], onesf, g, True, True, [b_c, bg], [bp2])
        P.act(decS, pt2[:, 0:NBLK], AF.Exp, [bp2], [bcol])
        P.tt(ekl, pt2[:, 0:NBLK], gcum, ALU.subtract, [bp2, bcol], [bcol])
        P.act(ekl, ekl, AF.Exp, [bcol], [bcol])
        cols[dd] = dict(gcum=gcum, negg=negg, negeg=negeg, ekl=ekl, decS=decS, beta=beta, b=[bcol, bbeta])
    def mk(n, dt, cnt):
        return ASlots([(A.f32([128, 128]) if dt == "f" else A.bf16([128, 128])) for _ in range(cnt)])
    st = {}
    for dd in range(2):
        st[dd] = dict(
            dg=mk(0, "f", 2), Dm=mk(0, "f", 2), ET=mk(0, "f", 2), EGB=mk(0, "f", 2), ETs=mk(0, "f", 2),
            attnT=mk(0, "b", 3), N=mk(0, "b", 2), qg=mk(0, "b", 3), khat=mk(0, "b", 3), NB=mk(0, "b", 2),
            Tm=mk(0, "b", 3), Um=mk(0, "b", 3), Ufin=mk(0, "b", 3), R0=mk(0, "b", 2), vnew=mk(0, "b", 2),
            S=[A.f32([128, 128]) for _ in range(2)], bS=[Buf(), Buf()],
            Sb=[A.bf16([128, 128]) for _ in range(2)], bSb=[Buf(), Buf()], n=0, pend=None)
        P.memset(st[dd]["S"][0], 0.0, [st[dd]["bS"][0]])
        P.memset(st[dd]["Sb"][0], 0.0, [st[dd]["bSb"][0]], eng="pool")
    orders = {0: list(range(NBLK)), 1: [1, 0] + list(range(NBLK - 1, 1, -1))}
    written = set()

    def offchain(dd, blk):
        s = st[dd]
        cl = cols[dd]
        bcl = cl["b"]
        sg = blk // 11
        c0 = blk * 128
        kTb, qTb = kT[:, c0:c0 + 128], qT[:, c0:c0 + 128]
        pKK, bKK = P.bank()
        P.mm(pKK[:, 0:128], kTb, kTb, True, True, [b_kT[sg]], [bKK])
        pQK, bQK = P.bank()
        P.mm(pQK[:, 0:128], kTb, qTb, True, True, [b_kT[sg], b_qT[sg]], [bQK])
        dg, bdg = s["dg"].next()
        P.ts(dg, identf, cl["gcum"][:, blk:blk + 1], ALU.mult, [b_c] + bcl, [bdg])
        pG, bG = P.bank()
        P.mm(pG[:, 0:128], onesf, dg, True, True, [b_c, bdg], [bG])
        Dm, bDm = s["Dm"].next()
        P.stt(Dm, pG[:, 0:128], cl["negg"][:, blk:blk + 1], cf[:, C_NEG0 + dd, :], ALU.add, ALU.add, [bG, b_c] + bcl, [bDm])
        ET, bET = s["ET"].next()
        P.act(ET, Dm, AF.Exp, [bDm], [bET])
        EGB, bEGB = s["EGB"].next()
        P.act(EGB, pG[:, 0:128], AF.Exp, [bG], [bEGB])
        ETs, bETs = s["ETs"].next()
        P.tt(ETs, ET, cf[:, C_STR0 + dd, :], ALU.mult, [bET, b_c], [bETs], eng="pool")
        attnT, battn = s["attnT"].next()
        P.tt(attnT, pQK[:, 0:128], ET, ALU.mult, [bQK, bET], [battn])
        N, bN = s["N"].next()
        P.stt(N, pKK[:, 0:128], cl["beta"][:, blk:blk + 1], ETs, ALU.mult, ALU.mult, [bKK, bETs] + bcl, [bN])
        qg, bqg = s["qg"].next()
        P.tt(qg, qTb, EGB, ALU.mult, [b_qT[sg], bEGB], [bqg], eng="pool")
        khat, bkhat = s["khat"].next()
        P.ts(khat, k_tm[:, blk, :], cl["ekl"][:, blk:blk + 1], ALU.mult, [b_ktm[blk]] + bcl, [bkhat], eng="pool")
        yield
        Tc, bTc = identb, b_c
        Uc, bUc = identb, b_c
        for l in range(7):
            pB, bB = P.bank()
            P.mm(pB[:, 0:128], N, Tc, True, True, [bN, bTc], [bB])
            NB, bNB = s["NB"].next()
            P.tt(NB, pB[:, 0:128], cf[:, C_LVL + dd * 7 + l, :], ALU.mult, [bB, b_c], [bNB])
            if l < 6:
                pT, bT_ = P.bank()
                P.mm(pT[:, 0:128], Uc, NB, True, False, [bUc, bNB], [bT_])
                P.mm(pT[:, 0:128], identb, Tc, False, True, [b_c, bTc], [bT_])
                Tn, bTn = s["Tm"].next()
                P.copy(Tn, pT[:, 0:128], [bT_], [bTn], eng="act")
            pU, bU_ = P.bank()
            P.mm(pU[:, 0:128], NB, Uc, True, False, [bNB, bUc], [bU_])
            P.mm(pU[:, 0:128], identb, Uc, False, True, [b_c, bUc], [bU_])
            Un, bUn = (s["Um"] if l < 6 else s["Ufin"]).next()
            P.copy(Un, pU[:, 0:128], [bU_], [bUn], eng="act")
            if l < 6:
                Tc, bTc = Tn, bTn
            Uc, bUc = Un, bUn
            yield
        s["newp"] = dict(blk=blk, U=Uc, bU=bUc, attnT=attnT, battn=battn, qg=qg, bqg=bqg, khat=khat, bkhat=bkhat)

    def inchain(dd, step):
        s = st[dd]
        pd = s["pend"]
        if pd is None:
            return
        s["pend"] = None
        cl = cols[dd]
        bcl = cl["b"]
        blk = pd["blk"]
        sg = blk // 11
        c0 = blk * 128
        n = s["n"]
        cur, nxt = n % 2, (n + 1) % 2
        pkS, bkS = P.bank()
        P.mm(pkS[:, 0:128], kT[:, c0:c0 + 128], s["Sb"][cur], True, True, [b_kT[sg], s["bSb"][cur]], [bkS])
        po, bpo = P.psum[6 + dd]
        P.mm(po[:, 0:128], s["Sb"][cur], pd["qg"], True, False, [s["bSb"][cur], pd["bqg"]], [bpo])
        R0, bR0 = s["R0"].next()
        P.stt(R0, pkS[:, 0:128], cl["negeg"][:, blk:blk + 1], v_tm[:, blk, :], ALU.mult, ALU.add,
              [bkS, b_vtm[blk]] + bcl, [bR0])
        pV, bV = P.bank()
        P.mm(pV[:, 0:128], pd["U"], R0, True, True, [pd["bU"], bR0], [bV])
        vnew, bvn = s["vnew"].next()
        P.act(vnew, pV[:, 0:128], AF.Identity, [bV] + bcl, [bvn], scale=cl["beta"][:, blk:blk + 1])
        pKV, bKV = P.bank()
        P.mm(pKV[:, 0:128], pd["khat"], vnew, True, True, [pd["bkhat"], bvn], [bKV])
        P.mm(po[:, 0:128], vnew, pd["attnT"], False, True, [bvn, pd["battn"]], [bpo])
        P.stt(s["S"][nxt], s["S"][cur], cl["decS"][:, blk:blk + 1], pKV[:, 0:128], ALU.mult, ALU.add,
              [s["bS"][cur], bKV] + bcl, [s["bS"][nxt]])
        P.copy(s["Sb"][nxt], s["S"][nxt], [s["bS"][nxt]], [s["bSb"][nxt]], eng="act")
        if blk not in written:
            written.add(blk)
            P.copy(of[:, c0:c0 + 128], po[:, 0:128], [bpo], [b_of[blk]], eng="dve")
        else:
            P.tt(of[:, c0:c0 + 128], of[:, c0:c0 + 128], po[:, 0:128], ALU.add, [b_of[blk], bpo], [b_of[blk]])
        s["n"] = n + 1

    dirs = [0, 1]
    if self.stage == 11:
        dirs = [0]
    if self.stage == 12:
        dirs = [1]
    for step in range(NBLK + 1):
        if step < NBLK:
            gens = [offchain(dd, orders[dd][step]) for dd in dirs]
            live = list(gens)
            while live:
                for gen_ in list(live):
                    try:
                        next(gen_)
                    except StopIteration:
                        live.remove(gen_)
        for dd in dirs:
            inchain(dd, step)
            st[dd]["pend"] = st[dd].pop("newp", None) if step < NBLK else None
    P.barrier()
    if self.stage in (10, 11, 12):
        for c0 in range(0, T, 2112):
            P.dma(d["yT"][0, :, c0:c0 + 2112], of[:, c0:c0 + 2112], reads=b_of)
        return
    W = 384
    scrA, scrC, scrD, scrE = Xs.t[0], Ys.t[0], Xs.t[1], Ys.t[1]
    scrB = vTs.t[0]
    scr = tuple(ASlots([x[:, 0:W], x[:, W:2 * W]]) for x in (scrA, scrB, scrC, scrD, scrE))
    self.readout(of, b_of, 3, sm[:, 5:6], 0, scr)


MixerPhase.gdn = _gdn


class Fused:
    def __init__(self):
        nc = self.nc = bass.Bass("TRN2", target_bir_lowering=False)
        self.d = d = {}

        def din(name, shape):
            d[name] = nc.dram_tensor(name, list(shape), F32, kind="ExternalInput").ap()

        def dscr(name, shape, dtype=F32):
            d[name] = nc.dram_tensor(name, list(shape), dtype, kind="Internal").ap()

        din("hT", [4, 128, KC, NTOK])
        din("cT", [128, KC, 2])
        din("consts", [NCONST, 128, 128])
        din("rmask", [128, 1408])
        din("small", [8, 128, 32])
        din("rope_cos", [4, 64, NTOK])
        din("rope_sin", [4, 64, NTOK])
        for l in range(DEPTH):
            din(f"wada{l}", [24, 128, 8 * 384]); din(f"bada{l}", [128, 72]); din(f"normw{l}", [128, 6, 8])
            for i in range(2):
                din(f"w1_{l}{i}", [NFC, 128, 2048]); din(f"w2_{l}{i}", [8, 128, NFC * 128])
            for nm, shp in (("wp", [38, 128, 1024]), ("wt", [128, 8 * 528]), ("wq", [128, 3 * 1024]),
                            ("wkn", [128, 2 * 512]), ("wkv", [128, 2 * 512]), ("qnw", [128, 3]), ("kvnw", [128, 2]),
                            ("wg", [8, 128, 3 * 8 * 128]), ("wb", [8, 128, 12 * 128]), ("wo", [8, 128, 1024])):
                din(f"{nm}{l}", shp)
        d["out"] = nc.dram_tensor("out", [4, 128, KC, NTOK], F32, kind="ExternalOutput").ap()
        dscr("hbuf", [4, 128, KC, NTOK])
        dscr("fm_seq", [4, NFM, 128, TSEQ])
        dscr("tm_ab_seq", [4, TSEQ, 4])
        dscr("tm_mv_seq", [4, TSEQ, 128])
        dscr("tm_hv_seq", [4, TSEQ, 128])
        dscr("yseq", [4, 3, 128, TSEQ])
        with ExitStack() as st:
            self.P = P = Prog(nc, st)
            P.arena = None
            P.init_psum(8)
            self.cf = P.sb("cf", [128, NCONST, 128], F32, persistent=True)
            self.cb = P.sb("cb", [128, NCONST, 128], BF16, persistent=True)
            self.b_c = Buf()
            P.dma(self.cf[:], d["consts"].rearrange("n p f -> p n f"), writes=[self.b_c])
            P.dma(self.cb[:], d["consts"].rearrange("n p f -> p n f"), writes=[self.b_c], eng="pool")
            self.smalls = P.sb("smalls", [128, 8, 32], F32, persistent=True)
            P.dma(self.smalls[:], d["small"].rearrange("n p f -> p n f"), writes=[self.b_c])
            Sper = [P.sb(f"Sper{l}", [128, 9, 8, 2], F32, persistent=True) for l in range(DEPTH)]
            bSper = [Buf() for _ in range(DEPTH)]
            self.A = Arena(P, 178)
            P.arena = self.A
            for l in range(DEPTH):
                P.barrier()
                self.A.reset()
                tp = TokenPhase.__new__(TokenPhase)
                tp.P = P
                tp.d = {"cT": d["cT"], "wadaM": d[f"wada{l}"], "badaM": d[f"bada{l}"], "normwM": d[f"normw{l}"]}
                S, bS = tp.compute_mod("M")
                P.copy(Sper[l][:], S, [bS], [bSper[l]])
            mods_l = [(Sper[l], bSper[l]) for l in range(DEPTH)]

            def tok_d(mode, j, lA, lB, src, dst):
                dd = {"hT": d[src][j], "hT_out": d[dst][j], "fm_seq": d["fm_seq"], "tm_ab_seq": d["tm_ab_seq"],
                      "tm_mv_seq": d["tm_mv_seq"], "tm_hv_seq": d["tm_hv_seq"], "yseq": d["yseq"],
                      "rope_cos": d["rope_cos"][j], "rope_sin": d["rope_sin"][j]}
                if lA is not None:
                    dd.update({"w1A": d[f"w1_{lA}1"], "w2A": d[f"w2_{lA}1"], "wg": d[f"wg{lA}"], "wb": d[f"wb{lA}"],
                               "wo": d[f"wo{lA}"]})
                if lB is not None:
                    dd.update({"w1B": d[f"w1_{lB}0"], "w2B": d[f"w2_{lB}0"]})
                    for nm in ("wp", "wt", "wq", "wkn", "wkv", "qnw", "kvnw"):
                        dd[nm] = d[f"{nm}{lB}"]
                return dd

            def token_phase(mode, lA, lB, src, dst):
                for j in range(4):
                    P.barrier()
                    self.A.reset()
                    mods = {}
                    if lA is not None:
                        mods["A"] = mods_l[lA]
                    if lB is not None:
                        mods["B"] = mods_l[lB]
                    TokenPhase(mode, ext=self, d=tok_d(mode, j, lA, lB, src, dst), mods=mods, shard=j)

            def mixer_phase(l):
                for hh in range(4):
                    md = {"fm": d["fm_seq"][hh], "tm_ab": d["tm_ab_seq"][hh], "tm_mv": d["tm_mv_seq"][hh],
                          "tm_hv": d["tm_hv_seq"][hh], "yT": d["yseq"][hh], "rmask": d["rmask"]}
                    MixerPhase(l, ext=self, d=md, small=self.smalls[:, l * 4 + hh, :])

            token_phase("first", None, 0, "hT", "hbuf")
            mixer_phase(0)
            token_phase("mid", 0, 1, "hbuf", "hbuf")
            mixer_phase(1)
            token_phase("last", 1, None, "hbuf", "out")
            P.barrier()
            P.emit()


def fused_inputs(inp, b):
    m = {"cT": _CONST["cT"][b], "consts": _CONST["consts"], "rmask": _CONST["rmask"]}
    h = []
    for j in range(4):
        h.append(tok_to_fm(np.concatenate([inp["ctx"][b, 64 * j:64 * j + 64], inp["x"][b, 2048 * j:2048 * (j + 1)]], 0)))
    m["hT"] = np.stack(h)
    m["small"] = np.stack([prep_mixer_small(inp, l, hh) for l in range(DEPTH) for hh in range(4)])
    m["rope_cos"] = np.stack([_CONST[("rope", j)][0] for j in range(4)])
    m["rope_sin"] = np.stack([_CONST[("rope", j)][1] for j in range(4)])
    for l in range(DEPTH):
        f0, f1 = _CONST[("ffn", l, 0)], _CONST[("ffn", l, 1)]
        m[f"wada{l}"], m[f"bada{l}"], m[f"normw{l}"] = f0["wada"], f0["bada"], f0["normw"]
        m[f"w1_{l}0"], m[f"w2_{l}0"] = f0["w1"], f0["w2"]
        m[f"w1_{l}1"], m[f"w2_{l}1"] = f1["w1"], f1["w2"]
        for k, v in _CONST[("proj", l)].items():
            m[f"{k}{l}"] = v
        for k, v in _CONST[("merge", l)].items():
            m[f"{k}{l}"] = v
    return m


def kernel(**inp):
    inp = {k: np.asarray(v) for k, v in inp.items()}
    _prepare(inp)
    if "fused" not in _PROGS:
        _PROGS["fused"] = Fused()
    fz = _PROGS["fused"]
    maps = [fused_inputs(inp, b) for b in range(NB)]
    res = run_bass_kernel_spmd(fz.nc, maps, core_ids=list(range(NB)))
    out = np.zeros((NB, SEQ, D), np.float32)
    for b in range(NB):
        o = res.results[b]["out"]
        for j in range(4):
            out[b, 2048 * j:2048 * (j + 1)] = fm_to_tok(o[j])[64:]
    return out
```

```python
import math
import numpy as np
from contextlib import ExitStack
import concourse.bass as bass
import concourse.mybir as mybir
from concourse.bass_utils import run_bass_kernel_spmd

F32 = mybir.dt.float32
BF16 = mybir.dt.bfloat16
AF = mybir.ActivationFunctionType
ALU = mybir.AluOpType

D = 1024
KC = 8
DFF = 2816
NFC = 22
DEPTH = 2
NB = 2
SEQ = 8192
CTX = 256
NCORE = 8
NTOK = 2112
TSEQ = CTX + SEQ
EPS = 1e-6
TILES = [(0, 64, 1), (64, 512, 0), (576, 512, 0), (1088, 512, 0), (1600, 512, 0)]
GROUPS = [[0, 1], [2], [3], [4]]
GOFF = [0, 576, 1088, 1600]
GW = [576, 512, 512, 512]
NG = 576
NFM = 11
N_DMA_SEM = 24
import os as _os
WENG = _os.environ.get('TOK_WENG', 'pool')


class Buf:
    __slots__ = ("name", "lw", "readers")

    def __init__(self, name=""):
        self.name = name
        self.lw = None
        self.readers = []


class Op:
    __slots__ = ("eng", "fn", "deps", "marked", "mark_no", "is_dma", "dsem", "dval", "dprev", "epoch", "sep")

    def __init__(self, eng, fn, is_dma):
        self.epoch = 0
        self.sep = 0
        self.eng = eng
        self.fn = fn
        self.deps = []
        self.marked = False
        self.mark_no = 0
        self.is_dma = is_dma
        self.dsem = None
        self.dval = 0
        self.dprev = None


class Prog:
    ENGS = ("sync", "act", "dve", "pool", "pe")

    def __init__(self, nc, stack):
        self.nc = nc
        self.stack = stack
        self.ops = {e: [] for e in self.ENGS}
        self.n_dma = 0
        self.dma_last = [None] * N_DMA_SEM
        self.dma_cnt = [0] * N_DMA_SEM
        self._uid = 0
        self.psum = []
        self.psum_i = 0
        self.rot = None
        self.epoch = 0
        self.sep = {e: 0 for e in self.ENGS}
        self.sep_start = {e: 0 for e in self.ENGS}

    def sb(self, name, shape, dtype=F32, persistent=False):
        if getattr(self, "arena", None) is not None and not persistent:
            return self.arena.bf16(list(shape)) if dtype == BF16 else self.arena.f32(list(shape))
        self._uid += 1
        return self.stack.enter_context(self.nc.sbuf_tensor(f"{name}_{self._uid}", list(shape), dtype))

    def ps(self, name, shape, dtype=F32):
        self._uid += 1
        return self.stack.enter_context(self.nc.psum_tensor(f"{name}_{self._uid}", list(shape), dtype))

    def init_psum(self, n=8):
        for i in range(n):
            self.psum.append((self.ps(f"bank{i}", [128, 512], F32), Buf(f"bank{i}")))

    def bank(self):
        rot = self.rot if self.rot is not None else list(range(len(self.psum)))
        t = self.psum[rot[self.psum_i % len(rot)]]
        self.psum_i += 1
        return t

    def barrier(self):
        lasts = []
        for e in self.ENGS:
            for op in reversed(self.ops[e]):
                if not op.is_dma and op.fn is not None:
                    lasts.append(op)
                    break
        lasts += [o for o in self.dma_last if o is not None]
        for e in self.ENGS:
            op = Op(e, None, False)
            op.epoch = self.epoch
            op.sep = self.sep[e]
            for d in lasts:
                op.deps.append(d)
                if not d.is_dma:
                    d.marked = True
            self.ops[e].append(op)
        self.epoch += 1
        for e in self.ENGS:
            n = sum(1 for o in self.ops[e][self.sep_start[e]:] if o.marked and not o.is_dma)
            if n > 6000:
                self.sep[e] += 1
                self.sep_start[e] = len(self.ops[e])

    def add(self, eng, fn, reads=(), writes=(), is_dma=False):
        op = Op(eng, fn, is_dma)
        op.epoch = self.epoch
        op.sep = self.sep[eng]
        deps = []
        for b in reads:
            if b.lw is not None:
                deps.append(b.lw)
        for b in writes:
            if b.lw is not None:
                deps.append(b.lw)
            deps.extend(b.readers)
        seen = set()
        for d in deps:
            if id(d) in seen or d.epoch < self.epoch:
                continue
            if eng == "pe" and d.eng == "pe" and not d.is_dma:
                continue
            seen.add(id(d))
            op.deps.append(d)
            if not d.is_dma:
                d.marked = True
        for b in reads:
            b.readers.append(op)
        for b in writes:
            b.lw = op
            b.readers = []
        if is_dma:
            s = self.n_dma % N_DMA_SEM
            self.n_dma += 1
            op.dsem = s
            op.dprev = self.dma_last[s]
            self.dma_cnt[s] += 16
            op.dval = self.dma_cnt[s]
            self.dma_last[s] = op
        self.ops[eng].append(op)
        return op

    def dma(self, out_ap, in_ap, reads=(), writes=(), eng="sync"):
        return self.add(eng, lambda e: e.dma_start(out=out_ap, in_=in_ap), reads, writes, True)

    def mm(self, out, lhsT, rhs, start, stop, reads, writes):
        return self.add("pe", lambda e: e.matmul(out, lhsT, rhs, start=start, stop=stop), reads, writes)

    def tr(self, out, in_, ident, reads, writes):
        return self.add("pe", lambda e: e.transpose(out, in_, ident), reads, writes)

    def act(self, out, in_, func, reads, writes, scale=None, bias=None, accum_out=None):
        kw = {}
        if scale is not None:
            kw["scale"] = scale
        if bias is not None:
            kw["bias"] = bias
        if accum_out is not None:
            kw["accum_out"] = accum_out
        return self.add("act", lambda e: e.activation(out=out, in_=in_, func=func, **kw), reads, writes)

    def tt(self, out, in0, in1, op, reads, writes, eng="dve"):
        return self.add(eng, lambda e: e.tensor_tensor(out=out, in0=in0, in1=in1, op=op), reads, writes)

    def ts(self, out, in0, s1, op0, reads, writes, s2=None, op1=None, eng="dve", accum_out=None):
        if op1 is None:
            return self.add(eng, lambda e: e.tensor_scalar(out=out, in0=in0, scalar1=s1, scalar2=None, op0=op0,
                                                            accum_out=accum_out), reads, writes)
        return self.add(eng, lambda e: e.tensor_scalar(out=out, in0=in0, scalar1=s1, scalar2=s2, op0=op0, op1=op1,
                                                        accum_out=accum_out), reads, writes)

    def stt(self, out, in0, scalar, in1, op0, op1, reads, writes):
        return self.add("dve", lambda e: e.scalar_tensor_tensor(out=out, in0=in0, scalar=scalar, in1=in1,
                                                                 op0=op0, op1=op1), reads, writes)

    def copy(self, out, in_, reads, writes, eng="dve"):
        if eng == "act":
            return self.add("act", lambda e: e.copy(out=out, in_=in_), reads, writes)
        return self.add(eng, lambda e: e.tensor_copy(out=out, in_=in_), reads, writes)

    def memset(self, ap, val, writes, eng="dve"):
        return self.add(eng, lambda e: e.memset(ap, val), (), writes)

    def emit(self):
        nc = self.nc
        st = self.stack
        esem = {(e, ep): st.enter_context(nc.semaphore(f"s_{e}_{ep}")) for e in self.ENGS
                for ep in range(self.sep[e] + 1)}
        dsem = [st.enter_context(nc.semaphore(f"s_dma{i}")) for i in range(N_DMA_SEM)]
        for e in self.ENGS:
            cnt = {}
            for op in self.ops[e]:
                if op.marked and not op.is_dma:
                    cnt[op.sep] = cnt.get(op.sep, 0) + 1
                    op.mark_no = cnt[op.sep]
        block = st.enter_context(nc.Block())
        handles = {"sync": block.sync, "act": block.scalar, "dve": block.vector,
                   "pool": block.gpsimd, "pe": block.tensor}
        final = [(dsem[i], self.dma_cnt[i]) for i in range(N_DMA_SEM) if self.dma_cnt[i] > 0]

        def make(ename):
            ops = self.ops[ename]

            def body(eng):
                waited = {}

                def wait(sem, key, val):
                    if waited.get(key, 0) >= val:
                        return
                    waited[key] = val
                    eng.wait_ge(sem, val)

                for op in ops:
                    for d in op.deps:
                        if d.is_dma:
                            wait(dsem[d.dsem], ("d", d.dsem), d.dval)
                        else:
                            wait(esem[(d.eng, d.sep)], ("e", d.eng, d.sep), d.mark_no)
                    if op.is_dma and op.dprev is not None:
                        wait(dsem[op.dsem], ("d", op.dsem), op.dprev.dval)
                    if op.fn is None:
                        continue
                    ins = op.fn(eng)
                    if op.is_dma:
                        ins.then_inc(dsem[op.dsem], 16)
                    elif op.marked:
                        ins.then_inc(esem[(ename, op.sep)], 1)
                if ename == "sync":
                    for sem, val in final:
                        eng.wait_ge(sem, val)
            return body

        for e in self.ENGS:
            handles[e](make(e))


class Slots:
    def __init__(self, P, name, shape, dtype, n):
        self.t = [P.sb(f"{name}{i}", shape, dtype) for i in range(n)]
        self.b = [Buf(f"{name}{i}") for i in range(n)]
        self.i = 0

    def next(self):
        k = self.i % len(self.t)
        self.i += 1
        return self.t[k], self.b[k]


class TokenPhase:
    def __init__(self, mode, ext=None, d=None, mods=None, shard=0):
        self.mode = mode
        self.ext = ext
        self.shard = shard
        self.do_merge = mode in ("mid", "last")
        self.do_proj = mode in ("first", "mid")
        if ext is not None:
            self.nc = ext.nc
            self.P = ext.P
            self.d = d
            self.mods = mods
            self.sets = [x for x in ("A", "B") if x in mods]
            self.build()
            return
        nc = self.nc = bass.Bass("TRN2", target_bir_lowering=False)
        dt = nc.dram_tensor
        self.d = {}

        import os
        wbf = os.environ.get("TOK_W_BF16") == "1"

        def din(name, shape):
            isw = wbf and name[:2] in ("w1", "w2", "wp", "wt", "wq", "wk", "wg", "wb", "wo", "wa")
            self.d[name] = dt(name, list(shape), BF16 if isw else F32, kind="ExternalInput").ap()

        def dout(name, shape, dtype=F32):
            self.d[name] = dt(name, list(shape), dtype, kind="ExternalOutput").ap()

        din("hT", [128, KC, NTOK])
        din("cT", [128, KC, 2])
        din("ones", [128, 128])
        dout("hT_out", [128, KC, NTOK])
        sets = []
        if self.do_merge:
            sets.append("A")
            din("yT", [128, 12, NTOK])
            for nm, shp in (("wg", [8, 128, 3 * 8 * 128]), ("wb", [8, 128, 12 * 128]), ("wo", [8, 128, 1024])):
                din(nm, shp)
        if self.do_proj:
            sets.append("B")
            for nm, shp in (("wp", [38, 128, 1024]), ("wt", [128, 8 * 528]), ("wq", [128, 3 * 1024]),
                            ("wkn", [128, 2 * 512]), ("wkv", [128, 2 * 512]), ("qnw", [128, 3]), ("kvnw", [128, 2]),
                            ("rope_cos", [64, NTOK]), ("rope_sin", [64, NTOK])):
                din(nm, shp)
            dout("fm", [4, NFM, 128, NTOK])
            dout("tm_ab", [NTOK, 16])
            dout("tm_mv", [NTOK, 512])
            dout("tm_hv", [NTOK, 512])
        for s in sets:
            din(f"wada{s}", [24, 128, 8 * 384])
            din(f"bada{s}", [128, 72])
            din(f"normw{s}", [128, 6, 8])
            din(f"w1{s}", [NFC, 128, 2048])
            din(f"w2{s}", [8, 128, NFC * 128])
        self.sets = sets
        with ExitStack() as st:
            self.P = P = Prog(nc, st)
            self.build()
            P.emit()

    def build(self):
        P = self.P
        d = self.d
        if self.ext is None:
            P.init_psum(8)
            self.ones = P.sb("ones", [128, 128], BF16)
            self.b_ones = Buf()
            P.dma(self.ones[:], d["ones"], writes=[self.b_ones], eng="pool")
        else:
            P.rot = None
            self.ones = self.ext.cb[:, C_ONES, :]
            self.b_ones = self.ext.b_c
        self.h = P.sb("h", [128, KC, NG], F32)
        self.hn = P.sb("hn", [128, KC, NG], BF16)
        self.a = P.sb("a", [128, NFC, NG], BF16)
        self.y = P.sb("y", [128, KC, NG], F32)
        self.bh = [Buf() for _ in range(2)]
        self.bhn = [Buf() for _ in range(2)]
        self.ba = [[Buf() for _ in range(2)] for _ in range(NFC)]
        self.by = [[Buf() for _ in range(2)] for _ in range(KC)]
        self.sq = Slots(P, "sq", [128, KC, 512], BF16, 1)
        self.tmp = Slots(P, "tmp", [128, 512], F32, 6)
        self.rstd = Slots(P, "rstd", [128, 512], F32, 2)
        self.lnt = Slots(P, "lnt", [128, 512], F32, 2)
        self.w1s = Slots(P, "w1s", [128, KC, 256], BF16, 3)
        self.w2s = Slots(P, "w2s", [128, NFC, 128], BF16, 3)
        self.wps = Slots(P, "wps", [128, KC, 128], BF16, 4)
        if self.ext is None:
            self.mods = {}
            for s in self.sets:
                self.mods[s] = self.compute_mod(s)
        if self.do_proj:
            self.setup_proj()
        if self.do_merge:
            self.setup_merge()
        for g in range(4):
            tiles = [TILES[i] for i in GROUPS[g]]
            go = GOFF[g]
            for (off, w, z) in tiles:
                ti = 0 if z == 1 else 1
                lo = 512 if z == 1 else 0
                P.dma(self.h[:, :, lo:lo + w], d["hT"][:, :, off:off + w], writes=[self.bh[ti]])
            if self.do_merge:
                self.merge(g, "A")
                self.ffn(g, "A", 1)
            if self.do_proj:
                self.ffn(g, "B", 0)
                self.proj(g, "B")
            for (off, w, z) in tiles:
                ti = 0 if z == 1 else 1
                lo = 512 if z == 1 else 0
                P.dma(d["hT_out"][:, :, off:off + w], self.h[:, :, lo:lo + w], reads=[self.bh[ti]])

    def seq_off(self, off):
        j = self.shard
        return 64 * j + off if off < 64 else 256 + 2048 * j + (off - 64)

    def fm_ap(self, hh, r, p0, p1, off, w):
        if self.ext is None:
            return self.d["fm"][hh, r, p0:p1, off:off + w]
        so = self.seq_off(off)
        return self.d["fm_seq"][hh, r, p0:p1, so:so + w]

    def tm_out(self, name, off, sw, st_, ncol, bs_):
        P = self.P
        if self.ext is None:
            P.dma(self.d[name][off:off + sw, :], st_[0:sw, 0:4 * ncol], reads=[bs_])
            return
        so = self.seq_off(off)
        for hh in range(4):
            P.dma(self.d[name + "_seq"][hh, so:so + sw, :], st_[0:sw, hh * ncol:(hh + 1) * ncol], reads=[bs_])

    def compute_mod(self, s):
        P = self.P
        d = self.d
        cT = P.sb("cT", [128, KC, 2], F32)
        b_c = Buf()
        P.dma(cT[:], d["cT"], writes=[b_c])
        sT = P.sb("sT", [128, KC, 2], BF16)
        b_s = Buf()
        P.act(sT[:], cT[:], AF.Silu, [b_c], [b_s])
        bada = P.sb("bada", [128, 72], F32)
        b_b = Buf()
        P.dma(bada[:], d[f"bada{s}"], writes=[b_b])
        nw = P.sb("nw", [128, 6, 8], F32)
        b_nw = Buf()
        P.dma(nw[:], d[f"normw{s}"], writes=[b_nw])
        M = P.sb("M", [128, 72, 2], F32)
        b_M = Buf()
        wsl = Slots(P, "wada", [128, KC, 384], BF16, 2)
        for sl in range(24):
            wt, bw = wsl.next()
            P.dma(wt[:], d[f"wada{s}"][sl].rearrange("p (k n) -> p k n", k=KC), writes=[bw], eng=WENG)
            pt, bp = P.bank()
            for ci in range(3):
                for k in range(KC):
                    P.mm(pt[:, 2 * ci:2 * ci + 2], wt[:, k, ci * 128:(ci + 1) * 128], sT[:, k, :],
                         k == 0, k == KC - 1, [bw, b_s], [bp])
            P.tt(M[:, sl * 3:(sl + 1) * 3, :], pt[:, 0:6].rearrange("p (c z) -> p c z", z=2),
                 bada[:, sl * 3:(sl + 1) * 3].unsqueeze(2).broadcast_to([128, 3, 2]), ALU.add, [bp, b_b], [b_M])
        S = P.sb("S", [128, 9, 8, 2], F32)
        b_S = Buf()

        def Mi(i):
            return M[:, i * 8:(i + 1) * 8, :]

        def nwb(i):
            return nw[:, i, :].unsqueeze(2).broadcast_to([128, 8, 2])
        one = P.sb("onep", [128, 8, 2], F32)
        b_one = Buf()
        for (k, nwi, sci) in ((0, 0, 1), (3, 2, 4), (6, 4, 7)):
            P.ts(one[:], Mi(sci), 1.0, ALU.add, [b_M], [b_one])
            P.tt(S[:, k], one[:], nwb(nwi), ALU.mult, [b_one, b_nw], [b_S])
        for (k, shi) in ((1, 0), (4, 3), (7, 6)):
            P.copy(S[:, k], Mi(shi), [b_M], [b_S])
        for (k, gi, nwi, f) in ((2, 2, 1, 0.5), (5, 5, 3, 1.0), (8, 8, 5, 0.5)):
            P.ts(one[:], Mi(gi), f, ALU.mult, [b_M], [b_one])
            P.tt(S[:, k], one[:], nwb(nwi), ALU.mult, [b_one, b_nw], [b_S])
        return S, b_S

    def rstd_of(self, src, b_src, nch, w, dn):
        P = self.P
        sq, bsq = self.sq.next()
        P.act(sq[:, 0:nch, 0:w], src, AF.Square, [b_src], [bsq])
        pt, bp = P.bank()
        for c in range(nch):
            P.mm(pt[:, 0:w], self.ones[:], sq[:, c, 0:w], c == 0, c == nch - 1, [bsq, self.b_ones], [bp])
        ln, bln = self.lnt.next()
        P.act(ln[:, 0:w], pt[:, 0:w], AF.Ln, [bp], [bln], scale=1.0 / dn, bias=EPS)
        r, br = self.rstd.next()
        P.act(r[:, 0:w], ln[:, 0:w], AF.Exp, [bln], [br], scale=-0.5)
        return r, br

    def norm_mod(self, g, s, kbase):
        P = self.P
        S, bS = self.mods[s]
        go = GOFF[g]
        for tix in GROUPS[g]:
            off, w, z = TILES[tix]
            ti = 0 if z == 1 else 1
            lo = 512 if z == 1 else 0
            r, br = self.rstd_of(self.h[:, :, lo:lo + w], self.bh[ti], KC, w, D)
            for c in range(KC):
                t, bt = self.tmp.next()
                P.tt(t[:, 0:w], self.h[:, c, lo:lo + w], r[:, 0:w], ALU.mult, [self.bh[ti], br], [bt])
                P.act(self.hn[:, c, lo:lo + w], t[:, 0:w], AF.Identity, [bt, bS], [self.bhn[ti]],
                      scale=S[:, kbase, c, z:z + 1], bias=S[:, kbase + 1, c, z:z + 1])

    def post_res(self, g, s, kgate):
        P = self.P
        S, bS = self.mods[s]
        go = GOFF[g]
        for tix in GROUPS[g]:
            off, w, z = TILES[tix]
            ti = 0 if z == 1 else 1
            lo = 512 if z == 1 else 0
            by_all = [self.by[c][ti] for c in range(KC)]
            bsrc = Buf()
            sq, bsq = self.sq.next()
            P.act(sq[:, :, 0:w], self.y[:, :, lo:lo + w], AF.Square, by_all, [bsq])
            pt, bp = P.bank()
            for c in range(KC):
                P.mm(pt[:, 0:w], self.ones[:], sq[:, c, 0:w], c == 0, c == KC - 1, [bsq, self.b_ones], [bp])
            ln, bln = self.lnt.next()
            P.act(ln[:, 0:w], pt[:, 0:w], AF.Ln, [bp], [bln], scale=1.0 / D, bias=EPS)
            r, br = self.rstd.next()
            P.act(r[:, 0:w], ln[:, 0:w], AF.Exp, [bln], [br], scale=-0.5)
            for c in range(KC):
                t, bt = self.tmp.next()
                P.tt(t[:, 0:w], self.y[:, c, lo:lo + w], r[:, 0:w], ALU.mult, [self.by[c][ti], br], [bt])
                P.stt(self.h[:, c, lo:lo + w], t[:, 0:w], S[:, kgate, c, z:z + 1], self.h[:, c, lo:lo + w],
                      ALU.mult, ALU.add, [bt, bS, self.bh[ti]], [self.bh[ti]])

    def ffn(self, g, s, i):
        P = self.P
        d = self.d
        kb = 0 if i == 0 else 6
        self.norm_mod(g, s, kb)
        go = GOFF[g]
        tl = [((0 if TILES[tix][2] == 1 else 1),) + TILES[tix] for tix in GROUPS[g]]
        w1 = d[f"w1{s}"]
        w2 = d[f"w2{s}"]
        for n in range(NFC):
            wt, bw = self.w1s.next()
            P.dma(wt[:], w1[n].rearrange("p (k n) -> p k n", k=KC), writes=[bw], eng=WENG)
            for (ti, off, w, z) in tl:
                lo = 512 if z == 1 else 0
                pg, bpg = P.bank()
                pu, bpu = P.bank()
                for k in range(KC):
                    P.mm(pg[:, 0:w], wt[:, k, 0:128], self.hn[:, k, lo:lo + w], k == 0, k == KC - 1,
                         [bw, self.bhn[ti]], [bpg])
                for k in range(KC):
                    P.mm(pu[:, 0:w], wt[:, k, 128:256], self.hn[:, k, lo:lo + w], k == 0, k == KC - 1,
                         [bw, self.bhn[ti]], [bpu])
                t, bt = self.tmp.next()
                P.act(t[:, 0:w], pg[:, 0:w], AF.Silu, [bpg], [bt])
                P.tt(self.a[:, n, lo:lo + w], t[:, 0:w], pu[:, 0:w], ALU.mult, [bt, bpu], [self.ba[n][ti]])
        for dc in range(KC):
            wt, bw = self.w2s.next()
            P.dma(wt[:], w2[dc].rearrange("p (k n) -> p k n", k=NFC), writes=[bw], eng=WENG)
            for (ti, off, w, z) in tl:
                lo = 512 if z == 1 else 0
                py, bpy = P.bank()
                for k in range(NFC):
                    P.mm(py[:, 0:w], wt[:, k, :], self.a[:, k, lo:lo + w], k == 0, k == NFC - 1,
                         [bw, self.ba[k][ti]], [bpy])
                P.copy(self.y[:, dc, lo:lo + w], py[:, 0:w], [bpy], [self.by[dc][ti]], eng="act")
        self.post_res(g, s, kb + 2)

    def setup_proj(self):
        P = self.P
        d = self.d
        self.wt = P.sb("wt", [128, KC, 528], BF16)
        self.b_wt = Buf()
        P.dma(self.wt[:], d["wt"].rearrange("p (k n) -> p k n", k=KC), writes=[self.b_wt], eng=WENG)
        self.wq = P.sb("wq", [128, 3, 1024], BF16)
        self.b_wq = Buf()
        P.dma(self.wq[:], d["wq"].rearrange("p (k n) -> p k n", k=3), writes=[self.b_wq], eng=WENG)
        self.wkn = P.sb("wkn", [128, 2, 512], BF16)
        self.b_wkn = Buf()
        P.dma(self.wkn[:], d["wkn"].rearrange("p (k n) -> p k n", k=2), writes=[self.b_wkn], eng=WENG)
        self.wkv = P.sb("wkv", [128, 2, 512], BF16)
        self.b_wkv = Buf()
        P.dma(self.wkv[:], d["wkv"].rearrange("p (k n) -> p k n", k=2), writes=[self.b_wkv], eng=WENG)
        self.qnw = P.sb("qnw", [128, 3], F32)
        self.kvnw = P.sb("kvnw", [128, 2], F32)
        self.b_nws = Buf()
        P.dma(self.qnw[:], d["qnw"], writes=[self.b_nws])
        P.dma(self.kvnw[:], d["kvnw"], writes=[self.b_nws])
        self.cos = P.sb("cos", [64, NG], F32)
        self.sin = P.sb("sin", [64, NG], F32)
        self.b_rope = Buf()
        self.qkva = self.y[:, 0:5, :]
        self.b_qkva = self.by[0:5]
        self.qkvn = self.a[:, 0:5, :]
        self.b_qkvn5 = self.ba[0:5]
        self.krr = self.y[0:64, 5:7, :]
        self.b_krr2 = self.by[5:7]
        self.stage = Slots(P, "stage", [128, 512], F32, 3)
        self.stage2 = self.stage

    def proj(self, g, s):
        P = self.P
        d = self.d
        self.norm_mod(g, s, 3)
        go = GOFF[g]
        gw = GW[g]
        tl = [((0 if TILES[tix][2] == 1 else 1),) + TILES[tix] for tix in GROUPS[g]]
        for (ti, off, w, z) in tl:
            lo = 512 if z == 1 else 0
            P.dma(self.cos[:, lo:lo + w], d["rope_cos"][:, off:off + w], writes=[self.b_rope])
            P.dma(self.sin[:, lo:lo + w], d["rope_sin"][:, off:off + w], writes=[self.b_rope])
        ev = 0
        for ci in range(38):
            wt, bw = self.wps.next()
            P.dma(wt[:], d["wp"][ci].rearrange("p (k n) -> p k n", k=KC), writes=[bw], eng=WENG)
            for (ti, off, w, z) in tl:
                lo = 512 if z == 1 else 0
                if ci < 37:
                    pt, bp = P.bank()
                    for k in range(KC):
                        P.mm(pt[:, 0:w], wt[:, k, :], self.hn[:, k, lo:lo + w], k == 0, k == KC - 1,
                             [bw, self.bhn[ti]], [bp])
                    if ci < 32:
                        hh, r = ci // 8, ci % 8
                        st_, bs_ = self.stage.next()
                        P.copy(st_[:, 0:w], pt[:, 0:w], [bp], [bs_], eng=("act" if ev % 2 == 0 else "dve"))
                        ev += 1
                        P.dma(self.fm_ap(hh, r, 0, 128, off, w), st_[:, 0:w], reads=[bs_])
                    else:
                        P.copy(self.qkva[:, ci - 32, lo:lo + w], pt[:, 0:w], [bp], [self.b_qkva[ci - 32][ti]],
                               eng=("act" if ev % 2 == 0 else "dve"))
                        ev += 1
                else:
                    for half in range(2):
                        pt, bp = P.bank()
                        for k in range(KC):
                            P.mm(pt[0:64, 0:w], wt[:, k, half * 64:(half + 1) * 64], self.hn[:, k, lo:lo + w],
                                 k == 0, k == KC - 1, [bw, self.bhn[ti]], [bp])
                        P.copy(self.krr[:, half, lo:lo + w], pt[0:64, 0:w], [bp], [self.b_krr2[half][ti]], eng="act")
        for (ti, off, w, z) in tl:
            lo = 512 if z == 1 else 0
            t1, b1 = self.stage2.next()
            t2, b2 = self.stage2.next()
            P.tt(t1[0:64, 0:w], self.krr[:, 0, lo:lo + w], self.cos[:, lo:lo + w], ALU.mult,
                 [self.b_krr2[0][ti], self.b_rope], [b1])
            P.tt(t2[0:64, 0:w], self.krr[:, 1, lo:lo + w], self.sin[:, lo:lo + w], ALU.mult,
                 [self.b_krr2[1][ti], self.b_rope], [b2])
            P.tt(t1[0:64, 0:w], t1[0:64, 0:w], t2[0:64, 0:w], ALU.add, [b1, b2], [b1])
            for hh in range(4):
                P.dma(self.fm_ap(hh, 10, 64, 128, off, w), t1[0:64, 0:w], reads=[b1])
        for (ti, off, w, z) in tl:
            lo = 512 if z == 1 else 0
            for (c0, nch, dn, nwt) in ((0, 3, 384, self.qnw), (3, 2, 256, self.kvnw)):
                bsrc = [self.b_qkva[c0 + c][ti] for c in range(nch)]
                sq, bsq = self.sq.next()
                P.act(sq[:, 0:nch, 0:w], self.qkva[:, c0:c0 + nch, lo:lo + w], AF.Square, bsrc, [bsq])
                pt, bp = P.bank()
                for c in range(nch):
                    P.mm(pt[:, 0:w], self.ones[:], sq[:, c, 0:w], c == 0, c == nch - 1, [bsq, self.b_ones], [bp])
                ln, bln = self.lnt.next()
                P.act(ln[:, 0:w], pt[:, 0:w], AF.Ln, [bp], [bln], scale=1.0 / dn, bias=EPS)
                r, br = self.rstd.next()
                P.act(r[:, 0:w], ln[:, 0:w], AF.Exp, [bln], [br], scale=-0.5)
                for c in range(nch):
                    t, bt = self.tmp.next()
                    P.tt(t[:, 0:w], self.qkva[:, c0 + c, lo:lo + w], r[:, 0:w], ALU.mult,
                         [self.b_qkva[c0 + c][ti], br], [bt])
                    P.act(self.qkvn[:, c0 + c, lo:lo + w], t[:, 0:w], AF.Identity, [bt, self.b_nws],
                          [self.b_qkvn5[c0 + c][ti]], scale=nwt[:, c:c + 1])
        for hh in range(4):
            for (ti, off, w, z) in tl:
                lo = 512 if z == 1 else 0
                pt, bp = P.bank()
                for k in range(3):
                    P.mm(pt[:, 0:w], self.wq[:, k, hh * 256:hh * 256 + 128], self.qkvn[:, k, lo:lo + w],
                         k == 0, k == 2, [self.b_wq, self.b_qkvn5[k][ti]], [bp])
                st_, bs_ = self.stage.next()
                P.copy(st_[:, 0:w], pt[:, 0:w], [bp], [bs_], eng="act")
                P.dma(self.fm_ap(hh, 8, 0, 128, off, w), st_[:, 0:w], reads=[bs_])
                pt, bp = P.bank()
                for k in range(2):
                    P.mm(pt[:, 0:w], self.wkn[:, k, hh * 128:(hh + 1) * 128], self.qkvn[:, 3 + k, lo:lo + w],
                         k == 0, k == 1, [self.b_wkn, self.b_qkvn5[3 + k][ti]], [bp])
                st_, bs_ = self.stage.next()
                P.copy(st_[:, 0:w], pt[:, 0:w], [bp], [bs_], eng="dve")
                P.dma(self.fm_ap(hh, 9, 0, 128, off, w), st_[:, 0:w], reads=[bs_])
                pr, bpr = P.bank()
                for k in range(3):
                    P.mm(pr[0:64, 0:w], self.wq[:, k, hh * 256 + 128:hh * 256 + 192], self.qkvn[:, k, lo:lo + w],
                         k == 0, k == 2, [self.b_wq, self.b_qkvn5[k][ti]], [bpr])
                ps_, bps = P.bank()
                for k in range(3):
                    P.mm(ps_[0:64, 0:w], self.wq[:, k, hh * 256 + 192:hh * 256 + 256], self.qkvn[:, k, lo:lo + w],
                         k == 0, k == 2, [self.b_wq, self.b_qkvn5[k][ti]], [bps])
                t1, b1 = self.stage2.next()
                t2, b2 = self.stage2.next()
                P.tt(t1[0:64, 0:w], pr[0:64, 0:w], self.cos[:, lo:lo + w], ALU.mult, [bpr, self.b_rope], [b1])
                P.tt(t2[0:64, 0:w], ps_[0:64, 0:w], self.sin[:, lo:lo + w], ALU.mult, [bps, self.b_rope], [b2])
                P.tt(t1[0:64, 0:w], t1[0:64, 0:w], t2[0:64, 0:w], ALU.add, [b1, b2], [b1])
                P.dma(self.fm_ap(hh, 10, 0, 64, off, w), t1[0:64, 0:w], reads=[b1])
        for (ti, off, w, z) in tl:
            lo = 512 if z == 1 else 0
            for s0 in range(0, w, 128):
                sw = min(128, w - s0)
                a0 = lo + s0
                pt, bp = P.bank()
                for k in range(KC):
                    P.mm(pt[0:sw, :], self.hn[:, k, a0:a0 + sw], self.wt[:, k, 0:512], k == 0, k == KC - 1,
                         [self.b_wt, self.bhn[ti]], [bp])
                st_, bs_ = self.stage.next()
                P.copy(st_[0:sw, :], pt[0:sw, :], [bp], [bs_], eng="act")
                self.tm_out("tm_hv", off + s0, sw, st_, 128, bs_)
                pt, bp = P.bank()
                for k in range(KC):
                    P.mm(pt[0:sw, 0:16], self.hn[:, k, a0:a0 + sw], self.wt[:, k, 512:528], k == 0, k == KC - 1,
                         [self.b_wt, self.bhn[ti]], [bp])
                st_, bs_ = self.stage.next()
                P.copy(st_[0:sw, 0:16], pt[0:sw, 0:16], [bp], [bs_], eng="dve")
                self.tm_out("tm_ab", off + s0, sw, st_, 4, bs_)
                pt, bp = P.bank()
                for k in range(2):
                    P.mm(pt[0:sw, :], self.qkvn[:, 3 + k, a0:a0 + sw], self.wkv[:, k, :], k == 0, k == 1,
                         [self.b_wkv, self.b_qkvn5[3 + k][ti]], [bp])
                st_, bs_ = self.stage.next()
                P.copy(st_[0:sw, :], pt[0:sw, :], [bp], [bs_], eng="act")
                self.tm_out("tm_mv", off + s0, sw, st_, 128, bs_)

    def setup_merge(self):
        P = self.P
        self.yT = self.a[:, 0:12, :]
        self.b_yTq = self.ba[0:12]
        self.mT = self.a[:, 12:20, :]
        self.b_mT = self.ba[12:20]
        self.wbs = Slots(P, "wbs", [128, 12, 128], BF16, 2)
        self.sg = self.tmp
        self.macc = Slots(P, "macc", [128, 512], F32, 2)

    def merge(self, g, s):
        P = self.P
        d = self.d
        S, bS = self.mods[s]
        go = GOFF[g]
        tl = [((0 if TILES[tix][2] == 1 else 1),) + TILES[tix] for tix in GROUPS[g]]
        for (ti, off, w, z) in tl:
            lo = 512 if z == 1 else 0
            if self.ext is None:
                P.dma(self.yT[:, :, lo:lo + w], d["yT"][:, :, off:off + w],
                      writes=[self.b_yTq[q][ti] for q in range(12)], eng="pool")
            else:
                so = self.seq_off(off)
                for bi in range(3):
                    for hh in range(4):
                        P.dma(self.yT[:, bi * 4 + hh, lo:lo + w], d["yseq"][hh, bi, :, so:so + w],
                              writes=[self.b_yTq[bi * 4 + hh][ti]], eng="pool")
        self.norm_mod(g, s, 3)
        for c in range(KC):
            wgl = []
            for j in range(3):
                wg, bwg = self.wps.next()
                P.dma(wg[:], d["wg"][c][:, j * 1024:(j + 1) * 1024].rearrange("p (k n) -> p k n", k=KC),
                      writes=[bwg], eng=WENG)
                wgl.append((wg, bwg))
            wb, bwb = self.wbs.next()
            P.dma(wb[:], d["wb"][c].rearrange("p (q n) -> p q n", q=12), writes=[bwb], eng=WENG)
            for (ti, off, w, z) in tl:
                lo = 512 if z == 1 else 0
                acc, bacc = self.macc.next()
                for j in range(3):
                    pg, bpg = P.bank()
                    wg, bwg = wgl[j]
                    for k in range(KC):
                        P.mm(pg[:, 0:w], wg[:, k, :], self.hn[:, k, lo:lo + w], k == 0, k == KC - 1,
                             [bwg, self.bhn[ti]], [bpg])
                    pb, bpb = P.bank()
                    for q in range(4):
                        P.mm(pb[:, 0:w], wb[:, j * 4 + q, :], self.yT[:, j * 4 + q, lo:lo + w], q == 0, q == 3,
                             [bwb, self.b_yTq[j * 4 + q][ti]], [bpb])
                    sg, bsg = self.sg.next()
                    P.act(sg[:, 0:w], pg[:, 0:w], AF.Sigmoid, [bpg], [bsg])
                    if j == 0:
                        P.tt(acc[:, 0:w], sg[:, 0:w], pb[:, 0:w], ALU.mult, [bsg, bpb], [bacc])
                    else:
                        P.tt(sg[:, 0:w], sg[:, 0:w], pb[:, 0:w], ALU.mult, [bsg, bpb], [bsg])
                        if j == 1:
                            P.tt(acc[:, 0:w], acc[:, 0:w], sg[:, 0:w], ALU.add, [bacc, bsg], [bacc])
                        else:
                            P.tt(self.mT[:, c, lo:lo + w], acc[:, 0:w], sg[:, 0:w], ALU.add, [bacc, bsg],
                                 [self.b_mT[c][ti]])
        for c in range(KC):
            wo, bwo = self.wps.next()
            P.dma(wo[:], d["wo"][c].rearrange("p (k n) -> p k n", k=KC), writes=[bwo], eng=WENG)
            for (ti, off, w, z) in tl:
                lo = 512 if z == 1 else 0
                py, bpy = P.bank()
                for k in range(KC):
                    P.mm(py[:, 0:w], wo[:, k, :], self.mT[:, k, lo:lo + w], k == 0, k == KC - 1,
                         [bwo, self.b_mT[k][ti]], [bpy])
                P.copy(self.y[:, c, lo:lo + w], py[:, 0:w], [bpy], [self.by[c][ti]], eng="act")
        self.post_res(g, s, 5)


def _fm(v):
    return np.ascontiguousarray(v.reshape(-1, 128).T)


def _wtile(w, cols):
    K = w.shape[0]
    sub = w[:, cols].reshape(K // 128, 128, len(cols))
    return np.ascontiguousarray(sub.transpose(1, 0, 2).reshape(128, -1))


def _partner(j):
    return j + 16 if (j % 32) < 16 else j - 16


def prep_ffn_set(inp, l, i):
    out = {}
    wa = inp["w_ada"][l]
    out["wada"] = np.stack([_wtile(wa, np.arange(sl * 384, (sl + 1) * 384)) for sl in range(24)])
    out["bada"] = _fm(inp["b_ada"][l])
    out["normw"] = np.ascontiguousarray(inp["norm_w"][l].reshape(6, 8, 128).transpose(2, 0, 1))
    w1 = inp["ffn_w_in"][l, i]
    out["w1"] = np.stack([_wtile(w1, np.concatenate([np.arange(n * 128, (n + 1) * 128),
                                                     DFF + np.arange(n * 128, (n + 1) * 128)])) for n in range(NFC)])
    w2 = inp["ffn_w_out"][l, i]
    out["w2"] = np.stack([_wtile(w2, np.arange(dc * 128, (dc + 1) * 128)) for dc in range(8)])
    return out


def prep_proj(inp, l):
    out = {}
    w_in = inp["w_in"][l]
    chunks = []
    for hh in range(4):
        for base in (0, 512, 1024, 1536, 2768, 3280, 3792, 4816):
            chunks.append(np.arange(base + hh * 128, base + (hh + 1) * 128))
    for c in range(3):
        chunks.append(np.arange(2064 + c * 128, 2064 + (c + 1) * 128))
    for c in range(2):
        chunks.append(np.arange(2448 + c * 128, 2448 + (c + 1) * 128))
    kr = 2704 + np.arange(64)
    chunks.append(np.concatenate([kr, 2704 + np.array([_partner(j) for j in range(64)])]))
    out["wp"] = np.stack([_wtile(w_in, c) for c in chunks])
    abc = np.array([[2048 + hh, 2052 + hh, 2056 + hh, 2060 + hh] for hh in range(4)]).reshape(-1)
    out["wt"] = _wtile(w_in, np.concatenate([4304 + np.arange(512), abc]))
    wq = inp["mla_w_q_b"][l]
    qc = []
    for hh in range(4):
        qc += list(hh * 192 + np.arange(128)) + list(hh * 192 + 128 + np.arange(64)) + \
            [hh * 192 + 128 + _partner(j) for j in range(64)]
    out["wq"] = _wtile(wq, np.array(qc))
    wkv = inp["mla_w_kv_b"][l]
    out["wkn"] = _wtile(wkv, np.concatenate([hh * 256 + np.arange(128) for hh in range(4)]))
    out["wkv"] = _wtile(wkv, np.concatenate([hh * 256 + 128 + np.arange(128) for hh in range(4)]))
    out["qnw"] = _fm(inp["mla_q_norm"][l])
    out["kvnw"] = _fm(inp["mla_kv_norm"][l])
    return out


def prep_merge(inp, l):
    out = {}
    w_in = inp["w_in"][l]
    out["wg"] = np.stack([np.concatenate([_wtile(w_in, 5328 + j * 1024 + c * 128 + np.arange(128)) for j in range(3)],
                                         axis=1) for c in range(8)])
    wb = inp["w_branch"][l]
    out["wb"] = np.stack([np.concatenate([_wtile(wb[j], c * 128 + np.arange(128)) for j in range(3)], axis=1)
                          for c in range(8)])
    out["wo"] = np.stack([_wtile(inp["w_out"][l], c * 128 + np.arange(128)) for c in range(8)])
    return out


def rope_tables(shard):
    cos = np.ones((64, NTOK), np.float64)
    sin = np.zeros((64, NTOK), np.float64)
    t = shard * 2048 + np.arange(2048)
    row = (t // 64).astype(np.float64)
    col = (t % 64).astype(np.float64)
    inv = (10000.0 ** (-np.arange(16, dtype=np.float32) / np.float32(16))).astype(np.float32).astype(np.float64)
    for dd in range(64):
        a_, s_, f_ = dd // 32, (dd % 32) // 16, dd % 16
        pos = row if a_ == 0 else col
        ang = (pos.astype(np.float32) * inv[f_].astype(np.float32)).astype(np.float64)
        cos[dd, 64:] = np.cos(ang)
        sin[dd, 64:] = np.sin(ang) * (-1.0 if s_ == 0 else 1.0)
    return cos.astype(np.float32), sin.astype(np.float32)


def tok_to_fm(h):
    return np.ascontiguousarray(h.reshape(NTOK, 8, 128).transpose(2, 1, 0))


def fm_to_tok(hT):
    return np.ascontiguousarray(hT.transpose(2, 1, 0).reshape(NTOK, 1024))


class Arena:
    def __init__(self, P, kb=192):
        self.cap = kb * 256
        self.t = P.sb("arena", [128, self.cap], F32)
        self.off = 0

    def reset(self):
        self.off = 0

    def _shape(self, ap, shape):
        if len(shape) == 3:
            return ap.rearrange("p (a b) -> p a b", a=shape[1])
        if len(shape) == 4:
            return ap.rearrange("p (a b c) -> p a b c", a=shape[1], b=shape[2])
        return ap

    def f32(self, shape):
        n = int(np.prod(shape[1:]))
        o = self.off
        self.off += n
        assert self.off <= self.cap, ("arena overflow", self.off, self.cap)
        return self._shape(self.t[0:shape[0], o:o + n], shape)

    def bf16(self, shape):
        n = int(np.prod(shape[1:]))
        nw = (n + 1) // 2
        o = self.off
        self.off += nw
        assert self.off <= self.cap, ("arena overflow", self.off, self.cap)
        ap = self.t[0:shape[0], o:o + nw].bitcast(BF16)[:, 0:n]
        return self._shape(ap, shape)


class ASlots:
    def __init__(self, aps):
        self.t = aps
        self.b = [Buf() for _ in aps]
        self.i = 0

    def next(self):
        k = self.i % len(self.t)
        self.i += 1
        return self.t[k], self.b[k]


NBLK = TSEQ // 128
MLA_SCALE = 192 ** -0.5
C_ONES, C_IDENT, C_HM0, C_HM1, C_NEG0, C_NEG1, C_STR0, C_STR1, C_TRI0, C_TRI1, C_ROWM = range(11)
C_LVL = 11
NCONST = 25


def make_consts():
    c = np.zeros((NCONST, 128, 128), np.float32)
    i = np.arange(128)
    c[C_ONES] = 1.0
    c[C_IDENT] = np.eye(128)
    J, I = np.meshgrid(i, i, indexing="ij")
    same32 = (J // 32) == (I // 32)
    c[C_HM0] = (same32 & (I >= J))
    c[C_HM1] = (same32 & (I <= J))
    c[C_NEG0] = np.where(I >= J, 0.0, -30000.0)
    c[C_NEG1] = np.where(I <= J, 0.0, -30000.0)
    c[C_STR0] = (I > J)
    c[C_STR1] = (I < J)
    c[C_TRI0] = (J <= I)
    c[C_TRI1] = (J >= I)
    for cc in range(4):
        c[C_ROWM][:, cc] = (i // 32 == cc)
    for d in range(2):
        for l in range(7):
            b = 2 ** l
            Ii, Jj = J, I
            sameblk = (Ii // (2 * b)) == (Jj // (2 * b))
            ih = (Ii // b) % 2
            jh = (Jj // b) % 2
            if d == 0:
                m = sameblk & (ih == 1) & (jh == 0)
            else:
                m = sameblk & (ih == 0) & (jh == 1)
            c[C_LVL + d * 7 + l] = -1.0 * m
    return c


class MixerPhase:
    def __init__(self, layer, parts=("gdn", "mla", "hg"), stage=99, ext=None, d=None, small=None):
        self.layer = layer
        self.stage = stage
        if ext is not None:
            self.nc, self.P, self.d, self.A = ext.nc, ext.P, d, ext.A
            self.cf, self.cb, self.b_c, self.small = ext.cf, ext.cb, ext.b_c, small
            self.mla()
            self.hgrn2()
            self.gdn()
            return
        nc = self.nc = bass.Bass("TRN2", target_bir_lowering=False)
        self.d = {}

        def din(name, shape):
            self.d[name] = nc.dram_tensor(name, list(shape), F32, kind="ExternalInput").ap()

        din("fm", [NFM, 128, TSEQ])
        din("tm_ab", [TSEQ, 4])
        din("tm_mv", [TSEQ, 128])
        din("tm_hv", [TSEQ, 128])
        din("consts", [NCONST, 128, 128])
        din("rmask", [128, 1408])
        din("small", [128, 32])
        self.d["yT"] = nc.dram_tensor("yT", [3, 128, TSEQ], F32, kind="ExternalOutput").ap()
        with ExitStack() as st:
            self.P = P = Prog(nc, st)
            P.init_psum(8)
            self.cf = P.sb("cf", [128, NCONST, 128], F32)
            self.cb = P.sb("cb", [128, NCONST, 128], BF16)
            self.b_c = Buf()
            P.dma(self.cf[:], self.d["consts"].rearrange("n p f -> p n f"), writes=[self.b_c])
            P.dma(self.cb[:], self.d["consts"].rearrange("n p f -> p n f"), writes=[self.b_c], eng="pool")
            self.small = P.sb("small", [128, 32], F32)
            P.dma(self.small[:], self.d["small"], writes=[self.b_c])
            self.A = Arena(P, 178)
            if "mla" in parts:
                self.mla()
            if "hg" in parts:
                self.hgrn2()
            if "gdn" in parts:
                self.gdn()
            P.emit()

    def mla(self):
        P, A, d = self.P, self.A, self.d
        P.barrier()
        A.reset()
        P.rot = [0, 1, 2, 3]
        T = TSEQ
        fm = d["fm"]
        qn = A.bf16([128, T]); kn = A.bf16([128, T]); qr = A.bf16([64, T]); kr = A.bf16([64, T])
        v = A.bf16([128, NBLK, 128])
        bq, bk, bv = Buf(), Buf(), Buf()
        for c0 in range(0, T, 2112):
            P.dma(kn[:, c0:c0 + 2112], fm[9][:, c0:c0 + 2112], writes=[bk], eng="pool")
            P.dma(kr[:, c0:c0 + 2112], fm[10][64:128, c0:c0 + 2112], writes=[bk], eng="pool")
            P.dma(qn[:, c0:c0 + 2112], fm[8][:, c0:c0 + 2112], writes=[bq], eng="pool")
            P.dma(qr[:, c0:c0 + 2112], fm[10][0:64, c0:c0 + 2112], writes=[bq], eng="pool")
        for n0 in range(0, NBLK, 22):
            P.dma(v[:, n0:n0 + 22, :], d["tm_mv"][n0 * 128:(n0 + 22) * 128, :].rearrange("(n p) d -> p n d", p=128),
                  writes=[bv], eng="pool")
        pts = ASlots([A.bf16([128, 512]) for _ in range(4)])
        rcs = ASlots([A.f32([128, 512]) for _ in range(2)])
        yos = ASlots([A.f32([128, 512]) for _ in range(2)])
        ones = self.cb[:, C_ONES, :]
        blocks = [(0, 256, 2)] + [(256 + 512 * b, 512, NBLK) for b in range(16)]
        for bi, (q0, qw, nk) in enumerate(blocks):
            O, bO = P.psum[4 + bi % 2]
            L, bL = P.psum[6 + bi % 2]
            LOOK = 2
            pend = []

            def scores(kc):
                S, bS = P.bank()
                P.mm(S[:, 0:qw], kn[:, kc * 128:(kc + 1) * 128], qn[:, q0:q0 + qw], True, False, [bk, bq], [bS])
                P.mm(S[:, 0:qw], kr[:, kc * 128:(kc + 1) * 128], qr[:, q0:q0 + qw], False, True, [bk, bq], [bS])
                pt, bpt = pts.next()
                P.act(pt[:, 0:qw], S[:, 0:qw], AF.Exp, [bS], [bpt], scale=MLA_SCALE)
                pend.append((kc, pt, bpt))

            def consume():
                kc, pt, bpt = pend.pop(0)
                P.mm(O[:, 0:qw], v[:, kc, :], pt[:, 0:qw], kc == 0, kc == nk - 1, [bv, bpt], [bO])
                P.mm(L[:, 0:qw], ones, pt[:, 0:qw], kc == 0, kc == nk - 1, [self.b_c, bpt], [bL])

            for kc in range(nk):
                scores(kc)
                if len(pend) > LOOK:
                    consume()
            while pend:
                consume()
            rc, brc = rcs.next()
            P.add("dve", (lambda o_, i_: (lambda e: e.reciprocal(out=o_, in_=i_)))(rc[:, 0:qw], L[:, 0:qw]), [bL], [brc])
            yo, byo = yos.next()
            P.tt(yo[:, 0:qw], O[:, 0:qw], rc[:, 0:qw], ALU.mult, [bO, brc], [byo])
            P.dma(d["yT"][1, :, q0:q0 + qw], yo[:, 0:qw], reads=[byo])

    def readout(self, of, b_of_all, gate_row, nw_ap, out_idx, scr):
        P, d = self.P, self.d
        W = 384
        g_s, sq_s, ln_s, r_s, t_s = scr
        ones = self.cb[:, C_ONES, :]
        for t0 in range(0, TSEQ, W):
            g, bg = g_s.next()
            P.dma(g[:, 0:W], d["fm"][gate_row][:, t0:t0 + W], writes=[bg])
            P.act(g[:, 0:W], g[:, 0:W], AF.Silu, [bg], [bg])
            sq, bsq = sq_s.next()
            P.act(sq[:, 0:W], of[:, t0:t0 + W], AF.Square, b_of_all, [bsq])
            pt, bp = P.bank()
            P.mm(pt[:, 0:W], ones, sq[:, 0:W], True, True, [self.b_c, bsq], [bp])
            ln, bln = ln_s.next()
            P.act(ln[:, 0:W], pt[:, 0:W], AF.Ln, [bp], [bln], scale=1.0 / 128, bias=EPS)
            r, br = r_s.next()
            P.act(r[:, 0:W], ln[:, 0:W], AF.Exp, [bln], [br], scale=-0.5)
            t, bt = t_s.next()
            P.tt(t[:, 0:W], of[:, t0:t0 + W], r[:, 0:W], ALU.mult, b_of_all + [br], [bt])
            P.stt(t[:, 0:W], t[:, 0:W], nw_ap, g[:, 0:W], ALU.mult, ALU.mult, [bt, bg, self.b_c], [bt])
            P.dma(d["yT"][out_idx, :, t0:t0 + W], t[:, 0:W], reads=[bt])

    def hgrn2(self):
        P, A, d = self.P, self.A, self.d
        P.barrier()
        A.reset()
        P.rot = [0, 1, 2, 3, 6, 7]
        T = TSEQ
        SEG = 1408
        NSEG = 6
        fm = d["fm"]
        sm = self.small
        v128 = A.bf16([128, NBLK, 128]); b_v = Buf()
        for n0 in range(0, NBLK, 22):
            P.dma(v128[:, n0:n0 + 22, :], d["tm_hv"][n0 * 128:(n0 + 22) * 128, :].rearrange("(n p) d -> p n d", p=128),
                  writes=[b_v], eng="pool")
        of = A.f32([128, T]); b_of = [Buf() for _ in range(NBLK)]
        qt = A.bf16([128, T]); kt = A.bf16([128, T]); kh = A.bf16([128, T])
        b_qt, b_kt, b_kh = [Buf() for _ in range(6)], [Buf() for _ in range(6)], [Buf() for _ in range(6)]
        khtm = A.bf16([128, NBLK, 128]); b_khtm = [Buf() for _ in range(NBLK)]
        dec = A.f32([128, 264]); b_dec = [Buf() for _ in range(6)]
        rmask = A.f32([128, SEG]); b_rm = Buf()
        P.dma(rmask, d["rmask"], writes=[b_rm])
        sc = [A.f32([128, SEG]) for _ in range(6)]
        bsc = [Buf() for _ in range(6)]
        sA, sB, sC, sD, sT, sQ = sc
        bA, bB, bC, bD, bT, bQ = bsc
        S = [A.f32([128, 128]) for _ in range(2)]; bS = [Buf(), Buf()]
        Sb = [A.bf16([128, 128]) for _ in range(2)]; bSb = [Buf(), Buf()]
        ams = ASlots([A.bf16([128, 128]) for _ in range(3)])
        kms = ASlots([A.bf16([128, 128]) for _ in range(8)])
        lbt = A.f32([128, 8]); b_lb = Buf()
        for dd in range(2):
            if self.layer == 0:
                P.memset(lbt[:, dd:dd + 1], 0.0, [b_lb])
            else:
                P.tt(lbt[:, dd:dd + 1], sm[:, 2 + dd:3 + dd], sm[:, dd:dd + 1], ALU.subtract, [self.b_c], [b_lb])
                P.act(lbt[:, dd:dd + 1], lbt[:, dd:dd + 1], AF.Sigmoid, [b_lb], [b_lb])
            P.ts(lbt[:, 2 + dd:3 + dd], lbt[:, dd:dd + 1], -1.0, ALU.mult, [b_lb], [b_lb], s2=1.0, op1=ALU.add)
            P.ts(lbt[:, 4 + dd:5 + dd], lbt[:, dd:dd + 1], 1.0, ALU.subtract, [b_lb], [b_lb])
        identb = self.cb[:, C_IDENT, :]

        def v3(ap):
            return ap.rearrange("p (c w) -> p c w", w=32)

        for dd in range(2):
            lb, oml, noml = lbt[:, dd:dd + 1], lbt[:, 2 + dd:3 + dd], lbt[:, 4 + dd:5 + dd]
            for seg in range(NSEG):
                s0 = seg * SEG
                P.dma(sA, fm[5 + dd][:, s0:s0 + SEG], writes=[bA])
                P.dma(sQ, fm[4][:, s0:s0 + SEG], writes=[bQ])
                P.act(sA, sA, AF.Sigmoid, [bA], [bA])
                P.ts(sB, sA, noml, ALU.mult, [bA, b_lb], [bB], s2=oml, op1=ALU.add)
                P.ts(sA, sA, oml, ALU.mult, [bA, b_lb], [bA], s2=lb, op1=ALU.add)
                P.act(sA, sA, AF.Ln, [bA], [bA])
                P.add("dve", lambda e: e.tensor_tensor_scan(out=sC, data0=rmask, data1=sA, initial=0.0,
                                                             op0=ALU.mult, op1=ALU.add), [bA, b_rm], [bC])
                totb = v3(sC)[:, :, 31:32].broadcast_to([128, SEG // 32, 32])
                if dd == 0:
                    G, bG = sC, bC
                else:
                    P.tt(sD, sA, sC, ALU.subtract, [bA, bC], [bD])
                    P.tt(v3(sD), v3(sD), totb, ALU.add, [bD, bC], [bD])
                    G, bG = sD, bD
                P.act(dec[:, seg * 44:(seg + 1) * 44], v3(sC)[:, :, 31], AF.Exp, [bC], [b_dec[seg]])
                P.tt(v3(sT), totb, v3(G), ALU.subtract, [bC, bG], [bT])
                P.act(sT, sT, AF.Exp, [bT], [bT])
                P.tt(kh[:, s0:s0 + SEG], sB, sT, ALU.mult, [bB, bT], [b_kh[seg]])
                P.act(sT, G, AF.Exp, [bG], [bT], scale=-1.0)
                P.tt(kt[:, s0:s0 + SEG], sB, sT, ALU.mult, [bB, bT], [b_kt[seg]])
                P.act(sT, G, AF.Exp, [bG], [bT])
                P.stt(qt[:, s0:s0 + SEG], sQ, 128 ** -0.5, sT, ALU.mult, ALU.mult, [bQ, bT], [b_qt[seg]])
            if self.stage < 1:
                P.dma(d["yT"][2, :, 0:SEG], sT, reads=[bT])
                return
            for blk in range(NBLK):
                ptr, bptr = P.bank()
                P.mm(ptr[:, 0:128], kh[:, blk * 128:(blk + 1) * 128], identb, True, True,
                     [b_kh[blk // 11], self.b_c], [bptr])
                P.copy(khtm[:, blk, :], ptr[:, 0:128], [bptr], [b_khtm[blk]],
                       eng=("act" if blk % 2 == 0 else "dve"))
            if self.stage < 2:
                P.dma(d["yT"][2, :, 0:SEG], sT, reads=[bT] + b_khtm)
                return
            P.memset(S[0], 0.0, [bS[0]])
            P.memset(Sb[0], 0.0, [bSb[0]], eng="pool")
            n = 0
            hm = self.cf[:, C_HM0 + dd, :]
            if dd == 0:
                order = list(range(NBLK)); corder = [0, 1, 2, 3]
            else:
                order = [1, 0] + list(range(NBLK - 1, 1, -1)); corder = [3, 2, 1, 0]
            for bi, blk in enumerate(order):
                sg = blk // 11
                c0 = blk * 128
                pa, bpa = P.bank()
                P.mm(pa[:, 0:128], kt[:, c0:c0 + 128], qt[:, c0:c0 + 128], True, True, [b_kt[sg], b_qt[sg]], [bpa])
                am, bam = ams.next()
                P.tt(am, pa[:, 0:128], hm, ALU.mult, [bpa, self.b_c], [bam])
                po, bpo = P.psum[4 + bi % 2]
                P.mm(po[:, 0:128], v128[:, blk, :], am, True, False, [b_v, bam], [bpo])
                kvs = []
                for ci, cc in enumerate(corder):
                    km, bkm = kms.next()
                    if (n + ci) % 2 == 0:
                        P.ts(km, khtm[:, blk, :], self.cf[:, C_ROWM, cc:cc + 1], ALU.mult, [b_khtm[blk], self.b_c], [bkm])
                    else:
                        P.act(km, khtm[:, blk, :], AF.Identity, [b_khtm[blk], self.b_c], [bkm],
                              scale=self.cf[:, C_ROWM, cc:cc + 1])
                    pk, bpk = P.bank()
                    P.mm(pk[:, 0:128], km, v128[:, blk, :], True, True, [bkm, b_v], [bpk])
                    kvs.append((pk, bpk))
                for ci, cc in enumerate(corder):
                    pk, bpk = kvs[ci]
                    P.mm(po[:, cc * 32:(cc + 1) * 32], Sb[n % 2], qt[:, c0 + cc * 32:c0 + (cc + 1) * 32], False, ci == 3,
                         [bSb[n % 2], b_qt[sg]], [bpo])
                    P.stt(S[(n + 1) % 2], S[n % 2], dec[:, blk * 4 + cc:blk * 4 + cc + 1], pk[:, 0:128], ALU.mult, ALU.add,
                          [bS[n % 2], bpk, b_dec[sg]], [bS[(n + 1) % 2]])
                    P.copy(Sb[(n + 1) % 2], S[(n + 1) % 2], [bS[(n + 1) % 2]], [bSb[(n + 1) % 2]], eng="act")
                    n += 1
                if dd == 0:
                    P.copy(of[:, c0:c0 + 128], po[:, 0:128], [bpo], [b_of[blk]], eng="act")
                else:
                    P.tt(of[:, c0:c0 + 128], of[:, c0:c0 + 128], po[:, 0:128], ALU.add, [b_of[blk], bpo], [b_of[blk]])
        W = 384
        scr = (ASlots([sA[:, 0:W], sA[:, W:2 * W]]), ASlots([kt[:, 0:W], kt[:, W:2 * W]]),
               ASlots([sB[:, 0:W], sB[:, W:2 * W]]), ASlots([sC[:, 0:W], sC[:, W:2 * W]]),
               ASlots([sD[:, 0:W], sD[:, W:2 * W]]))
        P.barrier()
        self.readout(of, b_of, 7, sm[:, 4:5], 2, scr)


def prep_mixer_small(inp, l, hh):
    sm = np.zeros((128, 32), np.float32)
    ch = hh * 128 + np.arange(128)
    lbl = inp["hg_lb_logits"]
    sm[:, 0] = lbl[0, 0, ch]; sm[:, 1] = lbl[0, 1, ch]
    sm[:, 2] = lbl[1, 0, ch]; sm[:, 3] = lbl[1, 1, ch]
    sm[:, 4] = inp["hg_norm"][l]
    sm[:, 5] = inp["gdn_norm"][l]
    sm[:, 6] = inp["gdn_a_log"][l, 0, hh]; sm[:, 7] = inp["gdn_a_log"][l, 1, hh]
    sm[:, 8] = inp["gdn_dt_bias"][l, 0, hh]; sm[:, 9] = inp["gdn_dt_bias"][l, 1, hh]
    cw = inp["gdn_conv"][l]
    for ti in range(3):
        for tap in range(3):
            sm[:, 10 + ti * 3 + tap] = cw[tap, ti * 512 + ch]
    return sm


def token_core_inputs(inp, mode, core, h_tok, y_tok=None):
    b, j = core // 4, core % 4
    m = {"hT": tok_to_fm(h_tok), "ones": _CONST["ones"], "cT": _CONST["cT"][b]}
    if mode in ("mid", "last"):
        l = 0 if mode == "mid" else 1
        for k, v in _CONST[("ffn", l, 1)].items():
            m[k + "A"] = v
        m.update(_CONST[("merge", l)])
        m["yT"] = np.ascontiguousarray(y_tok.reshape(3, NTOK, 4, 128).transpose(3, 0, 2, 1).reshape(128, 12, NTOK))
    if mode in ("first", "mid"):
        l = 0 if mode == "first" else 1
        for k, v in _CONST[("ffn", l, 0)].items():
            m[k + "B"] = v
        m.update(_CONST[("proj", l)])
        m["rope_cos"], m["rope_sin"] = _CONST[("rope", j)]
    return m


_CONST = {}
_PROGS = {}


def _prepare(inp):
    _CONST.clear()
    _CONST["ones"] = np.ones((128, 128), np.float32)
    _CONST["cT"] = [np.ascontiguousarray(np.stack([_fm(inp["c"][b]), _fm(inp["c_ctx"])], -1)) for b in range(NB)]
    for l in range(DEPTH):
        for i in range(2):
            _CONST[("ffn", l, i)] = prep_ffn_set(inp, l, i)
        _CONST[("proj", l)] = prep_proj(inp, l)
        _CONST[("merge", l)] = prep_merge(inp, l)
    for j in range(4):
        _CONST[("rope", j)] = rope_tables(j)
    _CONST["consts"] = make_consts()
    rm = np.ones((128, 1408), np.float32)
    rm[:, ::32] = 0.0
    _CONST["rmask"] = rm


def _prog(key):
    if key not in _PROGS:
        if key[0] == "tok":
            _PROGS[key] = TokenPhase(key[1])
        else:
            _PROGS[key] = MixerPhase(key[1])
    return _PROGS[key]


def run_token(inp, mode, h_cores, y_cores=None):
    tp = _prog(("tok", mode))
    maps = [token_core_inputs(inp, mode, c, h_cores[c], None if y_cores is None else y_cores[c]) for c in range(NCORE)]
    res = run_bass_kernel_spmd(tp.nc, maps, core_ids=list(range(NCORE)))
    return res.results


def _seq_order(parts_ctx, parts_lat, axis):
    return np.concatenate(parts_ctx + parts_lat, axis=axis)


def mixer_inputs(inp, l, tok_res):
    maps = []
    for core in range(NCORE):
        b, hh = core // 4, core % 4
        rs = [tok_res[b * 4 + j] for j in range(4)]
        fm = _seq_order([r["fm"][hh][:, :, 0:64] for r in rs], [r["fm"][hh][:, :, 64:] for r in rs], 2)
        ab = _seq_order([r["tm_ab"][0:64] for r in rs], [r["tm_ab"][64:] for r in rs], 0)
        ab = np.ascontiguousarray(ab[:, hh * 4:(hh + 1) * 4])
        mv = _seq_order([r["tm_mv"][0:64, hh * 128:(hh + 1) * 128] for r in rs],
                        [r["tm_mv"][64:, hh * 128:(hh + 1) * 128] for r in rs], 0)
        hv = _seq_order([r["tm_hv"][0:64, hh * 128:(hh + 1) * 128] for r in rs],
                        [r["tm_hv"][64:, hh * 128:(hh + 1) * 128] for r in rs], 0)
        maps.append({"fm": np.ascontiguousarray(fm), "tm_ab": ab, "tm_mv": np.ascontiguousarray(mv),
                     "tm_hv": np.ascontiguousarray(hv), "consts": _CONST["consts"], "rmask": _CONST["rmask"],
                     "small": prep_mixer_small(inp, l, hh)})
    return maps


def run_mixer(inp, l, tok_res):
    mp = _prog(("mix", l))
    res = run_bass_kernel_spmd(mp.nc, mixer_inputs(inp, l, tok_res), core_ids=list(range(NCORE)))
    return res.results


def y_for_token_cores(mix_res):
    out = []
    for core in range(NCORE):
        b, j = core // 4, core % 4
        y = np.zeros((3, NTOK, 512), np.float32)
        for hh in range(4):
            yT = mix_res[b * 4 + hh]["yT"]
            ctx = yT[:, :, 64 * j:64 * j + 64]
            lat = yT[:, :, 256 + 2048 * j:256 + 2048 * (j + 1)]
            y[:, :, hh * 128:(hh + 1) * 128] = np.concatenate([ctx, lat], 2).transpose(0, 2, 1)
        out.append(y)
    return out


def _gdn(self):
    P, A, d = self.P, self.A, self.d
    P.barrier()
    A.reset()
    P.rot = [0, 1, 2, 3, 4, 5]
    T = TSEQ
    SEG = 1408
    fm = d["fm"]
    sm = self.small
    cf, cb = self.cf, self.cb
    b_c = self.b_c
    onesb, identb = cb[:, C_ONES, :], cb[:, C_IDENT, :]
    onesf, identf = cf[:, C_ONES, :], cf[:, C_IDENT, :]
    qT = A.bf16([128, T]); kT = A.bf16([128, T])
    b_qT = [Buf() for _ in range(6)]; b_kT = [Buf() for _ in range(6)]
    k_tm = A.bf16([128, NBLK, 128]); v_tm = A.bf16([128, NBLK, 128])
    b_ktm = [Buf() for _ in range(NBLK)]; b_vtm = [Buf() for _ in range(NBLK)]
    of = A.f32([128, T]); b_of = [Buf() for _ in range(NBLK)]
    nsm = A.f32([128, 32]); b_nsm = Buf()
    P.ts(nsm, sm[:, :], -1.0, ALU.mult, [b_c], [b_nsm])
    Xs = ASlots([A.f32([128, SEG + 2]) for _ in range(2)])
    Ys = ASlots([A.f32([128, SEG]) for _ in range(2)])
    vTs = ASlots([A.bf16([128, SEG]) for _ in range(2)])
    sqs = ASlots([A.bf16([128, 352]) for _ in range(2)])
    lns = ASlots([A.f32([128, 352]) for _ in range(2)])
    rs = ASlots([A.f32([128, 352]) for _ in range(2)])
    for ti in range(3):
        for seg in range(6):
            s0 = seg * SEG
            X, bX = Xs.next()
            lo = max(s0 - 1, 0)
            hi = min(s0 + SEG + 1, T)
            if seg == 0:
                P.memset(X[:, 0:1], 0.0, [bX])
            if seg == 5:
                P.memset(X[:, SEG + 1:SEG + 2], 0.0, [bX])
            P.dma(X[:, lo - (s0 - 1):hi - (s0 - 1)], fm[ti][:, lo:hi], writes=[bX])
            Y, bY = Ys.next()
            wc = 10 + ti * 3
            P.ts(Y, X[:, 1:SEG + 1], sm[:, wc + 1:wc + 2], ALU.mult, [bX, b_c], [bY])
            P.stt(Y, X[:, 0:SEG], sm[:, wc:wc + 1], Y, ALU.mult, ALU.add, [bX, b_c, bY], [bY])
            P.stt(Y, X[:, 2:SEG + 2], sm[:, wc + 2:wc + 3], Y, ALU.mult, ALU.add, [bX, b_c, bY], [bY])
            if seg == 0:
                P.stt(Y[:, 255:256], X[:, 257:258], nsm[:, wc + 2:wc + 3], Y[:, 255:256], ALU.mult, ALU.add,
                      [bX, b_nsm, bY], [bY])
                P.stt(Y[:, 256:257], X[:, 256:257], nsm[:, wc:wc + 1], Y[:, 256:257], ALU.mult, ALU.add,
                      [bX, b_nsm, bY], [bY])
            P.act(Y, Y, AF.Silu, [bY], [bY])
            if ti < 2:
                dst, bdst = (qT, b_qT) if ti == 0 else (kT, b_kT)
                for t0 in range(0, SEG, 352):
                    sq, bsq = sqs.next()
                    P.act(sq, Y[:, t0:t0 + 352], AF.Square, [bY], [bsq])
                    pt, bp = P.bank()
                    P.mm(pt[:, 0:352], onesb, sq, True, True, [b_c, bsq], [bp])
                    ln, bln = lns.next()
                    P.act(ln, pt[:, 0:352], AF.Ln, [bp], [bln], bias=EPS)
                    r, br = rs.next()
                    P.act(r, ln, AF.Exp, [bln], [br], scale=-0.5, bias=(math.log(128 ** -0.5) if ti == 0 else 0.0))
                    P.tt(dst[:, s0 + t0:s0 + t0 + 352], Y[:, t0:t0 + 352], r, ALU.mult, [bY, br], [bdst[seg]])
                if ti == 1:
                    for bb in range(11):
                        blk = seg * 11 + bb
                        ptr, bptr = P.bank()
                        P.mm(ptr[:, 0:128], kT[:, blk * 128:(blk + 1) * 128], identb, True, True, [b_kT[seg], b_c], [bptr])
                        P.copy(k_tm[:, blk, :], ptr[:, 0:128], [bptr], [b_ktm[blk]], eng=("act" if bb % 2 else "dve"))
            else:
                vT, bvT = vTs.next()
                P.copy(vT, Y, [bY], [bvT], eng="pool")
                for bb in range(11):
                    blk = seg * 11 + bb
                    ptr, bptr = P.bank()
                    P.mm(ptr[:, 0:128], vT[:, bb * 128:(bb + 1) * 128], identb, True, True, [bvT, b_c], [bptr])
                    P.copy(v_tm[:, blk, :], ptr[:, 0:128], [bptr], [b_vtm[blk]], eng=("act" if bb % 2 else "dve"))
    ab = A.f32([128, NBLK, 4]); b_ab = Buf()
    P.dma(ab, d["tm_ab"].rearrange("(n p) c -> p n c", p=128), writes=[b_ab])
    cols = {}
    for dd in range(2):
        g = A.f32([128, NBLK]); bg = Buf()
        tmpc = A.f32([128, NBLK]); btmp = Buf()
        nA = A.f32([128, 1]); bnA = Buf()
        P.act(tmpc, ab[:, :, dd], AF.Exp, [b_ab, b_c], [btmp], bias=sm[:, 8 + dd:9 + dd])
        P.act(tmpc, tmpc, AF.Ln, [btmp], [btmp], bias=1.0)
        P.act(nA, sm[:, 6 + dd:7 + dd], AF.Exp, [b_c], [bnA])
        P.ts(nA, nA, -1.0, ALU.mult, [bnA], [bnA])
        P.ts(g, tmpc, nA, ALU.mult, [btmp, bnA], [bg])
        beta = A.f32([128, NBLK]); bbeta = Buf()
        P.act(beta, ab[:, :, 2 + dd], AF.Sigmoid, [b_ab], [bbeta])
        gcum = A.f32([128, NBLK]); negg = A.f32([128, NBLK]); negeg = A.f32([128, NBLK])
        ekl = A.f32([128, NBLK]); decS = A.f32([128, NBLK]); bcol = Buf()
        pt, bp = P.bank()
        P.mm(pt[:, 0:NBLK], cf[:, C_TRI0 + dd, :], g, True, True, [b_c, bg], [bp])
        P.copy(gcum, pt[:, 0:NBLK], [bp], [bcol])
        P.ts(negg, gcum, -1.0, ALU.mult, [bcol], [bcol])
        P.act(negeg, gcum, AF.Exp, [bcol], [bcol])
        P.ts(negeg, negeg, -1.0, ALU.mult, [bcol], [bcol])
        pt2, bp2 = P.bank()
        P.mm(pt2[:, 0:NBLK], onesf, g, True, True, [b_c, bg], [bp2])
        P.act(decS, pt2[:, 0:NBLK], AF.Exp, [bp2], [bcol])
        P.tt(ekl, pt2[:, 0:NBLK], gcum, ALU.subtract, [bp2, bcol], [bcol])
        P.act(ekl, ekl, AF.Exp, [bcol], [bcol])
        cols[dd] = dict(gcum=gcum, negg=negg, negeg=negeg, ekl=ekl, decS=decS, beta=beta, b=[bcol, bbeta])
    def mk(n, dt, cnt):
        return ASlots([(A.f32([128, 128]) if dt == "f" else A.bf16([128, 128])) for _ in range(cnt)])
    st = {}
    for dd in range(2):
        st[dd] = dict(
            dg=mk(0, "f", 2), Dm=mk(0, "f", 2), ET=mk(0, "f", 2), EGB=mk(0, "f", 2), ETs=mk(0, "f", 2),
            attnT=mk(0, "b", 3), N=mk(0, "b", 2), qg=mk(0, "b", 3), khat=mk(0, "b", 3), NB=mk(0, "b", 2),
            Tm=mk(0, "b", 3), Um=mk(0, "b", 3), Ufin=mk(0, "b", 3), R0=mk(0, "b", 2), vnew=mk(0, "b", 2),
            S=[A.f32([128, 128]) for _ in range(2)], bS=[Buf(), Buf()],
            Sb=[A.bf16([128, 128]) for _ in range(2)], bSb=[Buf(), Buf()], n=0, pend=None)
        P.memset(st[dd]["S"][0], 0.0, [st[dd]["bS"][0]])
        P.memset(st[dd]["Sb"][0], 0.0, [st[dd]["bSb"][0]], eng="pool")
    orders = {0: list(range(NBLK)), 1: [1, 0] + list(range(NBLK - 1, 1, -1))}
    written = set()

    def offchain(dd, blk):
        s = st[dd]
        cl = cols[dd]
        bcl = cl["b"]
        sg = blk // 11
        c0 = blk * 128
        kTb, qTb = kT[:, c0:c0 + 128], qT[:, c0:c0 + 128]
        pKK, bKK = P.bank()
        P.mm(pKK[:, 0:128], kTb, kTb, True, True, [b_kT[sg]], [bKK])
        pQK, bQK = P.bank()
        P.mm(pQK[:, 0:128], kTb, qTb, True, True, [b_kT[sg], b_qT[sg]], [bQK])
        dg, bdg = s["dg"].next()
        P.ts(dg, identf, cl["gcum"][:, blk:blk + 1], ALU.mult, [b_c] + bcl, [bdg])
        pG, bG = P.bank()
        P.mm(pG[:, 0:128], onesf, dg, True, True, [b_c, bdg], [bG])
        Dm, bDm = s["Dm"].next()
        P.stt(Dm, pG[:, 0:128], cl["negg"][:, blk:blk + 1], cf[:, C_NEG0 + dd, :], ALU.add, ALU.add, [bG, b_c] + bcl, [bDm])
        ET, bET = s["ET"].next()
        P.act(ET, Dm, AF.Exp, [bDm], [bET])
        EGB, bEGB = s["EGB"].next()
        P.act(EGB, pG[:, 0:128], AF.Exp, [bG], [bEGB])
        ETs, bETs = s["ETs"].next()
        P.tt(ETs, ET, cf[:, C_STR0 + dd, :], ALU.mult, [bET, b_c], [bETs], eng="pool")
        attnT, battn = s["attnT"].next()
        P.tt(attnT, pQK[:, 0:128], ET, ALU.mult, [bQK, bET], [battn])
        N, bN = s["N"].next()
        P.stt(N, pKK[:, 0:128], cl["beta"][:, blk:blk + 1], ETs, ALU.mult, ALU.mult, [bKK, bETs] + bcl, [bN])
        qg, bqg = s["qg"].next()
        P.tt(qg, qTb, EGB, ALU.mult, [b_qT[sg], bEGB], [bqg], eng="pool")
        khat, bkhat = s["khat"].next()
        P.ts(khat, k_tm[:, blk, :], cl["ekl"][:, blk:blk + 1], ALU.mult, [b_ktm[blk]] + bcl, [bkhat], eng="pool")
        yield
        Tc, bTc = identb, b_c
        Uc, bUc = identb, b_c
        for l in range(7):
            pB, bB = P.bank()
            P.mm(pB[:, 0:128], N, Tc, True, True, [bN, bTc], [bB])
            NB, bNB = s["NB"].next()
            P.tt(NB, pB[:, 0:128], cf[:, C_LVL + dd * 7 + l, :], ALU.mult, [bB, b_c], [bNB])
            if l < 6:
                pT, bT_ = P.bank()
                P.mm(pT[:, 0:128], Uc, NB, True, False, [bUc, bNB], [bT_])
                P.mm(pT[:, 0:128], identb, Tc, False, True, [b_c, bTc], [bT_])
                Tn, bTn = s["Tm"].next()
                P.copy(Tn, pT[:, 0:128], [bT_], [bTn], eng="act")
            pU, bU_ = P.bank()
            P.mm(pU[:, 0:128], NB, Uc, True, False, [bNB, bUc], [bU_])
            P.mm(pU[:, 0:128], identb, Uc, False, True, [b_c, bUc], [bU_])
            Un, bUn = (s["Um"] if l < 6 else s["Ufin"]).next()
            P.copy(Un, pU[:, 0:128], [bU_], [bUn], eng="act")
            if l < 6:
                Tc, bTc = Tn, bTn
            Uc, bUc = Un, bUn
            yield
        s["newp"] = dict(blk=blk, U=Uc, bU=bUc, attnT=attnT, battn=battn, qg=qg, bqg=bqg, khat=khat, bkhat=bkhat)

    def inchain(dd, step):
        s = st[dd]
        pd = s["pend"]
        if pd is None:
            return
        s["pend"] = None
        cl = cols[dd]
        bcl = cl["b"]
        blk = pd["blk"]
        sg = blk // 11
        c0 = blk * 128
        n = s["n"]
        cur, nxt = n % 2, (n + 1) % 2
        pkS, bkS = P.bank()
        P.mm(pkS[:, 0:128], kT[:, c0:c0 + 128], s["Sb"][cur], True, True, [b_kT[sg], s["bSb"][cur]], [bkS])
        po, bpo = P.psum[6 + dd]
        P.mm(po[:, 0:128], s["Sb"][cur], pd["qg"], True, False, [s["bSb"][cur], pd["bqg"]], [bpo])
        R0, bR0 = s["R0"].next()
        P.stt(R0, pkS[:, 0:128], cl["negeg"][:, blk:blk + 1], v_tm[:, blk, :], ALU.mult, ALU.add,
              [bkS, b_vtm[blk]] + bcl, [bR0])
        pV, bV = P.bank()
        P.mm(pV[:, 0:128], pd["U"], R0, True, True, [pd["bU"], bR0], [bV])
        vnew, bvn = s["vnew"].next()
        P.act(vnew, pV[:, 0:128], AF.Identity, [bV] + bcl, [bvn], scale=cl["beta"][:, blk:blk + 1])
        pKV, bKV = P.bank()
        P.mm(pKV[:, 0:128], pd["khat"], vnew, True, True, [pd["bkhat"], bvn], [bKV])
        P.mm(po[:, 0:128], vnew, pd["attnT"], False, True, [bvn, pd["battn"]], [bpo])
        P.stt(s["S"][nxt], s["S"][cur], cl["decS"][:, blk:blk + 1], pKV[:, 0:128], ALU.mult, ALU.add,
              [s["bS"][cur], bKV] + bcl, [s["bS"][nxt]])
        P.copy(s["Sb"][nxt], s["S"][nxt], [s["bS"][nxt]], [s["bSb"][nxt]], eng="act")
        if blk not in written:
            written.add(blk)
            P.copy(of[:, c0:c0 + 128], po[:, 0:128], [bpo], [b_of[blk]], eng="dve")
        else:
            P.tt(of[:, c0:c0 + 128], of[:, c0:c0 + 128], po[:, 0:128], ALU.add, [b_of[blk], bpo], [b_of[blk]])
        s["n"] = n + 1

    dirs = [0, 1]
    if self.stage == 11:
        dirs = [0]
    if self.stage == 12:
        dirs = [1]
    for step in range(NBLK + 1):
        if step < NBLK:
            gens = [offchain(dd, orders[dd][step]) for dd in dirs]
            live = list(gens)
            while live:
                for gen_ in list(live):
                    try:
                        next(gen_)
                    except StopIteration:
                        live.remove(gen_)
        for dd in dirs:
            inchain(dd, step)
            st[dd]["pend"] = st[dd].pop("newp", None) if step < NBLK else None
    P.barrier()
    if self.stage in (10, 11, 12):
        for c0 in range(0, T, 2112):
            P.dma(d["yT"][0, :, c0:c0 + 2112], of[:, c0:c0 + 2112], reads=b_of)
        return
    W = 384
    scrA, scrC, scrD, scrE = Xs.t[0], Ys.t[0], Xs.t[1], Ys.t[1]
    scrB = vTs.t[0]
    scr = tuple(ASlots([x[:, 0:W], x[:, W:2 * W]]) for x in (scrA, scrB, scrC, scrD, scrE))
    self.readout(of, b_of, 3, sm[:, 5:6], 0, scr)


MixerPhase.gdn = _gdn


class Fused:
    def __init__(self):
        nc = self.nc = bass.Bass("TRN2", target_bir_lowering=False)
        self.d = d = {}

        def din(name, shape):
            d[name] = nc.dram_tensor(name, list(shape), F32, kind="ExternalInput").ap()

        def dscr(name, shape, dtype=F32):
            d[name] = nc.dram_tensor(name, list(shape), dtype, kind="Internal").ap()

        din("hT", [4, 128, KC, NTOK])
        din("cT", [128, KC, 2])
        din("consts", [NCONST, 128, 128])
        din("rmask", [128, 1408])
        din("small", [8, 128, 32])
        din("rope_cos", [4, 64, NTOK])
        din("rope_sin", [4, 64, NTOK])
        for l in range(DEPTH):
            din(f"wada{l}", [24, 128, 8 * 384]); din(f"bada{l}", [128, 72]); din(f"normw{l}", [128, 6, 8])
            for i in range(2):
                din(f"w1_{l}{i}", [NFC, 128, 2048]); din(f"w2_{l}{i}", [8, 128, NFC * 128])
            for nm, shp in (("wp", [38, 128, 1024]), ("wt", [128, 8 * 528]), ("wq", [128, 3 * 1024]),
                            ("wkn", [128, 2 * 512]), ("wkv", [128, 2 * 512]), ("qnw", [128, 3]), ("kvnw", [128, 2]),
                            ("wg", [8, 128, 3 * 8 * 128]), ("wb", [8, 128, 12 * 128]), ("wo", [8, 128, 1024])):
                din(f"{nm}{l}", shp)
        d["out"] = nc.dram_tensor("out", [4, 128, KC, NTOK], F32, kind="ExternalOutput").ap()
        dscr("hbuf", [4, 128, KC, NTOK])
        dscr("fm_seq", [4, NFM, 128, TSEQ])
        dscr("tm_ab_seq", [4, TSEQ, 4])
        dscr("tm_mv_seq", [4, TSEQ, 128])
        dscr("tm_hv_seq", [4, TSEQ, 128])
        dscr("yseq", [4, 3, 128, TSEQ])
        with ExitStack() as st:
            self.P = P = Prog(nc, st)
            P.arena = None
            P.init_psum(8)
            self.cf = P.sb("cf", [128, NCONST, 128], F32, persistent=True)
            self.cb = P.sb("cb", [128, NCONST, 128], BF16, persistent=True)
            self.b_c = Buf()
            P.dma(self.cf[:], d["consts"].rearrange("n p f -> p n f"), writes=[self.b_c])
            P.dma(self.cb[:], d["consts"].rearrange("n p f -> p n f"), writes=[self.b_c], eng="pool")
            self.smalls = P.sb("smalls", [128, 8, 32], F32, persistent=True)
            P.dma(self.smalls[:], d["small"].rearrange("n p f -> p n f"), writes=[self.b_c])
            Sper = [P.sb(f"Sper{l}", [128, 9, 8, 2], F32, persistent=True) for l in range(DEPTH)]
            bSper = [Buf() for _ in range(DEPTH)]
            self.A = Arena(P, 178)
            P.arena = self.A
            for l in range(DEPTH):
                P.barrier()
                self.A.reset()
                tp = TokenPhase.__new__(TokenPhase)
                tp.P = P
                tp.d = {"cT": d["cT"], "wadaM": d[f"wada{l}"], "badaM": d[f"bada{l}"], "normwM": d[f"normw{l}"]}
                S, bS = tp.compute_mod("M")
                P.copy(Sper[l][:], S, [bS], [bSper[l]])
            mods_l = [(Sper[l], bSper[l]) for l in range(DEPTH)]

            def tok_d(mode, j, lA, lB, src, dst):
                dd = {"hT": d[src][j], "hT_out": d[dst][j], "fm_seq": d["fm_seq"], "tm_ab_seq": d["tm_ab_seq"],
                      "tm_mv_seq": d["tm_mv_seq"], "tm_hv_seq": d["tm_hv_seq"], "yseq": d["yseq"],
                      "rope_cos": d["rope_cos"][j], "rope_sin": d["rope_sin"][j]}
                if lA is not None:
                    dd.update({"w1A": d[f"w1_{lA}1"], "w2A": d[f"w2_{lA}1"], "wg": d[f"wg{lA}"], "wb": d[f"wb{lA}"],
                               "wo": d[f"wo{lA}"]})
                if lB is not None:
                    dd.update({"w1B": d[f"w1_{lB}0"], "w2B": d[f"w2_{lB}0"]})
                    for nm in ("wp", "wt", "wq", "wkn", "wkv", "qnw", "kvnw"):
                        dd[nm] = d[f"{nm}{lB}"]
                return dd

            def token_phase(mode, lA, lB, src, dst):
                for j in range(4):
                    P.barrier()
                    self.A.reset()
                    mods = {}
                    if lA is not None:
                        mods["A"] = mods_l[lA]
                    if lB is not None:
                        mods["B"] = mods_l[lB]
                    TokenPhase(mode, ext=self, d=tok_d(mode, j, lA, lB, src, dst), mods=mods, shard=j)

            def mixer_phase(l):
                for hh in range(4):
                    md = {"fm": d["fm_seq"][hh], "tm_ab": d["tm_ab_seq"][hh], "tm_mv": d["tm_mv_seq"][hh],
                          "tm_hv": d["tm_hv_seq"][hh], "yT": d["yseq"][hh], "rmask": d["rmask"]}
                    MixerPhase(l, ext=self, d=md, small=self.smalls[:, l * 4 + hh, :])

            token_phase("first", None, 0, "hT", "hbuf")
            mixer_phase(0)
            token_phase("mid", 0, 1, "hbuf", "hbuf")
            mixer_phase(1)
            token_phase("last", 1, None, "hbuf", "out")
            P.barrier()
            P.emit()


def fused_inputs(inp, b):
    m = {"cT": _CONST["cT"][b], "consts": _CONST["consts"], "rmask": _CONST["rmask"]}
    h = []
    for j in range(4):
        h.append(tok_to_fm(np.concatenate([inp["ctx"][b, 64 * j:64 * j + 64], inp["x"][b, 2048 * j:2048 * (j + 1)]], 0)))
    m["hT"] = np.stack(h)
    m["small"] = np.stack([prep_mixer_small(inp, l, hh) for l in range(DEPTH) for hh in range(4)])
    m["rope_cos"] = np.stack([_CONST[("rope", j)][0] for j in range(4)])
    m["rope_sin"] = np.stack([_CONST[("rope", j)][1] for j in range(4)])
    for l in range(DEPTH):
        f0, f1 = _CONST[("ffn", l, 0)], _CONST[("ffn", l, 1)]
        m[f"wada{l}"], m[f"bada{l}"], m[f"normw{l}"] = f0["wada"], f0["bada"], f0["normw"]
        m[f"w1_{l}0"], m[f"w2_{l}0"] = f0["w1"], f0["w2"]
        m[f"w1_{l}1"], m[f"w2_{l}1"] = f1["w1"], f1["w2"]
        for k, v in _CONST[("proj", l)].items():
            m[f"{k}{l}"] = v
        for k, v in _CONST[("merge", l)].items():
            m[f"{k}{l}"] = v
    return m


def kernel(**inp):
    inp = {k: np.asarray(v) for k, v in inp.items()}
    _prepare(inp)
    if "fused" not in _PROGS:
        _PROGS["fused"] = Fused()
    fz = _PROGS["fused"]
    maps = [fused_inputs(inp, b) for b in range(NB)]
    res = run_bass_kernel_spmd(fz.nc, maps, core_ids=list(range(NB)))
    out = np.zeros((NB, SEQ, D), np.float32)
    for b in range(NB):
        o = res.results[b]["out"]
        for j in range(4):
            out[b, 2048 * j:2048 * (j + 1)] = fm_to_tok(o[j])[64:]
    return out
```
